# Optimizing a Trainium2 kernel written in Bass

```python
import functools
import math
import jax
import jax.numpy as jnp
from jax import lax
import numpy as np

D_MODEL = 1024
BATCH = 2
SEQ = 8192
DEPTH = 2

GRID_W = 64
CTX_LEN = 256
ROPE_THETA = 10000.0
NORM_EPS = 1e-6
NEG_INF = -1e30
BLOCK = 128
WINDOW = 128

A_HEADS = 8
A_KV_HEADS = 2
A_HEAD_DIM = 64
B_HEADS = 4
B_QK_DIM = 64
B_V_DIM = 2 * B_QK_DIM
A_Q_W = A_HEADS * A_HEAD_DIM
A_KV_W = A_KV_HEADS * A_HEAD_DIM
B_QK_W = B_HEADS * 2 * B_QK_DIM
B_V_W = B_HEADS * B_V_DIM
AB_Q_W = A_Q_W + B_QK_W
AB_IN_W = AB_Q_W + 2 * A_KV_W + B_QK_W + B_V_W
AB_OUT_W = A_Q_W + B_V_W

C_HEADS = 16
C_Q_LORA = 384
C_KV_LORA = 256
C_NOPE = 64
C_ROPE = 32
C_V = 64
C_IN_W = C_Q_LORA + C_KV_LORA + C_ROPE
C_OUT_W = C_HEADS * C_V

FFN_HIDDEN = ((8 * D_MODEL + 3 * 256 - 1) // (3 * 256)) * 256

N_EVEN = (DEPTH + 1) // 2
N_ODD = DEPTH // 2

F32 = jnp.float32

kernel_name = 'hybrid_dit_swa_diff_mla_prefix'


def rmsnorm(x, g):
    xf = x.astype(F32)
    y = xf * lax.rsqrt(jnp.mean(xf * xf, axis=-1, keepdims=True) + NORM_EPS)
    return (y * g.astype(F32)).astype(x.dtype)


def modulate(xn, shift, scale):
    return xn * (1.0 + scale) + shift


def joint_softmax(*logits):
    m = functools.reduce(jnp.maximum, [s.max(axis=-1, keepdims=True) for s in logits])
    e = [jnp.exp(s - m) for s in logits]
    inv = 1.0 / functools.reduce(jnp.add, [t.sum(axis=-1, keepdims=True) for t in e])
    return [t * inv for t in e]


def axial_rope_tables(rows, rot_dim):
    row = jnp.repeat(jnp.arange(rows, dtype=F32), GRID_W)
    col = jnp.tile(jnp.arange(GRID_W, dtype=F32), rows)
    axis_dim = rot_dim // 2
    inv_freq = ROPE_THETA ** (-jnp.arange(0, axis_dim, 2, dtype=F32) / axis_dim)
    ang = jnp.concatenate([row[:, None] * inv_freq, col[:, None] * inv_freq], axis=-1)
    return jnp.cos(ang), jnp.sin(ang)


def apply_axial_rope(x, tab):
    cos, sin = tab
    quarter = x.shape[-1] // 4
    xr, xc = jnp.split(x.astype(F32), 2, axis=-1)

    def rot(u, cs, sn):
        u1, u2 = jnp.split(u, 2, axis=-1)
        return jnp.concatenate([u1 * cs - u2 * sn, u2 * cs + u1 * sn], axis=-1)

    out = jnp.concatenate([
        rot(xr, cos[:, None, :quarter], sin[:, None, :quarter]),
        rot(xc, cos[:, None, quarter:], sin[:, None, quarter:])], axis=-1)
    return out.astype(x.dtype)


def to_blocks(t):
    b, n = t.shape[:2]
    return jnp.moveaxis(t.reshape(b, n // BLOCK, BLOCK, *t.shape[2:]), 1, 0)


def from_blocks(t):
    t = jnp.moveaxis(t, 0, 1)
    return t.reshape(t.shape[0], -1, *t.shape[3:])


def swiglu(h, w13, w2):
    gate, up = jnp.split(h @ w13, 2, axis=-1)
    return (jax.nn.silu(gate) * up) @ w2


def lambda_init(layer_idx):
    return 0.8 - 0.6 * math.exp(-0.3 * layer_idx)


def window_gqa_latent(q, k, v, k_ctx, v_ctx, sink):
    bsz, n, _, dh = q.shape
    nb = n // BLOCK
    g = A_HEADS // A_KV_HEADS
    scale = dh ** -0.5
    qb = q.reshape(bsz, nb, BLOCK, A_KV_HEADS, g, dh)

    def band(t):
        tp = jnp.pad(t, ((0, 0), (BLOCK, BLOCK), (0, 0), (0, 0)))
        tp = tp.reshape(bsz, nb + 2, BLOCK, A_KV_HEADS, dh)
        return jnp.concatenate([tp[:, :-2], tp[:, 1:-1], tp[:, 2:]], axis=2)

    kb, vb = band(k), band(v)
    s_loc = jnp.einsum('bnqhgd,bnkhd->bhgnqk', qb, kb).astype(F32) * scale
    s_ctx = jnp.einsum('bnqhgd,bchd->bhgnqc', qb, k_ctx).astype(F32) * scale
    qi = jnp.arange(BLOCK)[:, None]
    kj = jnp.arange(3 * BLOCK)[None, :]
    key_pos = (jnp.arange(nb)[:, None, None] - 1) * BLOCK + kj[None]
    valid = (jnp.abs(kj - BLOCK - qi) <= WINDOW)[None] & (key_pos >= 0) & (key_pos < n)
    s_loc = jnp.where(valid, s_loc, NEG_INF)
    s_sink = sink.astype(F32).reshape(1, A_KV_HEADS, g, 1, 1, 1)
    p_loc, p_ctx, _ = joint_softmax(s_loc, s_ctx, s_sink)
    o = (jnp.einsum('bhgnqk,bnkhd->bnqhgd', p_loc.astype(v.dtype), vb)
         + jnp.einsum('bhgnqc,bchd->bnqhgd', p_ctx.astype(v.dtype), v_ctx))
    return o.reshape(bsz, n, A_HEADS * dh)


def gqa_ctx(q, k, v, sink):
    bsz, t, _, dh = q.shape
    g = A_HEADS // A_KV_HEADS
    qg = q.reshape(bsz, t, A_KV_HEADS, g, dh)
    s = jnp.einsum('bqhgd,bkhd->bhgqk', qg, k).astype(F32) * dh ** -0.5
    p, _ = joint_softmax(s, sink.astype(F32).reshape(1, A_KV_HEADS, g, 1, 1))
    o = jnp.einsum('bhgqk,bkhd->bqhgd', p.astype(v.dtype), v)
    return o.reshape(bsz, t, A_HEADS * dh)


def diff_attend(q, kv_parts, lam):
    scale = q.shape[-1] ** -0.5
    logits = [jnp.einsum('bqhmd,bkhmd->bmhqk', q, k).astype(F32) * scale for k, _ in kv_parts]
    probs = joint_softmax(*logits)
    outs = [jnp.einsum('bhqk,bkhd->bqhd', (p[:, 0] - lam * p[:, 1]).astype(v.dtype), v)
            for p, (_, v) in zip(probs, kv_parts)]
    return functools.reduce(jnp.add, outs)


def diff_out(o, subln_g, lam_init):
    b, t = o.shape[:2]
    return (rmsnorm(o, subln_g) * (1.0 - lam_init)).reshape(b, t, -1)


def ab_queries(p_q):
    b, t = p_q.shape[:2]
    qa = p_q[..., :A_Q_W].reshape(b, t, A_HEADS, A_HEAD_DIM)
    qb = p_q[..., A_Q_W:].reshape(b, t, B_HEADS, 2, B_QK_DIM)
    return qa, qb


def ab_keys_values(p_kv):
    b, t = p_kv.shape[:2]
    o1 = A_KV_W
    o2 = 2 * A_KV_W
    o3 = o2 + B_QK_W
    ka = p_kv[..., :o1].reshape(b, t, A_KV_HEADS, A_HEAD_DIM)
    va = p_kv[..., o1:o2].reshape(b, t, A_KV_HEADS, A_HEAD_DIM)
    kb = p_kv[..., o2:o3].reshape(b, t, B_HEADS, 2, B_QK_DIM)
    vb = p_kv[..., o3:].reshape(b, t, B_HEADS, B_V_DIM)
    return ka, va, kb, vb


def rope_diff(t, tab):
    b, n = t.shape[:2]
    return apply_axial_rope(t.reshape(b, n, B_HEADS * 2, B_QK_DIM), tab).reshape(t.shape)


def mixer_ab(hx, hc, w_in, w_out, sink, lam_vec, subln_g, lam_init, tab, ctx_queries):
    p = hx @ w_in
    qa, qb = ab_queries(p[..., :AB_Q_W])
    ka, va, kb, vb = ab_keys_values(p[..., AB_Q_W:])
    qa, ka = apply_axial_rope(qa, tab), apply_axial_rope(ka, tab)
    qb, kb = rope_diff(qb, tab), rope_diff(kb, tab)
    if ctx_queries:
        pc = hc @ w_in
        cqa, cqb = ab_queries(pc[..., :AB_Q_W])
        cka, cva, ckb, cvb = ab_keys_values(pc[..., AB_Q_W:])
    else:
        cka, cva, ckb, cvb = ab_keys_values(hc @ w_in[:, AB_Q_W:])
    lv = lam_vec.astype(F32)
    lam = jnp.exp(jnp.sum(lv[0] * lv[1])) - jnp.exp(jnp.sum(lv[2] * lv[3])) + lam_init

    oa = window_gqa_latent(qa, ka, va, cka, cva, sink)
    ob = from_blocks(lax.map(lambda qblk: diff_attend(qblk, [(kb, vb), (ckb, cvb)], lam),
                             to_blocks(qb)))
    yx = jnp.concatenate([oa, diff_out(ob, subln_g, lam_init)], axis=-1) @ w_out
    if not ctx_queries:
        return yx, None
    oca = gqa_ctx(cqa, cka, cva, sink)
    ocb = diff_out(diff_attend(cqb, [(ckb, cvb)], lam), subln_g, lam_init)
    yc = jnp.concatenate([oca, ocb], axis=-1) @ w_out
    return yx, yc


def mla_queries(p_q, q_norm_g, wq_b):
    b, t = p_q.shape[:2]
    q = (rmsnorm(p_q, q_norm_g) @ wq_b).reshape(b, t, C_HEADS, C_NOPE + C_ROPE)
    return q[..., :C_NOPE], q[..., C_NOPE:]


def mla_keys_values(p_kv, kv_norm_g, wkv_b):
    b, t = p_kv.shape[:2]
    kv_lat, k_rope = p_kv[..., :C_KV_LORA], p_kv[..., C_KV_LORA:]
    kv = (rmsnorm(kv_lat, kv_norm_g) @ wkv_b).reshape(b, t, C_HEADS, C_NOPE + C_V)
    return kv[..., :C_NOPE], k_rope, kv[..., C_NOPE:]


def mla_attend(q_nope, q_rope, kv_parts):
    scale = (C_NOPE + C_ROPE) ** -0.5
    logits = [(jnp.einsum('bqhd,bkhd->bhqk', q_nope, kn)
               + jnp.einsum('bqhr,bkr->bhqk', q_rope, kr)).astype(F32) * scale
              for kn, kr, _ in kv_parts]
    probs = joint_softmax(*logits)
    outs = [jnp.einsum('bhqk,bkhd->bqhd', p.astype(v.dtype), v) for p, (_, _, v) in zip(probs, kv_parts)]
    return functools.reduce(jnp.add, outs)


def mixer_mla(hx, hc, w_in, q_norm_g, kv_norm_g, wq_b, wkv_b, w_out, tab, ctx_queries):
    bsz, n, _ = hx.shape
    p = hx @ w_in
    qn, qr = mla_queries(p[..., :C_Q_LORA], q_norm_g, wq_b)
    kn, kr, v = mla_keys_values(p[..., C_Q_LORA:], kv_norm_g, wkv_b)
    qr = apply_axial_rope(qr, tab)
    kr = apply_axial_rope(kr[:, :, None, :], tab)[:, :, 0, :]
    if ctx_queries:
        pc = hc @ w_in
        cqn, cqr = mla_queries(pc[..., :C_Q_LORA], q_norm_g, wq_b)
        ckn, ckr, cv = mla_keys_values(pc[..., C_Q_LORA:], kv_norm_g, wkv_b)
    else:
        ckn, ckr, cv = mla_keys_values(hc @ w_in[:, C_Q_LORA:], kv_norm_g, wkv_b)
    o = from_blocks(lax.map(lambda qs: mla_attend(qs[0], qs[1], [(kn, kr, v), (ckn, ckr, cv)]),
                            (to_blocks(qn), to_blocks(qr))))
    yx = o.reshape(bsz, n, C_OUT_W) @ w_out
    if not ctx_queries:
        return yx, None
    oc = mla_attend(cqn, cqr, [(ckn, ckr, cv)])
    yc = oc.reshape(bsz, hc.shape[1], C_OUT_W) @ w_out
    return yx, yc


def setup_inputs(seed: int = 0) -> dict:
    key = jax.random.key(seed)
    ks = jax.random.split(key, 20)

    def nrm(k, shape, scale):
        return jax.random.normal(k, shape, F32) * scale

    d = D_MODEL
    return {
        'x': nrm(ks[0], (BATCH, SEQ, d), 1.0),
        'c': nrm(ks[1], (BATCH, d), 1.0),
        'ctx': nrm(ks[2], (BATCH, CTX_LEN, d), 1.0),
        'c_ctx': nrm(ks[3], (d,), 1.0),
        'ada_w': nrm(ks[4], (DEPTH, d, 6 * d), 0.5 * d ** -0.5),
        'ada_b': nrm(ks[5], (DEPTH, 6 * d), 0.01),
        'norm_g': 1.0 + nrm(ks[6], (DEPTH, 4, d), 0.05),
        'ffn_w13': nrm(ks[7], (DEPTH, d, 2 * FFN_HIDDEN), d ** -0.5),
        'ffn_w2': nrm(ks[8], (DEPTH, FFN_HIDDEN, d), FFN_HIDDEN ** -0.5),
        'ab_w_in': nrm(ks[9], (N_EVEN, d, AB_IN_W), d ** -0.5),
        'ab_w_out': nrm(ks[10], (N_EVEN, AB_OUT_W, d), AB_OUT_W ** -0.5),
        'ab_sink': nrm(ks[11], (N_EVEN, A_HEADS), 0.5),
        'diff_lambda': nrm(ks[12], (N_EVEN, 4, B_QK_DIM), 0.1),
        'diff_subln_g': 1.0 + nrm(ks[13], (N_EVEN, B_V_DIM), 0.05),
        'mla_w_in': nrm(ks[14], (N_ODD, d, C_IN_W), d ** -0.5),
        'mla_q_norm_g': 1.0 + nrm(ks[15], (N_ODD, C_Q_LORA), 0.05),
        'mla_kv_norm_g': 1.0 + nrm(ks[16], (N_ODD, C_KV_LORA), 0.05),
        'mla_wq_b': nrm(ks[17], (N_ODD, C_Q_LORA, C_HEADS * (C_NOPE + C_ROPE)), C_Q_LORA ** -0.5),
        'mla_wkv_b': nrm(ks[18], (N_ODD, C_KV_LORA, C_HEADS * (C_NOPE + C_V)), C_KV_LORA ** -0.5),
        'mla_w_out': nrm(ks[19], (N_ODD, C_OUT_W, d), C_OUT_W ** -0.5),
    }


def reference(x, c, ctx, c_ctx, ada_w, ada_b, norm_g, ffn_w13, ffn_w2,
              ab_w_in, ab_w_out, ab_sink, diff_lambda, diff_subln_g,
              mla_w_in, mla_q_norm_g, mla_kv_norm_g, mla_wq_b, mla_wkv_b, mla_w_out):
    bsz, n, d = x.shape
    ROWS = n // GRID_W
    tab_ab = axial_rope_tables(ROWS, A_HEAD_DIM)
    tab_c = axial_rope_tables(ROWS, C_ROPE)
    sc_x = jax.nn.silu(c)
    sc_c = jax.nn.silu(c_ctx)
    for l in range(DEPTH):
        ctx_out = l < DEPTH - 1
        mx = (sc_x @ ada_w[l] + ada_b[l]).reshape(bsz, 6, 1, d)
        mc = (sc_c @ ada_w[l] + ada_b[l]).reshape(6, d)
        hx = modulate(rmsnorm(x, norm_g[l, 0]), mx[:, 0], mx[:, 1])
        hc = modulate(rmsnorm(ctx, norm_g[l, 0]), mc[0], mc[1])
        if l % 2 == 0:
            e = l // 2
            yx, yc = mixer_ab(hx, hc, ab_w_in[e], ab_w_out[e], ab_sink[e], diff_lambda[e],
                              diff_subln_g[e], lambda_init(l), tab_ab, ctx_out)
        else:
            o = l // 2
            yx, yc = mixer_mla(hx, hc, mla_w_in[o], mla_q_norm_g[o], mla_kv_norm_g[o],
                               mla_wq_b[o], mla_wkv_b[o], mla_w_out[o], tab_c, ctx_out)
        x = x + mx[:, 2] * rmsnorm(yx, norm_g[l, 1])
        hx = modulate(rmsnorm(x, norm_g[l, 2]), mx[:, 3], mx[:, 4])
        x = x + mx[:, 5] * rmsnorm(swiglu(hx, ffn_w13[l], ffn_w2[l]), norm_g[l, 3])
        if ctx_out:
            ctx = ctx + mc[2] * rmsnorm(yc, norm_g[l, 1])
            hc = modulate(rmsnorm(ctx, norm_g[l, 2]), mc[3], mc[4])
            ctx = ctx + mc[5] * rmsnorm(swiglu(hc, ffn_w13[l], ffn_w2[l]), norm_g[l, 3])
    return x
```

```python
import numpy as np
import concourse.bass as bass
import concourse.mybir as mybir
from concourse.alu_op_type import AluOpType as ALU
from concourse.bass_utils import run_bass_kernel_spmd

AF = mybir.ActivationFunctionType
AX = mybir.AxisListType
F32 = mybir.dt.float32
BF16 = mybir.dt.bfloat16


class Buf:
    __slots__ = ("name", "last_w", "readers")

    def __init__(self, name):
        self.name = name
        self.last_w = None
        self.readers = []


class Op:
    __slots__ = ("eng", "fn", "deps", "needed", "sem", "val", "chan", "ndma", "idx", "cinc")

    def __init__(self, eng, fn, chan=None):
        self.eng = eng
        self.fn = fn
        self.deps = []
        self.needed = False
        self.sem = None
        self.val = None
        self.chan = chan
        self.ndma = 0
        self.idx = -1
        self.cinc = 16


ENGS = ("tensor", "scalar", "vector", "gpsimd", "sync")


class Prog:
    def __init__(self, nc):
        self.nc = nc
        self.ops = {e: [] for e in ENGS}
        self.nops = 0
        self.chan_last = {}
        self.bufs = {}
        self.bar = []
        self.bar_pending = set()

    def barrier(self):
        b = [self.ops[e][-1] for e in ENGS if self.ops[e]]
        b += list(self.chan_last.values())
        self.bar = b
        self.bar_pending = set(ENGS)

    def buf(self, name):
        b = self.bufs.get(name)
        if b is None:
            b = Buf(name)
            self.bufs[name] = b
        return b

    def _B(self, x):
        return x if isinstance(x, Buf) else self.buf(x)

    def op(self, eng, fn, reads=(), writes=(), chan=None, ndma=1, cinc=16):
        o = Op(eng, fn, chan)
        o.idx = self.nops
        self.nops += 1
        if eng in self.bar_pending:
            self.bar_pending.discard(eng)
            for p in self.bar:
                self._dep(o, p)
        if chan is not None:
            o.ndma = ndma
            o.cinc = cinc
            prev = self.chan_last.get(chan)
            if prev is not None:
                self._dep(o, prev)
            self.chan_last[chan] = o
        for r in reads:
            b = self._B(r)
            if b.last_w is not None:
                self._dep(o, b.last_w)
            b.readers.append(o)
        for w in writes:
            b = self._B(w)
            if b.last_w is not None:
                self._dep(o, b.last_w)
            for rd in b.readers:
                if rd is not o:
                    self._dep(o, rd)
            b.last_w = o
            b.readers = []
        self.ops[eng].append(o)
        return o

    def _dep(self, o, p):
        if p is o:
            return
        if p.eng == "tensor" and o.eng == "tensor" and p.chan is None and o.chan is None:
            return
        for d in o.deps:
            if d is p:
                return
        o.deps.append(p)
        p.needed = True

    def emit(self, final_ops=()):
        nc = self.nc
        for o in final_ops:
            o.needed = True
        import contextlib
        with contextlib.ExitStack() as st:
            esem = {e: st.enter_context(nc.semaphore("s_" + e)) for e in ENGS}
            chans = sorted({o.chan for e in ENGS for o in self.ops[e] if o.chan is not None})
            csem = {c: st.enter_context(nc.semaphore("c_" + c)) for c in chans}
            self.nsem = len(esem) + len(csem)
            ccount = {c: 0 for c in chans}
            allops = sorted((o for e in ENGS for o in self.ops[e]), key=lambda o: o.idx)
            ecount = {e: 0 for e in ENGS}
            for o in allops:
                if o.chan is not None:
                    ccount[o.chan] += o.cinc * o.ndma
                    o.sem, o.val = csem[o.chan], ccount[o.chan]
                elif o.needed:
                    ecount[o.eng] += 1
                    o.sem, o.val = esem[o.eng], ecount[o.eng]
            self.ecount = ecount
            block = st.enter_context(nc.Block())

            def run(ename, e):
                seen = {}
                for o in self.ops[ename]:
                    for d in o.deps:
                        k = id(d.sem)
                        if seen.get(k, 0) >= d.val:
                            continue
                        e.wait_ge(d.sem, d.val)
                        seen[k] = d.val
                    r = o.fn(e)
                    if o.chan is not None:
                        rs = r if isinstance(r, (list, tuple)) else [r]
                        assert len(rs) == o.ndma, (len(rs), o.ndma)
                        for ins in rs:
                            if o.cinc == 1:
                                ins.then_inc(o.sem)
                            else:
                                ins.then_inc(o.sem, o.cinc)
                    elif o.needed:
                        r.then_inc(o.sem, 1)
                if ename == "sync":
                    for o in final_ops:
                        e.wait_ge(o.sem, o.val)

            @block.tensor
            def _(e):
                run("tensor", e)

            @block.scalar
            def _(e):
                run("scalar", e)

            @block.vector
            def _(e):
                run("vector", e)

            @block.gpsimd
            def _(e):
                run("gpsimd", e)

            @block.sync
            def _(e):
                run("sync", e)


import contextlib
import math
import numpy as np

D = 1024
NT = 8192
OWN = 2048
CTX = 256
NQ = OWN + CTX
NK = NT + CTX
NKB = NK // 128
FFH = 2816
EPS = 1e-6
LAM0 = 0.8 - 0.6 * math.exp(-0.3 * 0)


class KB:
    def __init__(self, mode, dbg=()):
        self.mode = mode
        self.dbg = set(dbg)
        self.nc = bass.Bass("TRN2", target_bir_lowering=False)
        self.P = Prog(self.nc)
        self.st = contextlib.ExitStack()
        self.finals = []
        self.cnt = {}

    def din(self, name, shape, dt=F32):
        return self.nc.dram_tensor(name, list(shape), dt, kind="ExternalInput").ap()

    def dout(self, name, shape, dt=F32):
        return self.nc.dram_tensor(name, list(shape), dt, kind="ExternalOutput").ap()

    def dint(self, name, shape, dt=BF16):
        return self.nc.dram_tensor(name, list(shape), dt, kind="Internal").ap()

    def sb(self, name, shape, dt):
        return self.st.enter_context(self.nc.sbuf_tensor(name, list(shape), dt))

    def rot(self, key, n):
        v = self.cnt.get(key, 0)
        self.cnt[key] = v + 1
        return v % n

    def dma(self, q, out, in_, reads, writes, chan, **kw):
        return self.P.op(q, lambda e: e.dma_start(out=out, in_=in_, **kw), reads, writes, chan=chan)

    def mm(self, out, lhsT, rhs, start, stop, reads, writes):
        return self.P.op("tensor", lambda e: e.matmul(out, lhsT=lhsT, rhs=rhs, start=start, stop=stop,
                                                      skip_group_check=True), reads, writes)

    def tr(self, out, in_, reads, writes):
        ident = self.ident
        return self.P.op("tensor", lambda e: e.transpose(out, in_, ident[:]), list(reads) + ["ident"], writes)

    def act(self, out, in_, func, reads, writes, **kw):
        return self.P.op("scalar", lambda e: e.activation(out=out, in_=in_, func=func, **kw), reads, writes)

    def tt(self, eng, out, in0, in1, op, reads, writes):
        return self.P.op(eng, lambda e: e.tensor_tensor(out=out, in0=in0, in1=in1, op=op), reads, writes)

    def ts(self, eng, out, in0, s1, s2, op0, op1, reads, writes, **kw):
        if op1 is None:
            return self.P.op(eng, lambda e: e.tensor_scalar(out=out, in0=in0, scalar1=s1, scalar2=None, op0=op0, **kw), reads, writes)
        return self.P.op(eng, lambda e: e.tensor_scalar(out=out, in0=in0, scalar1=s1, scalar2=s2, op0=op0, op1=op1, **kw), reads, writes)

    def cp(self, eng, out, in_, reads, writes):
        return self.P.op(eng, lambda e: e.tensor_copy(out=out, in_=in_), reads, writes)

    def debug_out(self, name, src_ap, shape, reads, dt=F32):
        if name not in self.dbg:
            return
        o = self.dout("dbg_" + name, shape, dt)
        f = self.dma("sync", o, src_ap, reads, [], "dbg_" + name)
        self.finals.append(f)

    def setup_common(self):
        nc, P = self.nc, self.P
        self.psum = self.st.enter_context(nc.psum_tensor("psum", [128, 4096], F32))
        self.bank = [self.psum[:, i * 512:(i + 1) * 512] for i in range(8)]
        self.ident = self.sb("ident", [128, 128], BF16)
        ident = self.ident
        P.op("gpsimd", lambda e: e.memset(ident[:], 0.0), [], ["ident"])
        P.op("gpsimd", lambda e: e.affine_select(out=ident[:], in_=ident[:], pattern=[[-1, 128]],
                                                 compare_op=ALU.not_equal, fill=1.0, base=0,
                                                 channel_multiplier=1), ["ident"], ["ident"])
        self.xres = self.sb("xres", [128, 18, D], F32)
        self.big = self.sb("big", [128, 23040], BF16)
        self.mid = self.sb("mid", [128, 16896], BF16)
        self.sm = self.sb("sm", [128, 9216], BF16)
        self.aux = self.sb("aux", [128, 6144], BF16)
        self.gates = self.sb("gates", [128, 4, D], F32)
        self.modfm = self.sb("modfm", [128, 2, 8, 8], F32)
        self.ssq = self.sb("ssq", [128, 8], F32)
        self.rstd = self.sb("rstd", [128, 8], F32)
        self.sqj = self.sb("sqj", [128, D], BF16)
        self.perm64 = self.sb("perm64_sb", [128, 128], BF16)
        self.perm96 = self.sb("perm96_sb", [128, 128], BF16)
        self.epsb = self.sb("epsb", [128, 1], F32)
        epsb = self.epsb
        P.op("gpsimd", lambda e: e.memset(epsb[:], EPS), [], ["epsb"])

    def v_big(self, off, shape, dt=BF16):
        n = int(np.prod(shape))
        if dt == F32:
            ap = self.big[:, off:off + 2 * n].bitcast(F32)
        else:
            ap = self.big[:, off:off + n]
        return self._shape(ap, shape)

    def v_mid(self, off, shape, dt=BF16):
        n = int(np.prod(shape))
        if dt == F32:
            ap = self.mid[:, off:off + 2 * n].bitcast(F32)
        else:
            ap = self.mid[:, off:off + n]
        return self._shape(ap, shape)

    def v_sm(self, off, shape, dt=BF16):
        n = int(np.prod(shape))
        if dt == F32:
            ap = self.sm[:, off:off + 2 * n].bitcast(F32)
        else:
            ap = self.sm[:, off:off + n]
        return self._shape(ap, shape)

    @staticmethod
    def _shape(ap, shape):
        if len(shape) == 1:
            return ap
        if len(shape) == 2:
            return ap.rearrange("p (a b) -> p a b", b=shape[1])
        if len(shape) == 3:
            return ap.rearrange("p (a b c) -> p a b c", b=shape[1], c=shape[2])
        raise ValueError

    def phase_mod(self, layer, cfm, adaw, adab_fm, adab_row, ng_fm, ng_row):
        P = self.P
        P.barrier()
        bk = self.bank[7]
        sc = self.v_sm(0, [8, 2], F32)
        scb = self.v_sm(64, [8, 2], BF16)
        scbc = self.v_sm(128, [2, 8, 128], BF16)
        bfm = self.v_sm(2304, [48], F32)
        gfm = self.v_sm(2304 + 96, [4, 8], F32)
        modT = self.v_sm(2304 + 96 + 64, [48, 2], F32)
        brow = self.v_big(0, [2, D], F32)
        grow = self.v_big(4096, [2, D], F32)
        self.dma("sync", sc, cfm, [], ["m_sc"], "m_sc")
        self.dma("sync", bfm, adab_fm, [], ["m_bfm"], "m_bfm")
        self.dma("sync", gfm, ng_fm, [], ["m_gfm"], "m_gfm")
        self.P.op("sync", lambda e: [e.dma_start(out=brow[:, 0, :], in_=adab_row[2 * D:3 * D].partition_broadcast(128)),
                                     e.dma_start(out=brow[:, 1, :], in_=adab_row[5 * D:6 * D].partition_broadcast(128)),
                                     e.dma_start(out=grow[:, 0, :], in_=ng_row[1, :].partition_broadcast(128)),
                                     e.dma_start(out=grow[:, 1, :], in_=ng_row[3, :].partition_broadcast(128))],
                  [], ["m_rows"], chan="m_rows", ndma=4)
        self.act(sc, sc, AF.Silu, ["m_sc"], ["m_sc"])
        self.cp("vector", scb, sc, ["m_sc"], ["m_scb"])
        for j in range(2):
            self.cp("vector", scbc[:, j], sc[:, :, j:j + 1].to_broadcast([128, 8, 128]), ["m_sc"], ["m_scbc"])
        psT = bk[:, 0:96].rearrange("p (a b) -> p a b", b=2)
        for j in range(6):
            s = j % 2
            wt = self.v_mid(s * 8192, [8, D])
            self.dma("gpsimd", wt, adaw[j], [], [f"m_w{s}"], f"m_w{s}")
            for cc in range(8):
                for k in range(8):
                    self.mm(psT[:, j * 8 + cc, :], wt[:, k, cc * 128:(cc + 1) * 128], scb[:, k, :], k == 0,
                            k == 7, [f"m_w{s}", "m_scb"], ["ps7"])
            if j in (2, 5):
                gi = 0 if j == 2 else 1
                for t in range(2):
                    for hh in range(2):
                        pb = self.bank[5 + hh]
                        for k in range(8):
                            self.mm(pb, scbc[:, t, k, :], wt[:, k, hh * 512:(hh + 1) * 512], k == 0, k == 7,
                                    [f"m_w{s}", "m_scbc"], [f"ps{5 + hh}"])
                        g = self.gates[:, 2 * t + gi, hh * 512:(hh + 1) * 512]
                        self.tt("vector", g, pb, brow[:, gi, hh * 512:(hh + 1) * 512], ALU.add, [f"ps{5 + hh}", "m_rows"], ["gates"])
                        self.tt("vector", g, g, grow[:, gi, hh * 512:(hh + 1) * 512], ALU.mult, ["gates", "m_rows"], ["gates"])
        self.tt("vector", modT, psT, bfm.unsqueeze(2).to_broadcast([128, 48, 2]), ALU.add, ["ps7", "m_bfm"], ["m_modT"])
        mf = self.modfm
        for t in range(2):
            self.P.op("vector", lambda e, t=t: e.scalar_tensor_tensor(out=mf[:, layer, 4 * t + 0, :], in0=modT[:, 8:16, t], scalar=1.0,
                                                                      in1=gfm[:, 0, :], op0=ALU.add, op1=ALU.mult),
                      ["m_modT", "m_gfm"], ["modfm"])
            self.cp("vector", mf[:, layer, 4 * t + 1, :], modT[:, 0:8, t], ["m_modT"], ["modfm"])
            self.P.op("vector", lambda e, t=t: e.scalar_tensor_tensor(out=mf[:, layer, 4 * t + 2, :], in0=modT[:, 32:40, t], scalar=1.0,
                                                                      in1=gfm[:, 2, :], op0=ALU.add, op1=ALU.mult),
                      ["m_modT", "m_gfm"], ["modfm"])
            self.cp("vector", mf[:, layer, 4 * t + 3, :], modT[:, 24:32, t], ["m_modT"], ["modfm"])
        self.debug_out(f"modfm{layer}", mf[:, layer], [128, 8, 8], ["modfm"])
        self.debug_out(f"gates{layer}", self.gates[:], [128, 4, D], ["gates"])

    def norm_a(self, xsrc, xbuf, nslot=2):
        c = self.rot("ssq", 8)
        xs = self.rot(f"xn{nslot}", nslot)
        xn = self.v_sm(4096 + xs * 1024, [D])
        ssq = self.ssq[:, c:c + 1]
        rstd = self.rstd[:, c:c + 1]
        self.act(self.sqj[:], xsrc, AF.Square, [xbuf], ["sqj", f"ssq{c}"], accum_out=ssq)
        epsb = self.epsb
        self.act(rstd, ssq, AF.Sqrt, [f"ssq{c}", "epsb"], [f"rstd{c}"], scale=1.0 / D, bias=epsb[:])
        self.P.op("vector", lambda e: e.reciprocal(rstd, rstd), [f"rstd{c}"], [f"rstd{c}"])
        self.act(xn, xsrc, AF.Copy, [xbuf, f"rstd{c}"], [f"xn{xs}"], scale=rstd)
        return xn, f"xn{xs}"

    def norm_b(self, xn, xnbuf, dst, dstbuf, layer, t, stage):
        pT = self.bank[0].bitcast(BF16).rearrange("p (k t) -> p k t", t=128)
        for k in range(8):
            self.tr(pT[:, k, :], xn[:, k * 128:(k + 1) * 128], [xnbuf], ["ps0"])
        Aap = self.modfm[:, layer, 4 * t + 2 * stage, :]
        Bap = self.modfm[:, layer, 4 * t + 2 * stage + 1, :]
        self.tt("vector", dst, pT, Aap.unsqueeze(2).to_broadcast([128, 8, 128]), ALU.mult, ["ps0", "modfm"], [dstbuf])
        self.tt("vector", dst, dst, Bap.unsqueeze(2).to_broadcast([128, 8, 128]), ALU.add, [dstbuf, "modfm"], [dstbuf])

    def norm_T(self, xsrc, xbuf, dst, dstbuf, layer, t, stage):
        xn, xnbuf = self.norm_a(xsrc, xbuf)
        self.norm_b(xn, xnbuf, dst, dstbuf, layer, t, stage)

    A_SLOT = {**{b: b for b in range(17)}, 63: 17, 64: 18, 65: 19}

    def setup_l0(self):
        self.kAT = self.aux[:, 0:2560].rearrange("p (s t) -> p s t", t=128)
        self.vA = self.aux[:, 2560:5160].rearrange("p (s g d) -> p s g d", g=2, d=65)
        vA = self.vA
        self.P.op("gpsimd", lambda e: e.memset(vA[:, :, :, 64:65], 1.0), [], ["vA"])

    def load_perms(self, I):
        p64, p96 = self.perm64, self.perm96
        self.P.op("gpsimd", lambda e: [e.dma_start(out=p64[:], in_=I["perm64"]), e.dma_start(out=p96[:], in_=I["perm96"])],
                  [], ["permM"], chan="permM", ndma=2)

    def load_w(self, dst, src, kchunks, name="wbig"):
        return self.dma("gpsimd", dst, src, [], [name], name)

    def fm_rope_chunk(self, ci, ntok, hxT, hbuf, wf, wp, col0, M, tabC, tabS, tbuf, wname="wbig", perm=None):
        b = self.rot("fmbank", 2)
        pm, pp = self.bank[1 + b], self.bank[3 + b]
        K = hxT.shape[1]
        for k in range(K):
            self.mm(pm[0:M, :ntok], wf[:, k, col0:col0 + M], hxT[:, k, :ntok], k == 0, k == K - 1, [hbuf, wname], [f"ps{1 + b}"])
        kr = self.rot("kraw", 2)
        kraw = self.v_sm(8192 + kr * 512, [512])
        self.act(kraw[0:M, :ntok], pm[0:M, :ntok], AF.Copy, [f"ps{1 + b}"], [f"kraw{kr}"])
        pmat = self.perm64 if perm is None else perm
        self.mm(pp[0:M, :ntok], pmat[0:M, 0:M], kraw[0:M, :ntok], True, True, [f"kraw{kr}", "permM"], [f"ps{3 + b}"])
        t1 = self.v_big(20512, [512], F32)
        t2 = self.v_big(21536, [512], F32)
        self.tt("vector", t1[0:M, :ntok], pm[0:M, :ntok], tabC[0:M, :ntok], ALU.mult, [f"ps{1 + b}", tbuf, f"kraw{kr}"], ["t1"])
        self.tt("vector", t2[0:M, :ntok], pp[0:M, :ntok], tabS[0:M, :ntok], ALU.mult, [f"ps{3 + b}", tbuf], ["t2"])
        ks = self.rot("kst", 4)
        kst = self.v_mid(12288 + ks * 512, [512])
        self.tt("gpsimd", kst[0:M, :ntok], t1[0:M, :ntok], t2[0:M, :ntok], ALU.add, ["t1", "t2"], [f"kst{ks}"])
        return kst, f"kst{ks}"

    def phase_kv0(self, I):
        P = self.P
        P.barrier()
        wkf = self.v_big(0, [8, 640])
        wkp = self.v_big(5120, [8, 640])
        wv = self.v_big(10240, [8, 640])
        self.load_w(wkf, I["w0k_f"], 8)
        self.load_w(wv, I["w0v"], 8)
        vst = [self.v_big(16384 + s * 2064, [4, 4, 129]) for s in range(2)]
        for s in range(2):
            P.op("gpsimd", lambda e, s=s: e.memset(vst[s][:, :, :, 128:129], 1.0), [], [f"vst{s}"])
        groups = [list(range(4 * g, 4 * g + 4)) for g in range(16)] + [[64, 65]]
        def geom(gi):
            blocks = groups[gi]
            nb = len(blocks)
            ntok = 128 * nb
            tok0 = blocks[0] * 128
            hs = gi % 2
            hxT = self.v_mid(hs * 4096, [8, 512])
            tabC = self.v_mid(8192 + hs * 2048, [512], F32)
            tabS = self.v_mid(8192 + hs * 2048 + 1024, [512], F32)
            return blocks, nb, ntok, tok0, hs, hxT, tabC, tabS

        xns = {}

        def part_norm(gi):
            blocks, nb, ntok, tok0, hs, hxT, tabC, tabS = geom(gi)
            P.op("sync", lambda e, tabC=tabC, tabS=tabS, tok0=tok0, ntok=ntok: [
                e.dma_start(out=tabC[:, :ntok], in_=I["ropeC0"][:, tok0:tok0 + ntok]),
                e.dma_start(out=tabS[:, :ntok], in_=I["ropeS0"][:, tok0:tok0 + ntok])], [], [f"tab{hs}"], chan=f"tab{hs}", ndma=2)
            for j, blk in enumerate(blocks):
                if blk < 16:
                    xs, xb = self.xres[:, blk, :], f"xres{blk}"
                    self.dma("sync", xs, I["xk"][blk * 128:(blk + 1) * 128, :], [], [xb], f"xres{blk % 2}")
                elif blk >= 64:
                    xs, xb = self.xres[:, 16 + blk - 64, :], f"xres{16 + blk - 64}"
                    self.dma("sync", xs, I["ctxin"][(blk - 64) * 128:(blk - 63) * 128, :], [], [xb], f"xres{blk % 2}")
                else:
                    s = self.rot("xin", 2)
                    xs, xb = self.v_sm(s * 2048, [D], F32), f"xin{s}"
                    self.dma("sync", xs, I["xk"][blk * 128:(blk + 1) * 128, :], [], [xb], f"xin{s}")
                xns[(gi, j)] = self.norm_a(xs, xb, 4)

        def part_b(gi):
            blocks, nb, ntok, tok0, hs, hxT, tabC, tabS = geom(gi)
            for j, blk in enumerate(blocks):
                xn, xnb = xns.pop((gi, j))
                self.norm_b(xn, xnb, hxT[:, :, j * 128:(j + 1) * 128], f"hxT{hs}", 0, 1 if blk >= 64 else 0, 0)

        def part_proj(gi):
            blocks, nb, ntok, tok0, hs, hxT, tabC, tabS = geom(gi)
            needA = [(j, self.A_SLOT[b]) for j, b in enumerate(blocks) if b in self.A_SLOT]
            for ci in range(5):
                if ci == 0 and not needA:
                    continue
                kst, kb = self.fm_rope_chunk(ci, ntok, hxT, f"hxT{hs}", wkf, wkp, ci * 128, 128, tabC, tabS, f"tab{hs}")
                self.dma("sync", I["KT0"][ci, :, tok0:tok0 + ntok], kst[:, :ntok], [kb], [], f"kt_st{self.rot('ktst', 4)}")
                if ci == 0:
                    for j, slot in needA:
                        self.cp("gpsimd", self.kAT[:, slot, :], kst[:, j * 128:(j + 1) * 128], [kb], ["kAT"])
            vs = gi % 2
            for j, blk in enumerate(blocks):
                vb_ = 5 + self.rot("vbank", 2)
                pv, pa = self.bank[vb_], self.bank[7]
                for k in range(8):
                    self.mm(pv, hxT[:, k, j * 128:(j + 1) * 128], wv[:, k, 128:640], k == 0, k == 7, [f"hxT{hs}", "wbig"], [f"ps{vb_}"])
                self.act(vst[vs][:, :, j, 0:128], pv.rearrange("p (h d) -> p h d", d=128), AF.Copy, [f"ps{vb_}"], [f"vst{vs}"])
                if blk in self.A_SLOT:
                    for k in range(8):
                        self.mm(pa[:, 0:128], hxT[:, k, j * 128:(j + 1) * 128], wv[:, k, 0:128], k == 0, k == 7, [f"hxT{hs}", "wbig"], ["ps7"])
                    self.act(self.vA[:, self.A_SLOT[blk], :, 0:64], pa[:, 0:128].rearrange("p (h d) -> p h d", d=64), AF.Copy, ["ps7"], ["vA"])
            b0 = blocks[0]
            self.dma("sync", I["VB0"][:, :, b0:b0 + nb, :].rearrange("h p b d -> p h b d"), vst[vs][:, :, 0:nb, :], [f"vst{vs}"], [], f"vb_st{vs}")

        part_norm(0)
        part_b(0)
        for gi in range(len(groups)):
            if gi + 1 < len(groups):
                part_norm(gi + 1)
            part_proj(gi)
            if gi + 1 < len(groups):
                part_b(gi + 1)
        self.debug_out("kAT", self.kAT[:], [128, 20, 128], ["kAT"], BF16)
        self.debug_out("vA", self.vA[:], [128, 20, 2, 65], ["vA"], BF16)

    def phase_q0(self, I):
        P = self.P
        P.barrier()
        wqf = self.v_big(0, [8, 1024])
        wqp = self.v_big(8192, [8, 1024])
        self.load_w(wqf, I["w0q_f"], 8)
        groups = [list(range(4 * g, 4 * g + 4)) for g in range(4)] + [[16, 17]]
        def geom(gi):
            blocks = groups[gi]
            nb = len(blocks)
            ntok = 128 * nb
            q0 = blocks[0] * 128
            tok0 = q0 if blocks[0] < 16 else NT
            hs = gi % 2
            hxT = self.v_mid(hs * 4096, [8, 512])
            tabC = self.v_mid(8192 + hs * 2048, [512], F32)
            tabS = self.v_mid(8192 + hs * 2048 + 1024, [512], F32)
            return blocks, ntok, q0, tok0, hs, hxT, tabC, tabS

        def part_norm(gi):
            blocks, ntok, q0, tok0, hs, hxT, tabC, tabS = geom(gi)
            P.op("sync", lambda e, tabC=tabC, tabS=tabS, tok0=tok0, ntok=ntok: [
                e.dma_start(out=tabC[:, :ntok], in_=I["ropeC0"][:, tok0:tok0 + ntok]),
                e.dma_start(out=tabS[:, :ntok], in_=I["ropeS0"][:, tok0:tok0 + ntok])], [], [f"tab{hs}"], chan=f"tab{hs}", ndma=2)
            for j, blk in enumerate(blocks):
                xns[(gi, j)] = self.norm_a(self.xres[:, blk, :], f"xres{blk}", 4)

        def part_b(gi):
            blocks, ntok, q0, tok0, hs, hxT, tabC, tabS = geom(gi)
            for j, blk in enumerate(blocks):
                xn, xnb = xns.pop((gi, j))
                self.norm_b(xn, xnb, hxT[:, :, j * 128:(j + 1) * 128], f"hxT{hs}", 0, 1 if blk >= 16 else 0, 0)

        def part_proj(gi):
            blocks, ntok, q0, tok0, hs, hxT, tabC, tabS = geom(gi)
            for ci in range(8):
                kst, kb = self.fm_rope_chunk(ci, ntok, hxT, f"hxT{hs}", wqf, wqp, ci * 128, 128, tabC, tabS, f"tab{hs}")
                self.dma("sync", I["QT0"][ci, :, q0:q0 + ntok], kst[:, :ntok], [kb], [], f"kt_st{self.rot('ktst', 4)}")

        xns = {}
        part_norm(0)
        part_b(0)
        for gi in range(len(groups)):
            if gi + 1 < len(groups):
                part_norm(gi + 1)
            part_proj(gi)
            if gi + 1 < len(groups):
                part_b(gi + 1)

    def phase_attA(self, I):
        P = self.P
        P.barrier()
        osb = self.v_big(0, [18, 512])
        self.osb = osb
        self.oTB = self.v_big(9216, [4, NQ])
        amask = self.v_big(18432, [4, 512])
        self.dma("gpsimd", amask, I["amask"], [], ["amask"], "amask")
        esink = self.v_sm(0, [8], F32)
        self.dma("sync", esink, I["sink"].partition_broadcast(128), [], ["esink"], "esink")
        self.act(esink, esink, AF.Exp, ["esink"], ["esink"])
        ident = self.ident
        for qi in range(18):
            qs = self.rot("qa", 2)
            qa = self.v_mid(qs * 512, [4, 128])
            self.dma("sync", qa, I["QT0"][0:4, :, qi * 128:(qi + 1) * 128].rearrange("c p t -> p c t"), [], [f"qa{qs}"], f"qa{qs}")
            if qi < 16:
                kbs = [(qi - 1 if qi > 0 else 17, 0 if qi == 0 else 1), (qi, None), (qi + 1, 3 if qi == 15 else 2), (18, None), (19, None)]
            else:
                kbs = [(18, None), (19, None)]
            its = [(g, ki, slot, mk) for g in range(2) for ki, (slot, mk) in enumerate(kbs)]
            stA = {}

            def qk(i, qa=qa, qs=qs, its=its, stA=stA):
                g, ki, slot, mk = its[i]
                sbk = self.rot("psA", 2)
                ps = self.bank[sbk]
                self.mm(ps, self.kAT[64 * g:64 * g + 64, slot, :], qa[64 * g:64 * g + 64].rearrange("p c t -> p (c t)"),
                        True, mk is None, ["kAT", f"qa{qs}"], [f"ps{sbk}"])
                if mk is not None:
                    self.mm(ps, ident[:], amask[:, mk, :], False, True, ["ident", "amask"], [f"ps{sbk}"])
                stA[i] = sbk

            def exp_pv(i, qi=qi, its=its, stA=stA, nk=len(kbs)):
                g, ki, slot, mk = its[i]
                sbk = stA.pop(i)
                ps = self.bank[sbk]
                pov = self.bank[4 + g][:, 0:260].rearrange("p (c d) -> p c d", d=65)
                pt = self.rot("pTA", 3)
                pT = self.v_mid(1024 + pt * 512, [512])
                self.act(pT, ps, AF.Exp, [f"ps{sbk}"], [f"pTA{pt}"], scale=0.125)
                for c in range(4):
                    self.mm(pov[:, c, :], pT[:, c * 128:(c + 1) * 128], self.vA[:, slot, g, :], ki == 0 and c == 0,
                            ki == nk - 1, [f"pTA{pt}", "vA"], [f"ps{4 + g}"])
                if ki == nk - 1:
                    dn = self.rot("denA", 2)
                    den = self.v_sm(64 + dn * 16, [4], F32)
                    self.tt("vector", den, pov[:, :, 64], esink[:, 4 * g:4 * g + 4], ALU.add, [f"ps{4 + g}", "esink"], [f"denA{dn}"])
                    self.P.op("vector", lambda e, den=den: e.reciprocal(den, den), [f"denA{dn}"], [f"denA{dn}"])
                    self.tt("vector", osb[:, qi, g * 256:(g + 1) * 256].rearrange("p (c d) -> p c d", d=64), pov[:, :, 0:64],
                            den.unsqueeze(2).to_broadcast([128, 4, 64]), ALU.mult, [f"ps{4 + g}", f"denA{dn}"], [f"osb{qi}"])

            qk(0)
            for i in range(len(its)):
                if i + 1 < len(its):
                    qk(i + 1)
                exp_pv(i)

    def attn_pass(self, rows, pbase, q_src, k_srcs, v_src, dv1, scale, qgroups, fin, tag):
        per = 512 // dv1
        for (qb0, nqb, kbA, nkbs) in qgroups:
            ob = 4 + 2 * self.rot("obase", 2) if dv1 == 65 else 4
            nq = nqb * 128
            qs = self.rot("qT", 2)
            qT = self.v_mid(qs * 1024, [1024])
            self.dma("sync", qT[pbase:pbase + rows, :nq], q_src(qb0 * 128, nq), [], [f"qT{qs}"], f"qT{qs}")
            po = lambda qb, ob=ob: self.bank[ob + qb // per][:, (qb % per) * dv1:(qb % per + 1) * dv1]
            pobuf = lambda qb, ob=ob: f"ps{ob + qb // per}"
            started = set()
            its = []
            done = 0
            while done < nkbs:
                npb = min(11, nkbs - done)
                for kl in range(npb):
                    its.append((kbA + done, npb, kl, done + kl == nkbs - 1))
                done += npb
            state = {}
            loaded = {}

            def emit_qk(i):
                kb0, npb, kl, last = its[i]
                if kl == 0:
                    def load_piece(kb0_, npb_):
                        sl = self.rot("kvp", 3)
                        Kp = self.v_mid(2048 + sl * 1408, [1408])
                        Vp = self.v_mid(2048 + 3 * 1408 + sl * 1420, [11, 129])[:, 0:npb_, 0:dv1] if dv1 == 129 else \
                            self.v_mid(2048 + 3 * 1408 + sl * 1420, [11 * 65])[:, 0:npb_ * 65].rearrange("p (b d) -> p b d", d=65)
                        self.P.op("sync", lambda e, Kp=Kp: [
                            e.dma_start(out=Kp[pbase + ro:pbase + ro + nr, :npb_ * 128], in_=fn(kb0_ * 128, npb_ * 128)) for (ro, nr, fn) in k_srcs],
                            [], [f"Kp{sl}"], chan=f"Kp{sl}", ndma=len(k_srcs))
                        self.dma("sync", Vp, v_src(kb0_, npb_), [], [f"Vp{sl}"], f"Vp{sl}")
                        loaded[kb0_] = (sl, Kp, Vp)
                    if kb0 not in loaded:
                        load_piece(kb0, npb)
                    nxt = [(a_, b_) for (a_, b_, c_, d_) in its[i + 1:] if c_ == 0][:1]
                    for (a_, b_) in nxt:
                        if a_ not in loaded:
                            load_piece(a_, b_)
                    state["piece"] = loaded[kb0]
                sl, Kp, Vp = state["piece"]
                ss = self.rot("psS", 2)
                nh = (nq + 511) // 512
                for hh in range(nh):
                    w = min(512, nq - hh * 512)
                    self.mm(self.bank[2 * ss + hh][:, :w], Kp[pbase:pbase + rows, kl * 128:(kl + 1) * 128],
                            qT[pbase:pbase + rows, hh * 512:hh * 512 + w], True, True, [f"Kp{sl}", f"qT{qs}"], [f"psS{ss}"])
                state[i] = (ss, sl, Vp, kl, last)

            def emit_exp_pv(i):
                ss, sl, Vp, kl, last = state.pop(i)
                pt = self.rot("pT", 3)
                pT = self.v_mid(2048 + 3 * 1408 + 3 * 1420 + pt * 1024, [1024])
                self.act(pT[:, :nq], self.psum[:, 2 * ss * 512:2 * ss * 512 + nq], AF.Exp, [f"psS{ss}"], [f"pT{pt}"], scale=scale)
                for qb in range(nqb):
                    bk = ob + qb // per
                    st = bk not in started
                    started.add(bk)
                    self.mm(po(qb), pT[:, qb * 128:(qb + 1) * 128], Vp[:, kl, :], st, last, [f"pT{pt}", f"Vp{sl}"], [pobuf(qb)])

            emit_qk(0)
            for i in range(len(its)):
                if i + 1 < len(its):
                    emit_qk(i + 1)
                emit_exp_pv(i)
            fin(qb0, nqb, per, ob)

    def attn_pass_fm(self, pbase, q_src, k_src, v_src, scale, qg, fin):
        (qb0, nqb, kbA, nkbs) = qg
        rows = 64
        nq = nqb * 128
        qs = self.rot("qT", 2)
        qT = self.v_mid(qs * 1024, [1024])
        self.dma("sync", qT[pbase:pbase + rows, :nq], q_src(qb0 * 128, nq), [], [f"qT{qs}"], f"qT{qs}")
        ones = self.ones
        acc = self.v_mid(13604, [1024], F32)
        hi = self.v_sm(8192, [1024])
        lo = self.v_mid(15652, [1024])
        nh = (nq + 511) // 512
        its = []
        done = 0
        while done < nkbs:
            npb = min(11, nkbs - done)
            for kl in range(npb):
                its.append((kbA + done, npb, kl, done + kl == 0, done + kl == nkbs - 1))
            done += npb
        state = {}
        loaded = {}

        def emit_qk(i):
            kb0, npb, kl, first, last = its[i]
            if kl == 0:
                def load_piece(kb0_, npb_):
                    sl = self.rot("kvp", 3)
                    Kp = self.v_mid(2048 + sl * 1408, [1408])
                    Vp = self.v_mid(2048 + 3 * 1408 + sl * 1420, [11, 129])[:, 0:npb_, :]
                    self.dma("sync", Kp[pbase:pbase + rows, :npb_ * 128], k_src(kb0_ * 128, npb_ * 128), [], [f"Kp{sl}"], f"Kp{sl}")
                    self.dma("sync", Vp, v_src(kb0_, npb_), [], [f"Vp{sl}"], f"Vp{sl}")
                    loaded[kb0_] = (sl, Kp, Vp)
                if kb0 not in loaded:
                    load_piece(kb0, npb)
                nxt = [(a_, b_) for (a_, b_, c_, d_, e_) in its[i + 1:] if c_ == 0][:1]
                for (a_, b_) in nxt:
                    if a_ not in loaded:
                        load_piece(a_, b_)
                state["piece"] = loaded[kb0]
            sl, Kp, Vp = state["piece"]
            ss = self.rot("psS", 2)
            for hh in range(nh):
                w = min(512, nq - hh * 512)
                self.mm(self.bank[2 * ss + hh][:, :w], Kp[pbase:pbase + rows, kl * 128:(kl + 1) * 128],
                        qT[pbase:pbase + rows, hh * 512:hh * 512 + w], True, True, [f"Kp{sl}", f"qT{qs}"], [f"psS{ss}"])
            state[i] = (ss, sl, Vp, kl, first, last)

        def emit_exp_pv(i):
            ss, sl, Vp, kl, first, last = state.pop(i)
            pt = self.rot("pT", 3)
            pT = self.v_mid(2048 + 3 * 1408 + 3 * 1420 + pt * 1024, [1024])
            self.act(pT[:, :nq], self.psum[:, 2 * ss * 512:2 * ss * 512 + nq], AF.Exp, [f"psS{ss}"], [f"pT{pt}"], scale=scale)
            for hh in range(nh):
                w = min(512, nq - hh * 512)
                self.mm(self.bank[4 + hh][:, :w], Vp[:, kl, 0:128], pT[:, hh * 512:hh * 512 + w], first, last, [f"pT{pt}", f"Vp{sl}"], [f"ps{4 + hh}"])
                self.mm(self.bank[6 + hh][:, :w], ones[:], pT[:, hh * 512:hh * 512 + w], first, last, [f"pT{pt}", "ones"], [f"ps{6 + hh}"])

        emit_qk(0)
        for i in range(len(its)):
            if i + 1 < len(its):
                emit_qk(i + 1)
            emit_exp_pv(i)
        fin(qb0, nq, nh)

    def phase_attB(self, I):
        P = self.P
        P.barrier()
        self.precast_ffn(0, I["w13t0"], I["w2t0"])
        oTB = self.oTB
        self.ones = self.v_sm(3584, [128])
        ones = self.ones
        P.op("gpsimd", lambda e: e.memset(ones, 1.0), [], ["ones"])
        lv = self.v_sm(128, [4, 64], F32)
        prod = self.v_sm(640, [2, 64], F32)
        sums = self.v_sm(896, [2], F32)
        lam = self.v_sm(904, [1], F32)
        subg = self.v_sm(1024, [1], F32)
        self.dma("sync", lv, I["dlam"].partition_broadcast(128).rearrange("p (a b) -> p a b", b=64), [], ["lv"], "lv")
        self.dma("sync", subg, I["subg"].rearrange("(p o) -> p o", o=1), [], ["subg"], "subg")
        self.tt("vector", prod, lv[:, 0:4:2, :], lv[:, 1:4:2, :], ALU.mult, ["lv"], ["prod"])
        self.P.op("vector", lambda e: e.tensor_reduce(out=sums, in_=prod, axis=AX.X, op=ALU.add), ["prod"], ["sums"])
        self.act(sums, sums, AF.Exp, ["sums"], ["sums"])
        self.tt("vector", lam, sums[:, 1:2], sums[:, 0:1], ALU.subtract, ["sums"], ["lam"])
        self.ts("vector", lam, lam, -LAM0, None, ALU.add, None, ["lam"], ["lam"])
        self.ts("vector", subg, subg, 1.0 - LAM0, None, ALU.mult, None, ["subg"], ["subg"])
        R = self.v_sm(4096, [1024], F32)
        t1 = self.v_sm(6144, [1024], F32)
        sq = self.v_sm(8192, [1024])
        epsb = self.epsb

        def make_fin(h, m):
            def fin(qb0, nq, nh):
                Sb = self.psum[:, 6 * 512:6 * 512 + nq]
                O = self.psum[:, 4 * 512:4 * 512 + nq]
                pO = ["ps4", "ps5"][:nh]
                pS = ["ps6", "ps7"][:nh]
                Rv, tv = R[:, :nq], t1[:, :nq]
                self.act(Rv, Sb, AF.Ln, pS, ["Rb"])
                self.act(Rv, Rv, AF.Exp, ["Rb"], ["Rb"], scale=-1.0)
                if m == 0:
                    self.tt("vector", tv, O, Rv, ALU.mult, pO + ["Rb"], ["t1b"])
                    return
                self.tt("vector", Rv, O, Rv, ALU.mult, pO + ["Rb"], ["Rb"])
                self.P.op("vector", lambda e: e.scalar_tensor_tensor(out=tv, in0=Rv, scalar=lam[:, 0:1], in1=tv, op0=ALU.mult, op1=ALU.add),
                          ["Rb", "t1b", "lam"], ["t1b"])
                self.tt("vector", sq[:, :nq], tv, tv, ALU.mult, ["t1b"], ["sqb"])
                for hh in range(nh):
                    w = min(512, nq - hh * 512)
                    self.mm(self.bank[6 + hh][:, :w], ones[:], sq[:, hh * 512:hh * 512 + w], True, True, ["sqb", "ones"], [f"ps{6 + hh}"])
                self.act(Rv, Sb, AF.Ln, pS + ["epsb"], ["Rb"], scale=1.0 / 128, bias=epsb[:])
                self.act(Rv, Rv, AF.Exp, ["Rb"], ["Rb"], scale=-0.5)
                q0 = qb0 * 128
                self.P.op("vector", lambda e: e.scalar_tensor_tensor(out=oTB[:, h, q0:q0 + nq], in0=tv, scalar=subg[:, 0:1], in1=Rv,
                                                                      op0=ALU.mult, op1=ALU.mult), ["t1b", "Rb", "subg"], ["oTB"])
            return fin

        qgroups = [(0, 8, 0, 66), (8, 8, 0, 66), (16, 2, 64, 2)]
        for h in range(4):
            for qg in qgroups:
                for m in range(2):
                    self.attn_pass_fm(64 * m,
                                      lambda q0, nq, h=h, m=m: I["QT0"][4 + h, 64 * m:64 * m + 64, q0:q0 + nq],
                                      lambda c0, ncl, h=h, m=m: I["KT0"][1 + h, 64 * m:64 * m + 64, c0:c0 + ncl],
                                      lambda kb0, n, h=h: I["VB0"][h, :, kb0:kb0 + n, :],
                                      0.125, qg, make_fin(h, m))

    def resid(self, blk, ybanks, gidx):
        b0 = ybanks
        y = self.psum[:, b0 * 512:b0 * 512 + D]
        ybufs = [f"ps{b0}", f"ps{b0 + 1}"]
        c = self.rot("ssq", 8)
        ssq = self.ssq[:, c:c + 1]
        rstd = self.rstd[:, c:c + 1]
        epsb = self.epsb
        self.act(self.sqj[:], y, AF.Square, ybufs, ["sqj", f"ssq{c}"], accum_out=ssq)
        self.act(rstd, ssq, AF.Sqrt, [f"ssq{c}", "epsb"], [f"rstd{c}"], scale=1.0 / D, bias=epsb[:])
        self.P.op("vector", lambda e: e.reciprocal(rstd, rstd), [f"rstd{c}"], [f"rstd{c}"])
        tmpf = self.v_sm(6144, [D], F32)
        G = self.gates[:, gidx, :]
        self.P.op("vector", lambda e: e.scalar_tensor_tensor(out=tmpf, in0=y, scalar=rstd, in1=G, op0=ALU.mult, op1=ALU.mult),
                  ybufs + [f"rstd{c}", "gates"], ["tmpf"])
        xr = self.xres[:, blk, :]
        self.tt("gpsimd", xr, xr, tmpf, ALU.add, ["tmpf", f"xres{blk}"], [f"xres{blk}"])

    def phase_outproj(self, layer, wsrc):
        P = self.P
        P.barrier()
        osb = self.osb
        wout = self.v_mid(0, [8, D])
        self.load_w(wout, wsrc, 8, "wmid")
        nblk = 18 if layer == 0 else 16
        nt = 4 if layer == 0 else 8
        oTs = {}

        def stage_t(blk):
            pT = self.bank[0].bitcast(BF16).rearrange("p (k t) -> p k t", t=128)
            for k in range(nt):
                self.tr(pT[:, k, :], osb[:, blk, k * 128:(k + 1) * 128], [f"osb{blk}"], ["ps0"])
            s = self.rot("oT", 2)
            oT = self.v_mid(8192 + s * 1024, [8, 128])
            self.cp("vector", oT[:, 0:nt, :], pT[:, 0:nt, :], ["ps0"], [f"oT{s}"])
            oTs[blk] = (s, oT)

        def stage_m(blk):
            s, oT = oTs.pop(blk)
            yb = 1 + 2 * self.rot("ybank", 2)
            for hh in range(2):
                for k in range(8):
                    if k < nt:
                        lhs, rd = oT[:, k, :], f"oT{s}"
                    else:
                        lhs, rd = self.oTB[:, k - 4, blk * 128:(blk + 1) * 128], "oTB"
                    self.mm(self.bank[yb + hh], lhs, wout[:, k, hh * 512:(hh + 1) * 512], k == 0, k == 7, [rd, "wmid"], [f"ps{yb + hh}"])
            self.resid(blk, yb, 0 if blk < 16 else 2)

        stage_t(0)
        for blk in range(nblk):
            if blk + 1 < nblk:
                stage_t(blk + 1)
            stage_m(blk)

    def precast_ffn(self, layer, w13t, w2t):
        self.w13b = getattr(self, "w13b", {})
        self.w2b = getattr(self, "w2b", {})
        w13b = self.dint(f"w13b{layer}", [22, 128, 2048], BF16)
        w2b = self.dint(f"w2b{layer}", [128, 22 * D], BF16)
        self.w13b[layer], self.w2b[layer] = w13b, w2b
        for hc in range(22):
            self.dma("gpsimd", w13b[hc], w13t[hc].rearrange("p k c -> p (k c)"), [], [f"w13b{layer}_{hc}"], f"pc{self.rot('pc', 4)}")
        for hc in range(22):
            self.dma("gpsimd", w2b[:, hc * D:(hc + 1) * D], w2t[:, hc, :], [], [f"w2b{layer}"], f"pc{self.rot('pc', 4)}")

    def phase_ffn(self, layer, w13t, w2t, out_ap=None):
        P = self.P
        P.barrier()
        w2 = self.v_big(0, [22, D])
        w13b, w2b = self.w13b[layer], self.w2b[layer]
        self.dma("sync", w2, w2b.rearrange("p (k c) -> p k c", c=D), [f"w2b{layer}"], ["wbig"], "wbig_hw")
        actT = self.v_mid(0, [22, 768])
        hxT = self.aux[:, 0:6144].rearrange("p (k t) -> p k t", t=768)
        groups = [list(range(0, 6)), list(range(6, 12)), list(range(12, 18 if layer == 0 else 16))]
        for j, blk in enumerate(groups[0]):
            self.norm_T(self.xres[:, blk, :], f"xres{blk}", hxT[:, :, j * 128:(j + 1) * 128], "hxF", layer, 1 if blk >= 16 else 0, 1)
        for gi, blocks in enumerate(groups):
            ntok = 128 * len(blocks)
            nxt = groups[gi + 1] if gi + 1 < len(groups) else []
            halves = [(0, min(384, ntok))] + ([(384, ntok - 384)] if ntok > 384 else [])
            for hc in range(22):
                s = hc % 2
                wt = self.v_sm(s * 2048, [8, 256])
                self.dma("sync", wt, w13b[hc].rearrange("p (k c) -> p k c", c=256), [f"w13b{layer}_{hc}"], [f"w13_{s}"], f"w13_{s}")
                for hi, (t0, tw) in enumerate(halves):
                    pg, pu = self.bank[1 + hi], self.bank[3 + hi]
                    for k in range(8):
                        self.mm(pg[:, :tw], wt[:, k, 0:128], hxT[:, k, t0:t0 + tw], k == 0, k == 7, [f"w13_{s}", "hxF"], [f"ps{1 + hi}"])
                    for k in range(8):
                        self.mm(pu[:, :tw], wt[:, k, 128:256], hxT[:, k, t0:t0 + tw], k == 0, k == 7, [f"w13_{s}", "hxF"], [f"ps{3 + hi}"])
                    sgs = self.rot("sg", 2)
                    sg = self.v_sm(8192 + sgs * 384, [384])
                    self.act(sg[:, :tw], pg[:, :tw], AF.Silu, [f"ps{1 + hi}"], [f"sg{sgs}"])
                    self.tt("vector", actT[:, hc, t0:t0 + tw], sg[:, :tw], pu[:, :tw], ALU.mult, [f"sg{sgs}", f"ps{3 + hi}"], ["actT"])
            for j, blk in enumerate(blocks):
                pre = None
                if j < len(nxt):
                    nb_ = nxt[j]
                    pre = self.norm_a(self.xres[:, nb_, :], f"xres{nb_}")
                yb = 5 if j % 2 == 0 else 1
                for hh in range(2):
                    for hc in range(22):
                        self.mm(self.bank[yb + hh], actT[:, hc, j * 128:(j + 1) * 128], w2[:, hc, hh * 512:(hh + 1) * 512], hc == 0, hc == 21,
                                ["actT", "wbig"], [f"ps{yb + hh}"])
                if pre is not None:
                    self.norm_b(pre[0], pre[1], hxT[:, :, j * 128:(j + 1) * 128], "hxF", layer, 1 if nb_ >= 16 else 0, 1)
                self.resid(blk, yb, 1 if blk < 16 else 3)
                if out_ap is not None and blk < 16:
                    f = self.dma("sync", out_ap[blk * 128:(blk + 1) * 128, :], self.xres[:, blk, :], [f"xres{blk}"], [], f"xout{blk % 2}")
                    self.finals.append(f)

    def phase_mla_pre(self, I):
        P = self.P
        P.barrier()
        w1in = self.v_big(0, [8, 704])
        wqf = self.v_big(5632, [3, 1536])
        wqp = self.v_big(10240, [3, 1536])
        self.load_w(w1in, I["w1in"], 8, "wbig")
        self.load_w(wqf, I["wq1f"], 3, "wbig2")
        ng = self.v_sm(0, [5], F32)
        self.dma("sync", ng, I["mla_ng"], [], ["mla_ng"], "mla_ng")
        krC = self.v_mid(14336, [512], F32)
        krS = self.v_mid(15360, [512], F32)
        epsb = self.epsb
        groups = [list(range(4 * g, 4 * g + 4)) for g in range(4)] + [[16, 17]]
        def geom(gi):
            blocks = groups[gi]
            ntok = 128 * len(blocks)
            q0 = blocks[0] * 128
            s = gi % 2
            return blocks, ntok, q0, s, self.v_mid(s * 1536, [3, 512]), self.v_mid(3072 + s * 1024, [2, 512]), self.v_mid(5120 + s * 512, [512])

        def part1(gi):
            blocks, ntok, q0, s, qnT, kvnT, krT = geom(gi)
            P.op("sync", lambda e, q0=q0, ntok=ntok: [
                e.dma_start(out=krC[0:32, :ntok], in_=I["kr1C"][:, q0:q0 + ntok]),
                e.dma_start(out=krS[0:32, :ntok], in_=I["kr1S"][:, q0:q0 + ntok])], [], ["krtab"], chan="krtab", ndma=2)
            stq = {}

            def sA(j):
                blk = blocks[j]
                stq[("xn", j)] = self.norm_a(self.xres[:, blk, :], f"xres{blk}")

            def sB(j):
                blk = blocks[j]
                hs = self.rot("hx1", 2)
                hxT = self.aux[:, hs * 1024:(hs + 1) * 1024].rearrange("p (k t) -> p k t", t=128)
                xn, xnb = stq.pop(("xn", j))
                self.norm_b(xn, xnb, hxT, f"hx1_{hs}", 1, 1 if blk >= 16 else 0, 0)
                pa, pq = self.bank[5], self.bank[6]
                for k in range(8):
                    self.mm(pa[:, 0:320], hxT[:, k, :], w1in[:, k, 0:320], k == 0, k == 7, [f"hx1_{hs}", "wbig"], ["ps5"])
                for k in range(8):
                    self.mm(pq[:, 0:384], hxT[:, k, :], w1in[:, k, 320:704], k == 0, k == 7, [f"hx1_{hs}", "wbig"], ["ps6"])

            def sC1(j):
                pa, pq = self.bank[5], self.bank[6]
                c = self.rot("ssq", 8)
                c2 = self.rot("ssq", 8)
                for (cc, src, n, pb) in ((c, pa[:, 0:256], 256, "ps5"), (c2, pq[:, 0:384], 384, "ps6")):
                    ssq = self.ssq[:, cc:cc + 1]
                    rstd = self.rstd[:, cc:cc + 1]
                    self.act(self.sqj[:, 0:n], src, AF.Square, [pb], ["sqj", f"ssq{cc}"], accum_out=ssq)
                    self.act(rstd, ssq, AF.Sqrt, [f"ssq{cc}", "epsb"], [f"rstd{cc}"], scale=1.0 / n, bias=epsb[:])
                    self.P.op("vector", lambda e, rstd=rstd: e.reciprocal(rstd, rstd), [f"rstd{cc}"], [f"rstd{cc}"])
                ts_ = self.rot("tk", 2)
                tk = self.v_mid(6144 + ts_ * 704, [704])
                self.act(tk[:, 0:256], pa[:, 0:256], AF.Copy, ["ps5", f"rstd{c}"], [f"tk{ts_}"], scale=self.rstd[:, c:c + 1])
                self.cp("vector", tk[:, 256:320], pa[:, 256:320], ["ps5"], [f"tk{ts_}"])
                self.act(tk[:, 320:704], pq[:, 0:384], AF.Copy, ["ps6", f"rstd{c2}"], [f"tk{ts_}"], scale=self.rstd[:, c2:c2 + 1])
                stq[("tk", j)] = (ts_, tk)

            def sC2(j):
                ts_, tk = stq.pop(("tk", j))
                pT = self.bank[7].bitcast(BF16).rearrange("p (k t) -> p k t", t=128)
                self.tr(pT[:, 0, :], tk[:, 0:128], [f"tk{ts_}"], ["ps7"])
                self.tr(pT[:, 1, :], tk[:, 128:256], [f"tk{ts_}"], ["ps7"])
                self.tr(pT[0:32, 2, :], tk[:, 256:288], [f"tk{ts_}"], ["ps7"])
                self.tr(pT[0:32, 3, :], tk[:, 288:320], [f"tk{ts_}"], ["ps7"])
                for cc in range(3):
                    self.tr(pT[:, 4 + cc, :], tk[:, 320 + cc * 128:320 + (cc + 1) * 128], [f"tk{ts_}"], ["ps7"])
                tsl = slice(j * 128, (j + 1) * 128)
                self.tt("vector", kvnT[:, :, tsl], pT[:, 0:2, :], ng[:, 0:2].unsqueeze(2).to_broadcast([128, 2, 128]), ALU.mult,
                        ["ps7", "mla_ng"], [f"kvnT{s}"])
                self.tt("vector", qnT[:, :, tsl], pT[:, 4:7, :], ng[:, 2:5].unsqueeze(2).to_broadcast([128, 3, 128]), ALU.mult,
                        ["ps7", "mla_ng"], [f"qnT{s}"])
                t1 = self.v_big(20512, [512], F32)
                t2 = self.v_big(21536, [512], F32)
                self.tt("vector", t1[0:32, 0:128], pT[0:32, 2, :], krC[0:32, tsl], ALU.mult, ["ps7", "krtab"], ["t1"])
                self.tt("vector", t2[0:32, 0:128], pT[0:32, 3, :], krS[0:32, tsl], ALU.mult, ["ps7", "krtab"], ["t2"])
                self.tt("gpsimd", krT[0:32, tsl], t1[0:32, 0:128], t2[0:32, 0:128], ALU.add, ["t1", "t2"], [f"krT{s}"])

            nbk = len(blocks)
            sA(0)
            sB(0)
            for j in range(nbk):
                if j + 1 < nbk:
                    sA(j + 1)
                sC1(j)
                if j + 1 < nbk:
                    sB(j + 1)
                sC2(j)
            if blocks[0] < 16:
                gdst, gname = I["gown"][gi], f"gown{gi}"
            else:
                gdst, gname = I["gctx"], "gctx"
            self.dma("sync", gdst[0:256, 0:ntok].rearrange("(c p) t -> p c t", p=128), kvnT[:, :, :ntok], [f"kvnT{s}"], [gname], "g_st")
            self.dma("sync", gdst[256:288, 0:ntok], krT[0:32, :ntok], [f"krT{s}"], [gname], "g_st2")
            if blocks[0] >= 16:
                return
            if self.mode == "FUSED":
                rg = [[0, 1, 2, 3], [4, 5, 6, 7]]
                self.P.op("gpsimd", lambda e, gi=gi: e.collective_compute("AllGather", ALU.bypass, replica_groups=rg,
                                                                          ins=[I["gown"][gi]], outs=[I["gall"][gi]]),
                          [gname], [f"gall{gi}"], chan="cc", cinc=1)
            tabC = self.v_mid(8192 + s * 2048, [512], F32)
            tabS = self.v_mid(8192 + s * 2048 + 1024, [512], F32)
            P.op("sync", lambda e, tabC=tabC, tabS=tabS, q0=q0, ntok=ntok: [
                e.dma_start(out=tabC[0:96, :ntok], in_=I["rope1C"][:, q0:q0 + ntok]),
                e.dma_start(out=tabS[0:96, :ntok], in_=I["rope1S"][:, q0:q0 + ntok])], [], [f"tab{s}"], chan=f"tab{s}", ndma=2)

        def part2(gi):
            blocks, ntok, q0, s, qnT, kvnT, krT = geom(gi)
            if blocks[0] >= 16:
                return
            tabC = self.v_mid(8192 + s * 2048, [512], F32)
            tabS = self.v_mid(8192 + s * 2048 + 1024, [512], F32)
            for h in range(16):
                kst, kbn = self.fm_rope_chunk(h, ntok, qnT, f"qnT{s}", wqf, wqp, h * 96, 96, tabC, tabS, f"tab{s}", wname="wbig2", perm=self.perm96)
                self.dma("sync", I["QT1"][h, :, q0:q0 + ntok], kst[0:96, :ntok], [kbn], [], f"kt_st{self.rot('ktst', 4)}")

        part1(0)
        for gi in range(len(groups)):
            if gi + 1 < len(groups):
                part1(gi + 1)
            part2(gi)

    def phase_kv1(self, I):
        P = self.P
        P.barrier()
        wk1 = self.v_big(0, [2, D])
        wv1 = self.v_big(2048, [2, D])
        self.load_w(wk1, I["wk1"], 2, "wbig")
        self.load_w(wv1, I["wv1"], 2, "wbig")
        vst = [self.v_big(4096 + s2 * 4160, [16, 4, 65]) for s2 in range(2)]
        for s2 in range(2):
            P.op("gpsimd", lambda e, s2=s2: e.memset(vst[s2][:, :, :, 64:65], 1.0), [], [f"vst{s2}"])
        ngroups = 17

        def geom(gi):
            nb = 4 if gi < 16 else 2
            s = gi % 2
            return nb, 128 * nb, gi * 512, s, self.v_mid(s * 1024, [2, 512]), self.v_mid(2048 + s * 512, [512])

        def load(gi):
            nb, ntok, tok0, s, kvnT, krT = geom(gi)
            if gi < 16:
                r, jj = gi // 4, gi % 4
                src, sname = I["gall"][jj][r * 288:(r + 1) * 288, :], f"gall{jj}"
            else:
                src, sname = I["gctx"], "gctx"
            c0 = 0
            P.op("sync", lambda e, kvnT=kvnT, krT=krT, src=src, c0=c0, ntok=ntok: [
                e.dma_start(out=kvnT[:, :, :ntok], in_=src[0:256, c0:c0 + ntok].rearrange("(c p) t -> p c t", p=128)),
                e.dma_start(out=krT[0:32, :ntok], in_=src[256:288, c0:c0 + ntok])], [sname], [f"kvn{s}"], chan=f"kvn{s}", ndma=2)

        def compute(gi):
            nb, ntok, tok0, s, kvnT, krT = geom(gi)
            self.dma("sync", I["KR1"][:, tok0:tok0 + ntok], krT[0:32, :ntok], [f"kvn{s}"], [], "kr_st")
            vs = gi % 2

            def kchunk(cch):
                b = self.rot("k1bank", 2)
                pk_ = self.bank[1 + b]
                for k in range(2):
                    self.mm(pk_[:, :ntok], wk1[:, k, cch * 128:(cch + 1) * 128], kvnT[:, k, :ntok], k == 0, k == 1, [f"kvn{s}", "wbig"], [f"ps{1 + b}"])
                ks = self.rot("kst", 4)
                kst = self.v_mid(12288 + ks * 512, [512])
                self.cp("vector", kst[:, :ntok], pk_[:, :ntok], [f"ps{1 + b}"], [f"kst{ks}"])
                self.dma("sync", I["KT1"][cch, :, tok0:tok0 + ntok], kst[:, :ntok], [f"kst{ks}"], [], f"kt_st{self.rot('ktst', 4)}")

            def vpart(j, hh):
                pv = self.bank[5 + hh]
                for k in range(2):
                    self.mm(pv, kvnT[:, k, j * 128:(j + 1) * 128], wv1[:, k, hh * 512:(hh + 1) * 512], k == 0, k == 1, [f"kvn{s}", "wbig"], [f"ps{5 + hh}"])
                self.act(vst[vs][:, hh * 8:(hh + 1) * 8, j, 0:64], pv.rearrange("p (h d) -> p h d", d=64), AF.Copy, [f"ps{5 + hh}"], [f"vst{vs}"])

            vlist = [(j, hh) for j in range(nb) for hh in range(2)]
            for i in range(8):
                kchunk(i)
                if i < len(vlist):
                    vpart(*vlist[i])
            b0 = gi * 4
            self.dma("sync", I["V1"][:, :, b0:b0 + nb, :].rearrange("h p b d -> p h b d"), vst[vs][:, :, 0:nb, :], [f"vst{vs}"], [], f"vb_st{vs}")

        load(0)
        for gi in range(ngroups):
            if gi + 1 < ngroups:
                load(gi + 1)
            compute(gi)

    def phase_attC(self, I):
        P = self.P
        P.barrier()
        self.precast_ffn(1, I["w13t1"], I["w2t1"])
        osb = self.v_big(0, [18, D])
        self.osb = osb

        def make_fin(h):
            def fin(qb0, nqb, per, ob):
                for bk in range((nqb + per - 1) // per):
                    n = min(per, nqb - bk * per)
                    pv = self.bank[ob + bk][:, 0:n * 65].rearrange("p (q d) -> p q d", d=65)
                    pb = f"ps{ob + bk}"
                    r = self.rot("rsC", 2)
                    rs = self.v_sm(1280 + r * 16, [7], F32)[:, 0:n]
                    self.P.op("vector", lambda e, rs=rs, pv=pv: e.reciprocal(rs, pv[:, :, 64]), [pb], [f"rsC{r}"])
                    q0 = qb0 + bk * per
                    self.tt("vector", osb[:, q0:q0 + n, h * 64:(h + 1) * 64], pv[:, :, 0:64], rs.unsqueeze(2).to_broadcast([128, n, 64]),
                            ALU.mult, [pb, f"rsC{r}"], [f"osb{q0 + i}" for i in range(n)])
            return fin

        for h in range(16):
            for qg in [(0, 8, 0, 66), (8, 8, 0, 66)]:
                self.attn_pass(96, 0,
                               lambda q0, nq, h=h: I["QT1"][h, :, q0:q0 + nq],
                               [(0, 64, lambda c0, ncl, h=h: I["KT1"][h // 2, (h % 2) * 64:(h % 2) * 64 + 64, c0:c0 + ncl]),
                                (64, 32, lambda c0, ncl: I["KR1"][:, c0:c0 + ncl])],
                               lambda kb0, n, h=h: I["V1"][h, :, kb0:kb0 + n, :],
                               65, 96 ** -0.5, [qg], make_fin(h), "C")


import numpy as np

D = 1024; NT = 8192; OWN = 2048; CTX = 256
GRID_W = 64; THETA = 10000.0


def rope_tabs_fm(rot_dim, positions):
    axis_dim = rot_dim // 2
    q = rot_dim // 4
    inv = THETA ** (-np.arange(0, axis_dim, 2, dtype=np.float32) / axis_dim)
    row = (positions // GRID_W).astype(np.float32)
    col = (positions % GRID_W).astype(np.float32)
    ar = row[None, :] * inv[:, None]
    ac = col[None, :] * inv[:, None]
    ar = ar.astype(np.float32); ac = ac.astype(np.float32)
    C = np.concatenate([np.cos(ar), np.cos(ar), np.cos(ac), np.cos(ac)], 0)
    S = np.concatenate([-np.sin(ar), np.sin(ar), -np.sin(ac), np.sin(ac)], 0)
    return C.astype(np.float32), S.astype(np.float32)


def rope_perm(rot_dim):
    q = rot_dim // 4
    return np.concatenate([np.arange(q, 2 * q), np.arange(0, q), np.arange(3 * q, 4 * q), np.arange(2 * q, 3 * q)])


def pk(w):
    K = w.shape[0] // 128
    return np.ascontiguousarray(w.reshape(K, 128, w.shape[1]).transpose(1, 0, 2))


def prep_l0(inp, core):
    b, qc = core // 4, core % 4
    roll = qc * OWN
    f32 = np.float32
    m = {}
    m["xk"] = np.ascontiguousarray(np.roll(inp["x"][b], -roll, axis=0))
    m["ctxin"] = np.ascontiguousarray(inp["ctx"][b])
    pos = (np.arange(NT) + roll) % NT
    C, S = rope_tabs_fm(64, pos)
    C = np.concatenate([C, np.ones((64, CTX), f32)], 1)
    S = np.concatenate([S, np.zeros((64, CTX), f32)], 1)
    m["ropeC0"] = np.ascontiguousarray(np.concatenate([C, C], 0))
    m["ropeS0"] = np.ascontiguousarray(np.concatenate([S, S], 0))
    w = inp["ab_w_in"][0]
    p64 = rope_perm(64)
    def permcols(cols):
        return np.concatenate([c0 + p64 for c0 in cols])
    qa_cols = []
    for c in range(4):
        qa_cols += [64 * c, 64 * (4 + c)]
    qb_cols = [512 + 64 * i for i in range(8)]
    qcols = qa_cols + qb_cols
    kcols = [1024, 1088] + [1280 + 64 * i for i in range(8)]
    nat = lambda cols: np.concatenate([np.arange(c0, c0 + 64) for c0 in cols])
    m["w0q_f"] = pk(w[:, nat(qcols)])
    m["w0k_f"] = pk(w[:, nat(kcols)])
    pm64 = np.zeros((128, 128), f32)
    for hh in range(2):
        for d_ in range(64):
            pm64[hh * 64 + p64[d_], hh * 64 + d_] = 1.0
    m["perm64"] = pm64
    p32_ = rope_perm(32)
    pm96 = np.zeros((128, 128), f32)
    for d_ in range(64):
        pm96[d_, d_] = 1.0
    for d_ in range(32):
        pm96[64 + p32_[d_], 64 + d_] = 1.0
    m["perm96"] = pm96
    m["w0v"] = pk(np.concatenate([w[:, 1152:1280], w[:, 1792:2304]], 1))
    cf = np.stack([inp["c"][b].reshape(8, 128).T, inp["c_ctx"].reshape(8, 128).T], -1)
    m["cfm"] = np.ascontiguousarray(cf.astype(f32))
    for l in range(2):
        m[f"adaw{l}"] = np.ascontiguousarray(inp["ada_w"][l].reshape(8, 128, 6, D).transpose(2, 1, 0, 3))
        m[f"adab_fm{l}"] = np.ascontiguousarray(inp["ada_b"][l].reshape(48, 128).T)
        m[f"adab_row{l}"] = np.ascontiguousarray(inp["ada_b"][l])
        m[f"ng_fm{l}"] = np.ascontiguousarray(inp["norm_g"][l].reshape(4, 8, 128).transpose(2, 0, 1))
        m[f"ng_row{l}"] = np.ascontiguousarray(inp["norm_g"][l])
    return m


NEG = -30000.0


def prep_l0_more(inp, core, m):
    b, qc = core // 4, core % 4
    f32 = np.float32
    j = np.arange(128)[:, None]
    qi = np.arange(128)[None, :]
    prev = np.where(j >= qi, 0.0, NEG).astype(f32)
    nxt = np.where(j <= qi, 0.0, NEG).astype(f32)
    allneg = np.full((128, 128), NEG, f32)
    masks = [allneg if qc == 0 else prev, prev, nxt, allneg if qc == 3 else nxt]
    m["amask"] = np.ascontiguousarray(np.stack([np.tile(x, (1, 4)) for x in masks], 1))
    m["sink"] = np.ascontiguousarray(inp["ab_sink"][0])
    m["dlam"] = np.ascontiguousarray(inp["diff_lambda"][0].reshape(256))
    m["subg"] = np.ascontiguousarray(inp["diff_subln_g"][0])
    m["wout0"] = pk(inp["ab_w_out"][0])
    for l in (0,):
        ffn_pack(inp, l, m)
    return m


def ffn_pack(inp, l, m):
    w13 = inp["ffn_w13"][l]
    g = w13[:, :2816].reshape(8, 128, 22, 128)
    u = w13[:, 2816:].reshape(8, 128, 22, 128)
    t = np.concatenate([g, u], -1)
    m[f"w13t{l}"] = np.ascontiguousarray(t.transpose(2, 1, 0, 3))
    m[f"w2t{l}"] = pk(inp["ffn_w2"][l])


def prep_l1(inp, core, m):
    b, qc = core // 4, core % 4
    f32 = np.float32
    roll = qc * OWN
    pos = (np.arange(OWN) + roll) % NT
    C32, S32 = rope_tabs_fm(32, pos)
    p32 = rope_perm(32)
    w = inp["mla_w_in"][0]
    m["w1in"] = pk(np.concatenate([w[:, 384:640], w[:, 640:672], w[:, 640 + p32], w[:, 0:384]], 1))
    wq = inp["mla_wq_b"][0]
    permc = np.concatenate([np.concatenate([h * 96 + np.arange(64), h * 96 + 64 + p32]) for h in range(16)])
    m["wq1f"] = pk(wq)
    m["rope1C"] = np.ascontiguousarray(np.concatenate([np.ones((64, OWN), f32), C32], 0))
    m["rope1S"] = np.ascontiguousarray(np.concatenate([np.zeros((64, OWN), f32), S32], 0))
    m["kr1C"] = np.ascontiguousarray(np.concatenate([C32, np.ones((32, CTX), f32)], 1))
    m["kr1S"] = np.ascontiguousarray(np.concatenate([S32, np.zeros((32, CTX), f32)], 1))
    m["mla_ng"] = np.ascontiguousarray(np.concatenate([inp["mla_kv_norm_g"][0].reshape(2, 128).T,
                                                       inp["mla_q_norm_g"][0].reshape(3, 128).T], 1).astype(f32))
    wkv = inp["mla_wkv_b"][0].reshape(256, 16, 128)
    m["wk1"] = pk(np.ascontiguousarray(wkv[:, :, :64]).reshape(256, 1024))
    m["wv1"] = pk(np.ascontiguousarray(wkv[:, :, 64:]).reshape(256, 1024))
    m["wout1"] = pk(inp["mla_w_out"][0])
    ffn_pack(inp, 1, m)
    return m


import ml_dtypes

L0_KEYS = ["xk", "ctxin", "ropeC0", "ropeS0", "w0q_f", "w0k_f", "w0v", "cfm", "perm64", "perm96",
           "adaw0", "adab_fm0", "adab_row0", "ng_fm0", "ng_row0", "adaw1", "adab_fm1", "adab_row1", "ng_fm1", "ng_row1",
           "amask", "sink", "dlam", "subg", "wout0", "w13t0", "w2t0",
           "w1in", "wq1f", "rope1C", "rope1S", "kr1C", "kr1S", "mla_ng"]
L1_KEYS = ["cfm", "adaw1", "adab_fm1", "adab_row1", "ng_fm1", "ng_row1", "wk1", "wv1", "wout1", "w13t1", "w2t1"]


def layer0_body(kb, I):
    kb.phase_mod(0, I["cfm"], I["adaw0"], I["adab_fm0"], I["adab_row0"], I["ng_fm0"], I["ng_row0"])
    kb.phase_kv0(I)
    kb.phase_q0(I)
    kb.phase_attA(I)
    kb.phase_attB(I)
    kb.phase_outproj(0, I["wout0"])
    kb.phase_ffn(0, I["w13t0"], I["w2t0"])
    kb.phase_mod(1, I["cfm"], I["adaw1"], I["adab_fm1"], I["adab_row1"], I["ng_fm1"], I["ng_row1"])
    kb.phase_mla_pre(I)


def layer1_body(kb, I, out):
    kb.phase_kv1(I)
    kb.phase_attC(I)
    kb.phase_outproj(1, I["wout1"])
    kb.phase_ffn(1, I["w13t1"], I["w2t1"], out_ap=out)


def scratch_l0(kb, I):
    I["KT0"] = kb.dint("KT0", [5, 128, NK], BF16)
    I["VB0"] = kb.dint("VB0", [4, 128, NKB, 129], BF16)
    I["QT0"] = kb.dint("QT0", [8, 128, NQ], BF16)


def scratch_l1(kb, I):
    I["KT1"] = kb.dint("KT1", [8, 128, NK], BF16)
    I["KR1"] = kb.dint("KR1", [32, NK], BF16)
    I["V1"] = kb.dint("V1", [16, 128, NKB, 65], BF16)


def build_fused(m):
    kb = KB("FUSED")
    keys = L0_KEYS + [k for k in L1_KEYS if k not in L0_KEYS]
    I = {k: kb.din(k, m[k].shape) for k in keys}
    scratch_l0(kb, I)
    scratch_l1(kb, I)
    I["gown"] = [kb.dint(f"gown{j}", [288, 512], BF16) for j in range(4)]
    I["gall"] = [kb.dint(f"gall{j}", [4 * 288, 512], BF16) for j in range(4)]
    I["gctx"] = kb.dint("gctx", [288, CTX], BF16)
    I["QT1"] = kb.dint("QT1", [16, 96, OWN], BF16)
    out = kb.dout("out", [OWN, D])
    kb.setup_common(); kb.setup_l0()
    kb.load_perms(I)
    layer0_body(kb, I)
    layer1_body(kb, I, out)
    kb.P.emit(kb.finals)
    return kb, keys


def prep_all(inputs):
    inp = {k: np.asarray(v) for k, v in inputs.items()}
    maps = []
    for core in range(8):
        m = prep_l0(inp, core)
        prep_l0_more(inp, core, m)
        prep_l1(inp, core, m)
        maps.append(m)
    return maps


def kernel(**inputs):
    maps = prep_all(inputs)
    kb, keys = build_fused(maps[0])
    res = run_bass_kernel_spmd(kb.nc, [{k: m[k] for k in keys} for m in maps], core_ids=list(range(8)))
    out = np.zeros((2, NT, D), np.float32)
    for core in range(8):
        b, qc = core // 4, core % 4
        out[b, qc * OWN:(qc + 1) * OWN] = np.asarray(res.results[core]["out"])
    return out
```

```python
import numpy as np
import concourse.bass as bass
import concourse.mybir as mybir
from concourse.alu_op_type import AluOpType as ALU
from concourse.bass_utils import run_bass_kernel_spmd

AF = mybir.ActivationFunctionType
AX = mybir.AxisListType
F32 = mybir.dt.float32
BF16 = mybir.dt.bfloat16


class Buf:
    __slots__ = ("name", "last_w", "readers")

    def __init__(self, name):
        self.name = name
        self.last_w = None
        self.readers = []


class Op:
    __slots__ = ("eng", "fn", "deps", "needed", "sem", "val", "chan", "ndma", "idx", "cinc")

    def __init__(self, eng, fn, chan=None):
        self.eng = eng
        self.fn = fn
        self.deps = []
        self.needed = False
        self.sem = None
        self.val = None
        self.chan = chan
        self.ndma = 0
        self.idx = -1
        self.cinc = 16


ENGS = ("tensor", "scalar", "vector", "gpsimd", "sync")


class Prog:
    def __init__(self, nc):
        self.nc = nc
        self.ops = {e: [] for e in ENGS}
        self.nops = 0
        self.chan_last = {}
        self.bufs = {}
        self.bar = []
        self.bar_pending = set()

    def barrier(self):
        b = [self.ops[e][-1] for e in ENGS if self.ops[e]]
        b += list(self.chan_last.values())
        self.bar = b
        self.bar_pending = set(ENGS)

    def buf(self, name):
        b = self.bufs.get(name)
        if b is None:
            b = Buf(name)
            self.bufs[name] = b
        return b

    def _B(self, x):
        return x if isinstance(x, Buf) else self.buf(x)

    def op(self, eng, fn, reads=(), writes=(), chan=None, ndma=1, cinc=16):
        o = Op(eng, fn, chan)
        o.idx = self.nops
        self.nops += 1
        if eng in self.bar_pending:
            self.bar_pending.discard(eng)
            for p in self.bar:
                self._dep(o, p)
        if chan is not None:
            o.ndma = ndma
            o.cinc = cinc
            prev = self.chan_last.get(chan)
            if prev is not None:
                self._dep(o, prev)
            self.chan_last[chan] = o
        for r in reads:
            b = self._B(r)
            if b.last_w is not None:
                self._dep(o, b.last_w)
            b.readers.append(o)
        for w in writes:
            b = self._B(w)
            if b.last_w is not None:
                self._dep(o, b.last_w)
            for rd in b.readers:
                if rd is not o:
                    self._dep(o, rd)
            b.last_w = o
            b.readers = []
        self.ops[eng].append(o)
        return o

    def _dep(self, o, p):
        if p is o:
            return
        if p.eng == "tensor" and o.eng == "tensor" and p.chan is None and o.chan is None:
            return
        for d in o.deps:
            if d is p:
                return
        o.deps.append(p)
        p.needed = True

    def emit(self, final_ops=()):
        nc = self.nc
        for o in final_ops:
            o.needed = True
        import contextlib
        with contextlib.ExitStack() as st:
            esem = {e: st.enter_context(nc.semaphore("s_" + e)) for e in ENGS}
            chans = sorted({o.chan for e in ENGS for o in self.ops[e] if o.chan is not None})
            csem = {c: st.enter_context(nc.semaphore("c_" + c)) for c in chans}
            self.nsem = len(esem) + len(csem)
            ccount = {c: 0 for c in chans}
            allops = sorted((o for e in ENGS for o in self.ops[e]), key=lambda o: o.idx)
            ecount = {e: 0 for e in ENGS}
            for o in allops:
                if o.chan is not None:
                    ccount[o.chan] += o.cinc * o.ndma
                    o.sem, o.val = csem[o.chan], ccount[o.chan]
                elif o.needed:
                    ecount[o.eng] += 1
                    o.sem, o.val = esem[o.eng], ecount[o.eng]
            self.ecount = ecount
            block = st.enter_context(nc.Block())

            def run(ename, e):
                seen = {}
                for o in self.ops[ename]:
                    for d in o.deps:
                        k = id(d.sem)
                        if seen.get(k, 0) >= d.val:
                            continue
                        e.wait_ge(d.sem, d.val)
                        seen[k] = d.val
                    r = o.fn(e)
                    if o.chan is not None:
                        rs = r if isinstance(r, (list, tuple)) else [r]
                        assert len(rs) == o.ndma, (len(rs), o.ndma)
                        for ins in rs:
                            if o.cinc == 1:
                                ins.then_inc(o.sem)
                            else:
                                ins.then_inc(o.sem, o.cinc)
                    elif o.needed:
                        r.then_inc(o.sem, 1)
                if ename == "sync":
                    for o in final_ops:
                        e.wait_ge(o.sem, o.val)

            @block.tensor
            def _(e):
                run("tensor", e)

            @block.scalar
            def _(e):
                run("scalar", e)

            @block.vector
            def _(e):
                run("vector", e)

            @block.gpsimd
            def _(e):
                run("gpsimd", e)

            @block.sync
            def _(e):
                run("sync", e)


import contextlib
import math
import numpy as np

D = 1024
NT = 8192
OWN = 2048
CTX = 256
NQ = OWN + CTX
NK = NT + CTX
NKB = NK // 128
FFH = 2816
EPS = 1e-6
LAM0 = 0.8 - 0.6 * math.exp(-0.3 * 0)


class KB:
    def __init__(self, mode, dbg=()):
        self.mode = mode
        self.dbg = set(dbg)
        self.nc = bass.Bass("TRN2", target_bir_lowering=False)
        self.P = Prog(self.nc)
        self.st = contextlib.ExitStack()
        self.finals = []
        self.cnt = {}

    def din(self, name, shape, dt=F32):
        return self.nc.dram_tensor(name, list(shape), dt, kind="ExternalInput").ap()

    def dout(self, name, shape, dt=F32):
        return self.nc.dram_tensor(name, list(shape), dt, kind="ExternalOutput").ap()

    def dint(self, name, shape, dt=BF16):
        return self.nc.dram_tensor(name, list(shape), dt, kind="Internal").ap()

    def sb(self, name, shape, dt):
        return self.st.enter_context(self.nc.sbuf_tensor(name, list(shape), dt))

    def rot(self, key, n):
        v = self.cnt.get(key, 0)
        self.cnt[key] = v + 1
        return v % n

    def dma(self, q, out, in_, reads, writes, chan, **kw):
        return self.P.op(q, lambda e: e.dma_start(out=out, in_=in_, **kw), reads, writes, chan=chan)

    def mm(self, out, lhsT, rhs, start, stop, reads, writes):
        return self.P.op("tensor", lambda e: e.matmul(out, lhsT=lhsT, rhs=rhs, start=start, stop=stop,
                                                      skip_group_check=True), reads, writes)

    def tr(self, out, in_, reads, writes):
        ident = self.ident
        return self.P.op("tensor", lambda e: e.transpose(out, in_, ident[:]), list(reads) + ["ident"], writes)

    def act(self, out, in_, func, reads, writes, **kw):
        return self.P.op("scalar", lambda e: e.activation(out=out, in_=in_, func=func, **kw), reads, writes)

    def tt(self, eng, out, in0, in1, op, reads, writes):
        return self.P.op(eng, lambda e: e.tensor_tensor(out=out, in0=in0, in1=in1, op=op), reads, writes)

    def ts(self, eng, out, in0, s1, s2, op0, op1, reads, writes, **kw):
        if op1 is None:
            return self.P.op(eng, lambda e: e.tensor_scalar(out=out, in0=in0, scalar1=s1, scalar2=None, op0=op0, **kw), reads, writes)
        return self.P.op(eng, lambda e: e.tensor_scalar(out=out, in0=in0, scalar1=s1, scalar2=s2, op0=op0, op1=op1, **kw), reads, writes)

    def cp(self, eng, out, in_, reads, writes):
        return self.P.op(eng, lambda e: e.tensor_copy(out=out, in_=in_), reads, writes)

    def debug_out(self, name, src_ap, shape, reads, dt=F32):
        if name not in self.dbg:
            return
        o = self.dout("dbg_" + name, shape, dt)
        f = self.dma("sync", o, src_ap, reads, [], "dbg_" + name)
        self.finals.append(f)

    def setup_common(self):
        nc, P = self.nc, self.P
        self.psum = self.st.enter_context(nc.psum_tensor("psum", [128, 4096], F32))
        self.bank = [self.psum[:, i * 512:(i + 1) * 512] for i in range(8)]
        self.ident = self.sb("ident", [128, 128], BF16)
        ident = self.ident
        P.op("gpsimd", lambda e: e.memset(ident[:], 0.0), [], ["ident"])
        P.op("gpsimd", lambda e: e.affine_select(out=ident[:], in_=ident[:], pattern=[[-1, 128]],
                                                 compare_op=ALU.not_equal, fill=1.0, base=0,
                                                 channel_multiplier=1), ["ident"], ["ident"])
        self.xres = self.sb("xres", [128, 18, D], F32)
        self.big = self.sb("big", [128, 23040], BF16)
        self.mid = self.sb("mid", [128, 16896], BF16)
        self.sm = self.sb("sm", [128, 9216], BF16)
        self.aux = self.sb("aux", [128, 6144], BF16)
        self.gates = self.sb("gates", [128, 4, D], F32)
        self.modfm = self.sb("modfm", [128, 2, 8, 8], F32)
        self.ssq = self.sb("ssq", [128, 8], F32)
        self.rstd = self.sb("rstd", [128, 8], F32)
        self.sqj = self.sb("sqj", [128, D], BF16)
        self.perm64 = self.sb("perm64_sb", [128, 128], BF16)
        self.perm96 = self.sb("perm96_sb", [128, 128], BF16)
        self.epsb = self.sb("epsb", [128, 1], F32)
        epsb = self.epsb
        P.op("gpsimd", lambda e: e.memset(epsb[:], EPS), [], ["epsb"])

    def v_big(self, off, shape, dt=BF16):
        n = int(np.prod(shape))
        if dt == F32:
            ap = self.big[:, off:off + 2 * n].bitcast(F32)
        else:
            ap = self.big[:, off:off + n]
        return self._shape(ap, shape)

    def v_mid(self, off, shape, dt=BF16):
        n = int(np.prod(shape))
        if dt == F32:
            ap = self.mid[:, off:off + 2 * n].bitcast(F32)
        else:
            ap = self.mid[:, off:off + n]
        return self._shape(ap, shape)

    def v_sm(self, off, shape, dt=BF16):
        n = int(np.prod(shape))
        if dt == F32:
            ap = self.sm[:, off:off + 2 * n].bitcast(F32)
        else:
            ap = self.sm[:, off:off + n]
        return self._shape(ap, shape)

    @staticmethod
    def _shape(ap, shape):
        if len(shape) == 1:
            return ap
        if len(shape) == 2:
            return ap.rearrange("p (a b) -> p a b", b=shape[1])
        if len(shape) == 3:
            return ap.rearrange("p (a b c) -> p a b c", b=shape[1], c=shape[2])
        raise ValueError

    def phase_mod(self, layer, cfm, adaw, adab_fm, adab_row, ng_fm, ng_row):
        P = self.P
        P.barrier()
        bk = self.bank[7]
        sc = self.v_sm(0, [8, 2], F32)
        scb = self.v_sm(64, [8, 2], BF16)
        scbc = self.v_sm(128, [2, 8, 128], BF16)
        bfm = self.v_sm(2304, [48], F32)
        gfm = self.v_sm(2304 + 96, [4, 8], F32)
        modT = self.v_sm(2304 + 96 + 64, [48, 2], F32)
        brow = self.v_big(0, [2, D], F32)
        grow = self.v_big(4096, [2, D], F32)
        self.dma("sync", sc, cfm, [], ["m_sc"], "m_sc")
        self.dma("sync", bfm, adab_fm, [], ["m_bfm"], "m_bfm")
        self.dma("sync", gfm, ng_fm, [], ["m_gfm"], "m_gfm")
        self.P.op("sync", lambda e: [e.dma_start(out=brow[:, 0, :], in_=adab_row[2 * D:3 * D].partition_broadcast(128)),
                                     e.dma_start(out=brow[:, 1, :], in_=adab_row[5 * D:6 * D].partition_broadcast(128)),
                                     e.dma_start(out=grow[:, 0, :], in_=ng_row[1, :].partition_broadcast(128)),
                                     e.dma_start(out=grow[:, 1, :], in_=ng_row[3, :].partition_broadcast(128))],
                  [], ["m_rows"], chan="m_rows", ndma=4)
        self.act(sc, sc, AF.Silu, ["m_sc"], ["m_sc"])
        self.cp("vector", scb, sc, ["m_sc"], ["m_scb"])
        for j in range(2):
            self.cp("vector", scbc[:, j], sc[:, :, j:j + 1].to_broadcast([128, 8, 128]), ["m_sc"], ["m_scbc"])
        psT = bk[:, 0:96].rearrange("p (a b) -> p a b", b=2)
        for j in range(6):
            s = j % 2
            wt = self.v_mid(s * 8192, [8, D])
            self.dma("gpsimd", wt, adaw[j], [], [f"m_w{s}"], f"m_w{s}")
            for cc in range(8):
                for k in range(8):
                    self.mm(psT[:, j * 8 + cc, :], wt[:, k, cc * 128:(cc + 1) * 128], scb[:, k, :], k == 0,
                            k == 7, [f"m_w{s}", "m_scb"], ["ps7"])
            if j in (2, 5):
                gi = 0 if j == 2 else 1
                for t in range(2):
                    for hh in range(2):
                        pb = self.bank[5 + hh]
                        for k in range(8):
                            self.mm(pb, scbc[:, t, k, :], wt[:, k, hh * 512:(hh + 1) * 512], k == 0, k == 7,
                                    [f"m_w{s}", "m_scbc"], [f"ps{5 + hh}"])
                        g = self.gates[:, 2 * t + gi, hh * 512:(hh + 1) * 512]
                        self.tt("vector", g, pb, brow[:, gi, hh * 512:(hh + 1) * 512], ALU.add, [f"ps{5 + hh}", "m_rows"], ["gates"])
                        self.tt("vector", g, g, grow[:, gi, hh * 512:(hh + 1) * 512], ALU.mult, ["gates", "m_rows"], ["gates"])
        self.tt("vector", modT, psT, bfm.unsqueeze(2).to_broadcast([128, 48, 2]), ALU.add, ["ps7", "m_bfm"], ["m_modT"])
        mf = self.modfm
        for t in range(2):
            self.P.op("vector", lambda e, t=t: e.scalar_tensor_tensor(out=mf[:, layer, 4 * t + 0, :], in0=modT[:, 8:16, t], scalar=1.0,
                                                                      in1=gfm[:, 0, :], op0=ALU.add, op1=ALU.mult),
                      ["m_modT", "m_gfm"], ["modfm"])
            self.cp("vector", mf[:, layer, 4 * t + 1, :], modT[:, 0:8, t], ["m_modT"], ["modfm"])
            self.P.op("vector", lambda e, t=t: e.scalar_tensor_tensor(out=mf[:, layer, 4 * t + 2, :], in0=modT[:, 32:40, t], scalar=1.0,
                                                                      in1=gfm[:, 2, :], op0=ALU.add, op1=ALU.mult),
                      ["m_modT", "m_gfm"], ["modfm"])
            self.cp("vector", mf[:, layer, 4 * t + 3, :], modT[:, 24:32, t], ["m_modT"], ["modfm"])
        self.debug_out(f"modfm{layer}", mf[:, layer], [128, 8, 8], ["modfm"])
        self.debug_out(f"gates{layer}", self.gates[:], [128, 4, D], ["gates"])

    def norm_a(self, xsrc, xbuf, nslot=2):
        c = self.rot("ssq", 8)
        xs = self.rot(f"xn{nslot}", nslot)
        xn = self.v_sm(4096 + xs * 1024, [D])
        ssq = self.ssq[:, c:c + 1]
        rstd = self.rstd[:, c:c + 1]
        self.act(self.sqj[:], xsrc, AF.Square, [xbuf], ["sqj", f"ssq{c}"], accum_out=ssq)
        epsb = self.epsb
        self.act(rstd, ssq, AF.Sqrt, [f"ssq{c}", "epsb"], [f"rstd{c}"], scale=1.0 / D, bias=epsb[:])
        self.P.op("vector", lambda e: e.reciprocal(rstd, rstd), [f"rstd{c}"], [f"rstd{c}"])
        self.act(xn, xsrc, AF.Copy, [xbuf, f"rstd{c}"], [f"xn{xs}"], scale=rstd)
        return xn, f"xn{xs}"

    def norm_b(self, xn, xnbuf, dst, dstbuf, layer, t, stage):
        pT = self.bank[0].bitcast(BF16).rearrange("p (k t) -> p k t", t=128)
        for k in range(8):
            self.tr(pT[:, k, :], xn[:, k * 128:(k + 1) * 128], [xnbuf], ["ps0"])
        Aap = self.modfm[:, layer, 4 * t + 2 * stage, :]
        Bap = self.modfm[:, layer, 4 * t + 2 * stage + 1, :]
        self.tt("vector", dst, pT, Aap.unsqueeze(2).to_broadcast([128, 8, 128]), ALU.mult, ["ps0", "modfm"], [dstbuf])
        self.tt("vector", dst, dst, Bap.unsqueeze(2).to_broadcast([128, 8, 128]), ALU.add, [dstbuf, "modfm"], [dstbuf])

    def norm_T(self, xsrc, xbuf, dst, dstbuf, layer, t, stage):
        xn, xnbuf = self.norm_a(xsrc, xbuf)
        self.norm_b(xn, xnbuf, dst, dstbuf, layer, t, stage)

    A_SLOT = {**{b: b for b in range(17)}, 63: 17, 64: 18, 65: 19}

    def setup_l0(self):
        self.kAT = self.aux[:, 0:2560].rearrange("p (s t) -> p s t", t=128)
        self.vA = self.aux[:, 2560:5160].rearrange("p (s g d) -> p s g d", g=2, d=65)
        vA = self.vA
        self.P.op("gpsimd", lambda e: e.memset(vA[:, :, :, 64:65], 1.0), [], ["vA"])

    def load_perms(self, I):
        p64, p96 = self.perm64, self.perm96
        self.P.op("gpsimd", lambda e: [e.dma_start(out=p64[:], in_=I["perm64"]), e.dma_start(out=p96[:], in_=I["perm96"])],
                  [], ["permM"], chan="permM", ndma=2)

    def load_w(self, dst, src, kchunks, name="wbig"):
        return self.dma("gpsimd", dst, src, [], [name], name)

    def fm_x(self, ntok, hxT, hbuf, wf, col0, M, wname):
        b = self.rot("fmbank", 2)
        pm = self.bank[1 + b]
        K = hxT.shape[1]
        for k in range(K):
            self.mm(pm[0:M, :ntok], wf[:, k, col0:col0 + M], hxT[:, k, :ntok], k == 0, k == K - 1, [hbuf, wname], [f"ps{1 + b}"])
        kr = self.rot("kraw", 2)
        kraw = self.v_sm(8192 + kr * 512, [512])
        self.act(kraw[0:M, :ntok], pm[0:M, :ntok], AF.Copy, [f"ps{1 + b}"], [f"kraw{kr}"])
        return (b, kr, kraw)

    def fm_y(self, st, ntok, M, tabC, tabS, tbuf, perm=None):
        b, kr, kraw = st
        pm, pp = self.bank[1 + b], self.bank[3 + b]
        pmat = self.perm64 if perm is None else perm
        self.mm(pp[0:M, :ntok], pmat[0:M, 0:M], kraw[0:M, :ntok], True, True, [f"kraw{kr}", "permM"], [f"ps{3 + b}"])
        t1 = self.v_big(20512, [512], F32)
        t2 = self.v_big(21536, [512], F32)
        self.tt("vector", t1[0:M, :ntok], pm[0:M, :ntok], tabC[0:M, :ntok], ALU.mult, [f"ps{1 + b}", tbuf, f"kraw{kr}"], ["t1"])
        self.tt("vector", t2[0:M, :ntok], pp[0:M, :ntok], tabS[0:M, :ntok], ALU.mult, [f"ps{3 + b}", tbuf], ["t2"])
        ks = self.rot("kst", 4)
        kst = self.v_mid(12288 + ks * 512, [512])
        self.tt("gpsimd", kst[0:M, :ntok], t1[0:M, :ntok], t2[0:M, :ntok], ALU.add, ["t1", "t2"], [f"kst{ks}"])
        return kst, f"kst{ks}"

    def fm_rope_loop(self, chunks, ntok, hxT, hbuf, wf, M, tabC, tabS, tbuf, wname, perm, sink):
        if not chunks:
            return
        sts = {0: self.fm_x(ntok, hxT, hbuf, wf, chunks[0][1], M, wname)}
        for i, (cid, col0) in enumerate(chunks):
            if i + 1 < len(chunks):
                sts[i + 1] = self.fm_x(ntok, hxT, hbuf, wf, chunks[i + 1][1], M, wname)
            kst, kb = self.fm_y(sts.pop(i), ntok, M, tabC, tabS, tbuf, perm)
            sink(cid, kst, kb)

    def phase_kv0(self, I):
        P = self.P
        P.barrier()
        wkf = self.v_big(0, [8, 640])
        wkp = self.v_big(5120, [8, 640])
        wv = self.v_big(10240, [8, 640])
        self.load_w(wkf, I["w0k_f"], 8)
        self.load_w(wv, I["w0v"], 8)
        vst = [self.v_big(16384 + s * 2064, [4, 4, 129]) for s in range(2)]
        for s in range(2):
            P.op("gpsimd", lambda e, s=s: e.memset(vst[s][:, :, :, 128:129], 1.0), [], [f"vst{s}"])
        groups = [list(range(4 * g, 4 * g + 4)) for g in range(16)] + [[64, 65]]
        def geom(gi):
            blocks = groups[gi]
            nb = len(blocks)
            ntok = 128 * nb
            tok0 = blocks[0] * 128
            hs = gi % 2
            hxT = self.v_mid(hs * 4096, [8, 512])
            tabC = self.v_mid(8192 + hs * 2048, [512], F32)
            tabS = self.v_mid(8192 + hs * 2048 + 1024, [512], F32)
            return blocks, nb, ntok, tok0, hs, hxT, tabC, tabS

        xns = {}

        def part_norm(gi):
            blocks, nb, ntok, tok0, hs, hxT, tabC, tabS = geom(gi)
            P.op("sync", lambda e, tabC=tabC, tabS=tabS, tok0=tok0, ntok=ntok: [
                e.dma_start(out=tabC[:, :ntok], in_=I["ropeC0"][:, tok0:tok0 + ntok]),
                e.dma_start(out=tabS[:, :ntok], in_=I["ropeS0"][:, tok0:tok0 + ntok])], [], [f"tab{hs}"], chan=f"tab{hs}", ndma=2)
            for j, blk in enumerate(blocks):
                if blk < 16:
                    xs, xb = self.xres[:, blk, :], f"xres{blk}"
                    self.dma("sync", xs, I["xk"][blk * 128:(blk + 1) * 128, :], [], [xb], f"xres{blk % 2}")
                elif blk >= 64:
                    xs, xb = self.xres[:, 16 + blk - 64, :], f"xres{16 + blk - 64}"
                    self.dma("sync", xs, I["ctxin"][(blk - 64) * 128:(blk - 63) * 128, :], [], [xb], f"xres{blk % 2}")
                else:
                    s = self.rot("xin", 2)
                    xs, xb = self.v_sm(s * 2048, [D], F32), f"xin{s}"
                    self.dma("sync", xs, I["xk"][blk * 128:(blk + 1) * 128, :], [], [xb], f"xin{s}")
                xns[(gi, j)] = self.norm_a(xs, xb, 4)

        def part_b(gi):
            blocks, nb, ntok, tok0, hs, hxT, tabC, tabS = geom(gi)
            for j, blk in enumerate(blocks):
                xn, xnb = xns.pop((gi, j))
                self.norm_b(xn, xnb, hxT[:, :, j * 128:(j + 1) * 128], f"hxT{hs}", 0, 1 if blk >= 64 else 0, 0)

        def part_proj(gi):
            blocks, nb, ntok, tok0, hs, hxT, tabC, tabS = geom(gi)
            needA = [(j, self.A_SLOT[b]) for j, b in enumerate(blocks) if b in self.A_SLOT]
            def sink(ci, kst, kb):
                self.dma("sync", I["KT0"][ci, :, tok0:tok0 + ntok], kst[:, :ntok], [kb], [], f"kt_st{self.rot('ktst', 4)}")
                if ci == 0:
                    for j, slot in needA:
                        self.cp("gpsimd", self.kAT[:, slot, :], kst[:, j * 128:(j + 1) * 128], [kb], ["kAT"])
            chunks = [(ci, ci * 128) for ci in range(5) if not (ci == 0 and not needA)]
            self.fm_rope_loop(chunks, ntok, hxT, f"hxT{hs}", wkf, 128, tabC, tabS, f"tab{hs}", "wbig", None, sink)
            vs = gi % 2
            for j, blk in enumerate(blocks):
                vb_ = 5 + self.rot("vbank", 2)
                pv, pa = self.bank[vb_], self.bank[7]
                for k in range(8):
                    self.mm(pv, hxT[:, k, j * 128:(j + 1) * 128], wv[:, k, 128:640], k == 0, k == 7, [f"hxT{hs}", "wbig"], [f"ps{vb_}"])
                self.act(vst[vs][:, :, j, 0:128], pv.rearrange("p (h d) -> p h d", d=128), AF.Copy, [f"ps{vb_}"], [f"vst{vs}"])
                if blk in self.A_SLOT:
                    for k in range(8):
                        self.mm(pa[:, 0:128], hxT[:, k, j * 128:(j + 1) * 128], wv[:, k, 0:128], k == 0, k == 7, [f"hxT{hs}", "wbig"], ["ps7"])
                    self.act(self.vA[:, self.A_SLOT[blk], :, 0:64], pa[:, 0:128].rearrange("p (h d) -> p h d", d=64), AF.Copy, ["ps7"], ["vA"])
            b0 = blocks[0]
            self.dma("sync", I["VB0"][:, :, b0:b0 + nb, :].rearrange("h p b d -> p h b d"), vst[vs][:, :, 0:nb, :], [f"vst{vs}"], [], f"vb_st{vs}")

        part_norm(0)
        part_b(0)
        for gi in range(len(groups)):
            if gi + 1 < len(groups):
                part_norm(gi + 1)
            part_proj(gi)
            if gi + 1 < len(groups):
                part_b(gi + 1)
        self.debug_out("kAT", self.kAT[:], [128, 20, 128], ["kAT"], BF16)
        self.debug_out("vA", self.vA[:], [128, 20, 2, 65], ["vA"], BF16)

    def phase_q0(self, I):
        P = self.P
        P.barrier()
        wqf = self.v_big(0, [8, 1024])
        wqp = self.v_big(8192, [8, 1024])
        self.load_w(wqf, I["w0q_f"], 8)
        groups = [list(range(4 * g, 4 * g + 4)) for g in range(4)] + [[16, 17]]
        def geom(gi):
            blocks = groups[gi]
            nb = len(blocks)
            ntok = 128 * nb
            q0 = blocks[0] * 128
            tok0 = q0 if blocks[0] < 16 else NT
            hs = gi % 2
            hxT = self.v_mid(hs * 4096, [8, 512])
            tabC = self.v_mid(8192 + hs * 2048, [512], F32)
            tabS = self.v_mid(8192 + hs * 2048 + 1024, [512], F32)
            return blocks, ntok, q0, tok0, hs, hxT, tabC, tabS

        def part_norm(gi):
            blocks, ntok, q0, tok0, hs, hxT, tabC, tabS = geom(gi)
            P.op("sync", lambda e, tabC=tabC, tabS=tabS, tok0=tok0, ntok=ntok: [
                e.dma_start(out=tabC[:, :ntok], in_=I["ropeC0"][:, tok0:tok0 + ntok]),
                e.dma_start(out=tabS[:, :ntok], in_=I["ropeS0"][:, tok0:tok0 + ntok])], [], [f"tab{hs}"], chan=f"tab{hs}", ndma=2)
            for j, blk in enumerate(blocks):
                xns[(gi, j)] = self.norm_a(self.xres[:, blk, :], f"xres{blk}", 4)

        def part_b(gi):
            blocks, ntok, q0, tok0, hs, hxT, tabC, tabS = geom(gi)
            for j, blk in enumerate(blocks):
                xn, xnb = xns.pop((gi, j))
                self.norm_b(xn, xnb, hxT[:, :, j * 128:(j + 1) * 128], f"hxT{hs}", 0, 1 if blk >= 16 else 0, 0)

        def part_proj(gi):
            blocks, ntok, q0, tok0, hs, hxT, tabC, tabS = geom(gi)
            def sink(ci, kst, kb):
                self.dma("sync", I["QT0"][ci, :, q0:q0 + ntok], kst[:, :ntok], [kb], [], f"kt_st{self.rot('ktst', 4)}")
            self.fm_rope_loop([(ci, ci * 128) for ci in range(8)], ntok, hxT, f"hxT{hs}", wqf, 128, tabC, tabS, f"tab{hs}", "wbig", None, sink)

        xns = {}
        part_norm(0)
        part_b(0)
        for gi in range(len(groups)):
            if gi + 1 < len(groups):
                part_norm(gi + 1)
            part_proj(gi)
            if gi + 1 < len(groups):
                part_b(gi + 1)

    def phase_attA(self, I):
        P = self.P
        P.barrier()
        osb = self.v_big(0, [18, 512])
        self.osb = osb
        self.oTB = self.v_big(9216, [4, NQ])
        amask = self.v_big(18432, [4, 512])
        self.dma("gpsimd", amask, I["amask"], [], ["amask"], "amask")
        esink = self.v_sm(0, [8], F32)
        self.dma("sync", esink, I["sink"].partition_broadcast(128), [], ["esink"], "esink")
        self.act(esink, esink, AF.Exp, ["esink"], ["esink"])
        ident = self.ident
        for qi in range(18):
            qs = self.rot("qa", 2)
            qa = self.v_mid(qs * 512, [4, 128])
            self.dma("sync", qa, I["QT0"][0:4, :, qi * 128:(qi + 1) * 128].rearrange("c p t -> p c t"), [], [f"qa{qs}"], f"qa{qs}")
            if qi < 16:
                kbs = [(qi - 1 if qi > 0 else 17, 0 if qi == 0 else 1), (qi, None), (qi + 1, 3 if qi == 15 else 2), (18, None), (19, None)]
            else:
                kbs = [(18, None), (19, None)]
            its = [(g, ki, slot, mk) for g in range(2) for ki, (slot, mk) in enumerate(kbs)]
            stA = {}

            def qk(i, qa=qa, qs=qs, its=its, stA=stA):
                g, ki, slot, mk = its[i]
                sbk = self.rot("psA", 2)
                ps = self.bank[sbk]
                self.mm(ps, self.kAT[64 * g:64 * g + 64, slot, :], qa[64 * g:64 * g + 64].rearrange("p c t -> p (c t)"),
                        True, mk is None, ["kAT", f"qa{qs}"], [f"ps{sbk}"])
                if mk is not None:
                    self.mm(ps, ident[:], amask[:, mk, :], False, True, ["ident", "amask"], [f"ps{sbk}"])
                stA[i] = sbk

            def exp_pv(i, qi=qi, its=its, stA=stA, nk=len(kbs)):
                g, ki, slot, mk = its[i]
                sbk = stA.pop(i)
                ps = self.bank[sbk]
                pov = self.bank[4 + g][:, 0:260].rearrange("p (c d) -> p c d", d=65)
                pt = self.rot("pTA", 3)
                pT = self.v_mid(1024 + pt * 512, [512])
                self.act(pT, ps, AF.Exp, [f"ps{sbk}"], [f"pTA{pt}"], scale=0.125)
                for c in range(4):
                    self.mm(pov[:, c, :], pT[:, c * 128:(c + 1) * 128], self.vA[:, slot, g, :], ki == 0 and c == 0,
                            ki == nk - 1, [f"pTA{pt}", "vA"], [f"ps{4 + g}"])
                if ki == nk - 1:
                    dn = self.rot("denA", 2)
                    den = self.v_sm(64 + dn * 16, [4], F32)
                    self.tt("vector", den, pov[:, :, 64], esink[:, 4 * g:4 * g + 4], ALU.add, [f"ps{4 + g}", "esink"], [f"denA{dn}"])
                    self.P.op("vector", lambda e, den=den: e.reciprocal(den, den), [f"denA{dn}"], [f"denA{dn}"])
                    self.tt("vector", osb[:, qi, g * 256:(g + 1) * 256].rearrange("p (c d) -> p c d", d=64), pov[:, :, 0:64],
                            den.unsqueeze(2).to_broadcast([128, 4, 64]), ALU.mult, [f"ps{4 + g}", f"denA{dn}"], [f"osb{qi}"])

            qk(0)
            for i in range(len(its)):
                if i + 1 < len(its):
                    qk(i + 1)
                exp_pv(i)

    def attn_pass(self, rows, pbase, q_src, k_srcs, v_src, dv1, scale, qgroups, fin, tag):
        per = 512 // dv1
        for (qb0, nqb, kbA, nkbs) in qgroups:
            ob = 4 + 2 * self.rot("obase", 2) if dv1 == 65 else 4
            nq = nqb * 128
            qs = self.rot("qT", 2)
            qT = self.v_mid(qs * 1024, [1024])
            self.dma("sync", qT[pbase:pbase + rows, :nq], q_src(qb0 * 128, nq), [], [f"qT{qs}"], f"qT{qs}")
            po = lambda qb, ob=ob: self.bank[ob + qb // per][:, (qb % per) * dv1:(qb % per + 1) * dv1]
            pobuf = lambda qb, ob=ob: f"ps{ob + qb // per}"
            started = set()
            its = []
            done = 0
            while done < nkbs:
                npb = min(11, nkbs - done)
                for kl in range(npb):
                    its.append((kbA + done, npb, kl, done + kl == nkbs - 1))
                done += npb
            state = {}
            loaded = {}

            def emit_qk(i):
                kb0, npb, kl, last = its[i]
                if kl == 0:
                    def load_piece(kb0_, npb_):
                        sl = self.rot("kvp", 3)
                        Kp = self.v_mid(2048 + sl * 1408, [1408])
                        Vp = self.v_mid(2048 + 3 * 1408 + sl * 1420, [11, 129])[:, 0:npb_, 0:dv1] if dv1 == 129 else \
                            self.v_mid(2048 + 3 * 1408 + sl * 1420, [11 * 65])[:, 0:npb_ * 65].rearrange("p (b d) -> p b d", d=65)
                        self.P.op("sync", lambda e, Kp=Kp: [
                            e.dma_start(out=Kp[pbase + ro:pbase + ro + nr, :npb_ * 128], in_=fn(kb0_ * 128, npb_ * 128)) for (ro, nr, fn) in k_srcs],
                            [], [f"Kp{sl}"], chan=f"Kp{sl}", ndma=len(k_srcs))
                        self.dma("sync", Vp, v_src(kb0_, npb_), [], [f"Vp{sl}"], f"Vp{sl}")
                        loaded[kb0_] = (sl, Kp, Vp)
                    if kb0 not in loaded:
                        load_piece(kb0, npb)
                    nxt = [(a_, b_) for (a_, b_, c_, d_) in its[i + 1:] if c_ == 0][:1]
                    for (a_, b_) in nxt:
                        if a_ not in loaded:
                            load_piece(a_, b_)
                    state["piece"] = loaded[kb0]
                sl, Kp, Vp = state["piece"]
                ss = self.rot("psS", 2)
                nh = (nq + 511) // 512
                for hh in range(nh):
                    w = min(512, nq - hh * 512)
                    self.mm(self.bank[2 * ss + hh][:, :w], Kp[pbase:pbase + rows, kl * 128:(kl + 1) * 128],
                            qT[pbase:pbase + rows, hh * 512:hh * 512 + w], True, True, [f"Kp{sl}", f"qT{qs}"], [f"psS{ss}"])
                state[i] = (ss, sl, Vp, kl, last)

            def emit_exp_pv(i):
                ss, sl, Vp, kl, last = state.pop(i)
                pt = self.rot("pT", 3)
                pT = self.v_mid(2048 + 3 * 1408 + 3 * 1420 + pt * 1024, [1024])
                self.act(pT[:, :nq], self.psum[:, 2 * ss * 512:2 * ss * 512 + nq], AF.Exp, [f"psS{ss}"], [f"pT{pt}"], scale=scale)
                for qb in range(nqb):
                    bk = ob + qb // per
                    st = bk not in started
                    started.add(bk)
                    self.mm(po(qb), pT[:, qb * 128:(qb + 1) * 128], Vp[:, kl, :], st, last, [f"pT{pt}", f"Vp{sl}"], [pobuf(qb)])

            emit_qk(0)
            for i in range(len(its)):
                if i + 1 < len(its):
                    emit_qk(i + 1)
                emit_exp_pv(i)
            fin(qb0, nqb, per, ob)

    def attn_pass_fm(self, pbase, q_src, k_src, v_src, scale, qg, fin):
        (qb0, nqb, kbA, nkbs) = qg
        rows = 64
        nq = nqb * 128
        qs = self.rot("qT", 2)
        qT = self.v_mid(qs * 1024, [1024])
        self.dma("sync", qT[pbase:pbase + rows, :nq], q_src(qb0 * 128, nq), [], [f"qT{qs}"], f"qT{qs}")
        ones = self.ones
        acc = self.v_mid(13604, [1024], F32)
        hi = self.v_sm(8192, [1024])
        lo = self.v_mid(15652, [1024])
        nh = (nq + 511) // 512
        its = []
        done = 0
        while done < nkbs:
            npb = min(11, nkbs - done)
            for kl in range(npb):
                its.append((kbA + done, npb, kl, done + kl == 0, done + kl == nkbs - 1))
            done += npb
        state = {}
        loaded = {}

        def emit_qk(i):
            kb0, npb, kl, first, last = its[i]
            if kl == 0:
                def load_piece(kb0_, npb_):
                    sl = self.rot("kvp", 3)
                    Kp = self.v_mid(2048 + sl * 1408, [1408])
                    Vp = self.v_mid(2048 + 3 * 1408 + sl * 1420, [11, 129])[:, 0:npb_, :]
                    self.dma("sync", Kp[pbase:pbase + rows, :npb_ * 128], k_src(kb0_ * 128, npb_ * 128), [], [f"Kp{sl}"], f"Kp{sl}")
                    self.dma("sync", Vp, v_src(kb0_, npb_), [], [f"Vp{sl}"], f"Vp{sl}")
                    loaded[kb0_] = (sl, Kp, Vp)
                if kb0 not in loaded:
                    load_piece(kb0, npb)
                nxt = [(a_, b_) for (a_, b_, c_, d_, e_) in its[i + 1:] if c_ == 0][:1]
                for (a_, b_) in nxt:
                    if a_ not in loaded:
                        load_piece(a_, b_)
                state["piece"] = loaded[kb0]
            sl, Kp, Vp = state["piece"]
            ss = self.rot("psS", 2)
            for hh in range(nh):
                w = min(512, nq - hh * 512)
                self.mm(self.bank[2 * ss + hh][:, :w], Kp[pbase:pbase + rows, kl * 128:(kl + 1) * 128],
                        qT[pbase:pbase + rows, hh * 512:hh * 512 + w], True, True, [f"Kp{sl}", f"qT{qs}"], [f"psS{ss}"])
            state[i] = (ss, sl, Vp, kl, first, last)

        def emit_exp_pv(i):
            ss, sl, Vp, kl, first, last = state.pop(i)
            pt = self.rot("pT", 3)
            pT = self.v_mid(2048 + 3 * 1408 + 3 * 1420 + pt * 1024, [1024])
            self.act(pT[:, :nq], self.psum[:, 2 * ss * 512:2 * ss * 512 + nq], AF.Exp, [f"psS{ss}"], [f"pT{pt}"], scale=scale)
            for hh in range(nh):
                w = min(512, nq - hh * 512)
                self.mm(self.bank[4 + hh][:, :w], Vp[:, kl, 0:128], pT[:, hh * 512:hh * 512 + w], first, last, [f"pT{pt}", f"Vp{sl}"], [f"ps{4 + hh}"])
                self.mm(self.bank[6 + hh][:, :w], ones[:], pT[:, hh * 512:hh * 512 + w], first, last, [f"pT{pt}", "ones"], [f"ps{6 + hh}"])

        emit_qk(0)
        for i in range(len(its)):
            if i + 1 < len(its):
                emit_qk(i + 1)
            emit_exp_pv(i)
        fin(qb0, nq, nh)

    def phase_attB(self, I):
        P = self.P
        P.barrier()
        self.precast_ffn(0, I["w13t0"], I["w2t0"])
        oTB = self.oTB
        self.ones = self.v_sm(3584, [128])
        ones = self.ones
        P.op("gpsimd", lambda e: e.memset(ones, 1.0), [], ["ones"])
        lv = self.v_sm(128, [4, 64], F32)
        prod = self.v_sm(640, [2, 64], F32)
        sums = self.v_sm(896, [2], F32)
        lam = self.v_sm(904, [1], F32)
        subg = self.v_sm(1024, [1], F32)
        self.dma("sync", lv, I["dlam"].partition_broadcast(128).rearrange("p (a b) -> p a b", b=64), [], ["lv"], "lv")
        self.dma("sync", subg, I["subg"].rearrange("(p o) -> p o", o=1), [], ["subg"], "subg")
        self.tt("vector", prod, lv[:, 0:4:2, :], lv[:, 1:4:2, :], ALU.mult, ["lv"], ["prod"])
        self.P.op("vector", lambda e: e.tensor_reduce(out=sums, in_=prod, axis=AX.X, op=ALU.add), ["prod"], ["sums"])
        self.act(sums, sums, AF.Exp, ["sums"], ["sums"])
        self.tt("vector", lam, sums[:, 1:2], sums[:, 0:1], ALU.subtract, ["sums"], ["lam"])
        self.ts("vector", lam, lam, -LAM0, None, ALU.add, None, ["lam"], ["lam"])
        self.ts("vector", subg, subg, 1.0 - LAM0, None, ALU.mult, None, ["subg"], ["subg"])
        R = self.v_sm(4096, [1024], F32)
        t1 = self.v_sm(6144, [1024], F32)
        sq = self.v_sm(8192, [1024])
        epsb = self.epsb

        def make_fin(h, m):
            def fin(qb0, nq, nh):
                Sb = self.psum[:, 6 * 512:6 * 512 + nq]
                O = self.psum[:, 4 * 512:4 * 512 + nq]
                pO = ["ps4", "ps5"][:nh]
                pS = ["ps6", "ps7"][:nh]
                Rv, tv = R[:, :nq], t1[:, :nq]
                self.act(Rv, Sb, AF.Ln, pS, ["Rb"])
                self.act(Rv, Rv, AF.Exp, ["Rb"], ["Rb"], scale=-1.0)
                if m == 0:
                    self.tt("vector", tv, O, Rv, ALU.mult, pO + ["Rb"], ["t1b"])
                    return
                self.tt("vector", Rv, O, Rv, ALU.mult, pO + ["Rb"], ["Rb"])
                self.P.op("vector", lambda e: e.scalar_tensor_tensor(out=tv, in0=Rv, scalar=lam[:, 0:1], in1=tv, op0=ALU.mult, op1=ALU.add),
                          ["Rb", "t1b", "lam"], ["t1b"])
                self.tt("vector", sq[:, :nq], tv, tv, ALU.mult, ["t1b"], ["sqb"])
                for hh in range(nh):
                    w = min(512, nq - hh * 512)
                    self.mm(self.bank[6 + hh][:, :w], ones[:], sq[:, hh * 512:hh * 512 + w], True, True, ["sqb", "ones"], [f"ps{6 + hh}"])
                self.act(Rv, Sb, AF.Ln, pS + ["epsb"], ["Rb"], scale=1.0 / 128, bias=epsb[:])
                self.act(Rv, Rv, AF.Exp, ["Rb"], ["Rb"], scale=-0.5)
                q0 = qb0 * 128
                self.P.op("vector", lambda e: e.scalar_tensor_tensor(out=oTB[:, h, q0:q0 + nq], in0=tv, scalar=subg[:, 0:1], in1=Rv,
                                                                      op0=ALU.mult, op1=ALU.mult), ["t1b", "Rb", "subg"], ["oTB"])
            return fin

        qgroups = [(0, 8, 0, 66), (8, 8, 0, 66), (16, 2, 64, 2)]
        for h in range(4):
            for qg in qgroups:
                for m in range(2):
                    self.attn_pass_fm(64 * m,
                                      lambda q0, nq, h=h, m=m: I["QT0"][4 + h, 64 * m:64 * m + 64, q0:q0 + nq],
                                      lambda c0, ncl, h=h, m=m: I["KT0"][1 + h, 64 * m:64 * m + 64, c0:c0 + ncl],
                                      lambda kb0, n, h=h: I["VB0"][h, :, kb0:kb0 + n, :],
                                      0.125, qg, make_fin(h, m))

    def resid(self, blk, ybanks, gidx):
        b0 = ybanks
        y = self.psum[:, b0 * 512:b0 * 512 + D]
        ybufs = [f"ps{b0}", f"ps{b0 + 1}"]
        c = self.rot("ssq", 8)
        ssq = self.ssq[:, c:c + 1]
        rstd = self.rstd[:, c:c + 1]
        epsb = self.epsb
        self.act(self.sqj[:], y, AF.Square, ybufs, ["sqj", f"ssq{c}"], accum_out=ssq)
        self.act(rstd, ssq, AF.Sqrt, [f"ssq{c}", "epsb"], [f"rstd{c}"], scale=1.0 / D, bias=epsb[:])
        self.P.op("vector", lambda e: e.reciprocal(rstd, rstd), [f"rstd{c}"], [f"rstd{c}"])
        tmpf = self.v_sm(6144, [D], F32)
        G = self.gates[:, gidx, :]
        self.P.op("vector", lambda e: e.scalar_tensor_tensor(out=tmpf, in0=y, scalar=rstd, in1=G, op0=ALU.mult, op1=ALU.mult),
                  ybufs + [f"rstd{c}", "gates"], ["tmpf"])
        xr = self.xres[:, blk, :]
        self.tt("gpsimd", xr, xr, tmpf, ALU.add, ["tmpf", f"xres{blk}"], [f"xres{blk}"])

    def phase_outproj(self, layer, wsrc):
        P = self.P
        P.barrier()
        osb = self.osb
        wout = self.v_mid(0, [8, D])
        self.load_w(wout, wsrc, 8, "wmid")
        nblk = 18 if layer == 0 else 16
        nt = 4 if layer == 0 else 8
        oTs = {}

        def stage_t(blk):
            pT = self.bank[0].bitcast(BF16).rearrange("p (k t) -> p k t", t=128)
            for k in range(nt):
                self.tr(pT[:, k, :], osb[:, blk, k * 128:(k + 1) * 128], [f"osb{blk}"], ["ps0"])
            s = self.rot("oT", 2)
            oT = self.v_mid(8192 + s * 1024, [8, 128])
            self.cp("vector", oT[:, 0:nt, :], pT[:, 0:nt, :], ["ps0"], [f"oT{s}"])
            oTs[blk] = (s, oT)

        def stage_m(blk):
            s, oT = oTs.pop(blk)
            yb = 1 + 2 * self.rot("ybank", 2)
            for hh in range(2):
                for k in range(8):
                    if k < nt:
                        lhs, rd = oT[:, k, :], f"oT{s}"
                    else:
                        lhs, rd = self.oTB[:, k - 4, blk * 128:(blk + 1) * 128], "oTB"
                    self.mm(self.bank[yb + hh], lhs, wout[:, k, hh * 512:(hh + 1) * 512], k == 0, k == 7, [rd, "wmid"], [f"ps{yb + hh}"])
            self.resid(blk, yb, 0 if blk < 16 else 2)

        stage_t(0)
        for blk in range(nblk):
            if blk + 1 < nblk:
                stage_t(blk + 1)
            stage_m(blk)

    def precast_ffn(self, layer, w13t, w2t):
        self.w13b = getattr(self, "w13b", {})
        self.w2b = getattr(self, "w2b", {})
        w13b = self.dint(f"w13b{layer}", [22, 128, 2048], BF16)
        w2b = self.dint(f"w2b{layer}", [128, 22 * D], BF16)
        self.w13b[layer], self.w2b[layer] = w13b, w2b
        for hc in range(22):
            self.dma("gpsimd", w13b[hc], w13t[hc].rearrange("p k c -> p (k c)"), [], [f"w13b{layer}_{hc}"], f"pc{self.rot('pc', 4)}")
        for hc in range(22):
            self.dma("gpsimd", w2b[:, hc * D:(hc + 1) * D], w2t[:, hc, :], [], [f"w2b{layer}"], f"pc{self.rot('pc', 4)}")

    def phase_ffn(self, layer, w13t, w2t, out_ap=None):
        P = self.P
        P.barrier()
        w2 = self.v_big(0, [22, D])
        w13b, w2b = self.w13b[layer], self.w2b[layer]
        self.dma("sync", w2, w2b.rearrange("p (k c) -> p k c", c=D), [f"w2b{layer}"], ["wbig"], "wbig_hw")
        actT = self.v_mid(0, [22, 768])
        hxT = self.aux[:, 0:6144].rearrange("p (k t) -> p k t", t=768)
        groups = [list(range(0, 6)), list(range(6, 12)), list(range(12, 18 if layer == 0 else 16))]
        for j, blk in enumerate(groups[0]):
            self.norm_T(self.xres[:, blk, :], f"xres{blk}", hxT[:, :, j * 128:(j + 1) * 128], "hxF", layer, 1 if blk >= 16 else 0, 1)
        for gi, blocks in enumerate(groups):
            ntok = 128 * len(blocks)
            nxt = groups[gi + 1] if gi + 1 < len(groups) else []
            halves = [(0, min(384, ntok))] + ([(384, ntok - 384)] if ntok > 384 else [])
            for hc in range(22):
                s = hc % 2
                wt = self.v_sm(s * 2048, [8, 256])
                self.dma("sync", wt, w13b[hc].rearrange("p (k c) -> p k c", c=256), [f"w13b{layer}_{hc}"], [f"w13_{s}"], f"w13_{s}")
                for hi, (t0, tw) in enumerate(halves):
                    pg, pu = self.bank[1 + hi], self.bank[3 + hi]
                    for k in range(8):
                        self.mm(pg[:, :tw], wt[:, k, 0:128], hxT[:, k, t0:t0 + tw], k == 0, k == 7, [f"w13_{s}", "hxF"], [f"ps{1 + hi}"])
                    for k in range(8):
                        self.mm(pu[:, :tw], wt[:, k, 128:256], hxT[:, k, t0:t0 + tw], k == 0, k == 7, [f"w13_{s}", "hxF"], [f"ps{3 + hi}"])
                    sgs = self.rot("sg", 2)
                    sg = self.v_sm(8192 + sgs * 384, [384])
                    self.act(sg[:, :tw], pg[:, :tw], AF.Silu, [f"ps{1 + hi}"], [f"sg{sgs}"])
                    self.tt("vector", actT[:, hc, t0:t0 + tw], sg[:, :tw], pu[:, :tw], ALU.mult, [f"sg{sgs}", f"ps{3 + hi}"], ["actT"])
            for j, blk in enumerate(blocks):
                pre = None
                if j < len(nxt):
                    nb_ = nxt[j]
                    pre = self.norm_a(self.xres[:, nb_, :], f"xres{nb_}")
                yb = 5 if j % 2 == 0 else 1
                for hh in range(2):
                    for hc in range(22):
                        self.mm(self.bank[yb + hh], actT[:, hc, j * 128:(j + 1) * 128], w2[:, hc, hh * 512:(hh + 1) * 512], hc == 0, hc == 21,
                                ["actT", "wbig"], [f"ps{yb + hh}"])
                if pre is not None:
                    self.norm_b(pre[0], pre[1], hxT[:, :, j * 128:(j + 1) * 128], "hxF", layer, 1 if nb_ >= 16 else 0, 1)
                self.resid(blk, yb, 1 if blk < 16 else 3)
                if out_ap is not None and blk < 16:
                    f = self.dma("sync", out_ap[blk * 128:(blk + 1) * 128, :], self.xres[:, blk, :], [f"xres{blk}"], [], f"xout{blk % 2}")
                    self.finals.append(f)

    def phase_mla_pre(self, I):
        P = self.P
        P.barrier()
        w1in = self.v_big(0, [8, 704])
        wqf = self.v_big(5632, [3, 1536])
        wqp = self.v_big(10240, [3, 1536])
        self.load_w(w1in, I["w1in"], 8, "wbig")
        self.load_w(wqf, I["wq1f"], 3, "wbig2")
        ng = self.v_sm(0, [5], F32)
        self.dma("sync", ng, I["mla_ng"], [], ["mla_ng"], "mla_ng")
        krC = self.v_mid(14336, [512], F32)
        krS = self.v_mid(15360, [512], F32)
        epsb = self.epsb
        groups = [list(range(4 * g, 4 * g + 4)) for g in range(4)] + [[16, 17]]
        def geom(gi):
            blocks = groups[gi]
            ntok = 128 * len(blocks)
            q0 = blocks[0] * 128
            s = gi % 2
            return blocks, ntok, q0, s, self.v_mid(s * 1536, [3, 512]), self.v_mid(3072 + s * 1024, [2, 512]), self.v_mid(5120 + s * 512, [512])

        def part1(gi):
            blocks, ntok, q0, s, qnT, kvnT, krT = geom(gi)
            P.op("sync", lambda e, q0=q0, ntok=ntok: [
                e.dma_start(out=krC[0:32, :ntok], in_=I["kr1C"][:, q0:q0 + ntok]),
                e.dma_start(out=krS[0:32, :ntok], in_=I["kr1S"][:, q0:q0 + ntok])], [], ["krtab"], chan="krtab", ndma=2)
            stq = {}

            def sA(j):
                blk = blocks[j]
                stq[("xn", j)] = self.norm_a(self.xres[:, blk, :], f"xres{blk}")

            def sB(j):
                blk = blocks[j]
                hs = self.rot("hx1", 2)
                hxT = self.aux[:, hs * 1024:(hs + 1) * 1024].rearrange("p (k t) -> p k t", t=128)
                xn, xnb = stq.pop(("xn", j))
                self.norm_b(xn, xnb, hxT, f"hx1_{hs}", 1, 1 if blk >= 16 else 0, 0)
                pa, pq = self.bank[5], self.bank[6]
                for k in range(8):
                    self.mm(pa[:, 0:320], hxT[:, k, :], w1in[:, k, 0:320], k == 0, k == 7, [f"hx1_{hs}", "wbig"], ["ps5"])
                for k in range(8):
                    self.mm(pq[:, 0:384], hxT[:, k, :], w1in[:, k, 320:704], k == 0, k == 7, [f"hx1_{hs}", "wbig"], ["ps6"])

            def sC1(j):
                pa, pq = self.bank[5], self.bank[6]
                c = self.rot("ssq", 8)
                c2 = self.rot("ssq", 8)
                for (cc, src, n, pb) in ((c, pa[:, 0:256], 256, "ps5"), (c2, pq[:, 0:384], 384, "ps6")):
                    ssq = self.ssq[:, cc:cc + 1]
                    rstd = self.rstd[:, cc:cc + 1]
                    self.act(self.sqj[:, 0:n], src, AF.Square, [pb], ["sqj", f"ssq{cc}"], accum_out=ssq)
                    self.act(rstd, ssq, AF.Sqrt, [f"ssq{cc}", "epsb"], [f"rstd{cc}"], scale=1.0 / n, bias=epsb[:])
                    self.P.op("vector", lambda e, rstd=rstd: e.reciprocal(rstd, rstd), [f"rstd{cc}"], [f"rstd{cc}"])
                ts_ = self.rot("tk", 2)
                tk = self.v_mid(6144 + ts_ * 704, [704])
                self.act(tk[:, 0:256], pa[:, 0:256], AF.Copy, ["ps5", f"rstd{c}"], [f"tk{ts_}"], scale=self.rstd[:, c:c + 1])
                self.cp("vector", tk[:, 256:320], pa[:, 256:320], ["ps5"], [f"tk{ts_}"])
                self.act(tk[:, 320:704], pq[:, 0:384], AF.Copy, ["ps6", f"rstd{c2}"], [f"tk{ts_}"], scale=self.rstd[:, c2:c2 + 1])
                stq[("tk", j)] = (ts_, tk)

            def sC2(j):
                ts_, tk = stq.pop(("tk", j))
                pT = self.bank[7].bitcast(BF16).rearrange("p (k t) -> p k t", t=128)
                self.tr(pT[:, 0, :], tk[:, 0:128], [f"tk{ts_}"], ["ps7"])
                self.tr(pT[:, 1, :], tk[:, 128:256], [f"tk{ts_}"], ["ps7"])
                self.tr(pT[0:32, 2, :], tk[:, 256:288], [f"tk{ts_}"], ["ps7"])
                self.tr(pT[0:32, 3, :], tk[:, 288:320], [f"tk{ts_}"], ["ps7"])
                for cc in range(3):
                    self.tr(pT[:, 4 + cc, :], tk[:, 320 + cc * 128:320 + (cc + 1) * 128], [f"tk{ts_}"], ["ps7"])
                tsl = slice(j * 128, (j + 1) * 128)
                self.tt("vector", kvnT[:, :, tsl], pT[:, 0:2, :], ng[:, 0:2].unsqueeze(2).to_broadcast([128, 2, 128]), ALU.mult,
                        ["ps7", "mla_ng"], [f"kvnT{s}"])
                self.tt("vector", qnT[:, :, tsl], pT[:, 4:7, :], ng[:, 2:5].unsqueeze(2).to_broadcast([128, 3, 128]), ALU.mult,
                        ["ps7", "mla_ng"], [f"qnT{s}"])
                t1 = self.v_big(20512, [512], F32)
                t2 = self.v_big(21536, [512], F32)
                self.tt("vector", t1[0:32, 0:128], pT[0:32, 2, :], krC[0:32, tsl], ALU.mult, ["ps7", "krtab"], ["t1"])
                self.tt("vector", t2[0:32, 0:128], pT[0:32, 3, :], krS[0:32, tsl], ALU.mult, ["ps7", "krtab"], ["t2"])
                self.tt("gpsimd", krT[0:32, tsl], t1[0:32, 0:128], t2[0:32, 0:128], ALU.add, ["t1", "t2"], [f"krT{s}"])

            nbk = len(blocks)
            sA(0)
            sB(0)
            for j in range(nbk):
                if j + 1 < nbk:
                    sA(j + 1)
                sC1(j)
                if j + 1 < nbk:
                    sB(j + 1)
                sC2(j)
            if blocks[0] < 16:
                gdst, gname = I["gown"][gi], f"gown{gi}"
            else:
                gdst, gname = I["gctx"], "gctx"
            self.dma("sync", gdst[0:256, 0:ntok].rearrange("(c p) t -> p c t", p=128), kvnT[:, :, :ntok], [f"kvnT{s}"], [gname], "g_st")
            self.dma("sync", gdst[256:288, 0:ntok], krT[0:32, :ntok], [f"krT{s}"], [gname], "g_st2")
            if blocks[0] >= 16:
                return
            if self.mode == "FUSED":
                rg = [[0, 1, 2, 3], [4, 5, 6, 7]]
                self.P.op("gpsimd", lambda e, gi=gi: e.collective_compute("AllGather", ALU.bypass, replica_groups=rg,
                                                                          ins=[I["gown"][gi]], outs=[I["gall"][gi]]),
                          [gname], [f"gall{gi}"], chan="cc", cinc=1)
            tabC = self.v_mid(8192 + s * 2048, [512], F32)
            tabS = self.v_mid(8192 + s * 2048 + 1024, [512], F32)
            P.op("sync", lambda e, tabC=tabC, tabS=tabS, q0=q0, ntok=ntok: [
                e.dma_start(out=tabC[0:96, :ntok], in_=I["rope1C"][:, q0:q0 + ntok]),
                e.dma_start(out=tabS[0:96, :ntok], in_=I["rope1S"][:, q0:q0 + ntok])], [], [f"tab{s}"], chan=f"tab{s}", ndma=2)

        def part2(gi):
            blocks, ntok, q0, s, qnT, kvnT, krT = geom(gi)
            if blocks[0] >= 16:
                return
            tabC = self.v_mid(8192 + s * 2048, [512], F32)
            tabS = self.v_mid(8192 + s * 2048 + 1024, [512], F32)
            def sink(h, kst, kbn):
                self.dma("sync", I["QT1"][h, :, q0:q0 + ntok], kst[0:96, :ntok], [kbn], [], f"kt_st{self.rot('ktst', 4)}")
            self.fm_rope_loop([(h, h * 96) for h in range(16)], ntok, qnT, f"qnT{s}", wqf, 96, tabC, tabS, f"tab{s}", "wbig2", self.perm96, sink)

        part1(0)
        for gi in range(len(groups)):
            if gi + 1 < len(groups):
                part1(gi + 1)
            part2(gi)

    def phase_kv1(self, I):
        P = self.P
        P.barrier()
        wk1 = self.v_big(0, [2, D])
        wv1 = self.v_big(2048, [2, D])
        self.load_w(wk1, I["wk1"], 2, "wbig")
        self.load_w(wv1, I["wv1"], 2, "wbig")
        vst = [self.v_big(4096 + s2 * 4160, [16, 4, 65]) for s2 in range(2)]
        for s2 in range(2):
            P.op("gpsimd", lambda e, s2=s2: e.memset(vst[s2][:, :, :, 64:65], 1.0), [], [f"vst{s2}"])
        ngroups = 17

        def geom(gi):
            nb = 4 if gi < 16 else 2
            s = gi % 2
            return nb, 128 * nb, gi * 512, s, self.v_mid(s * 1024, [2, 512]), self.v_mid(2048 + s * 512, [512])

        def load(gi):
            nb, ntok, tok0, s, kvnT, krT = geom(gi)
            if gi < 16:
                r, jj = gi // 4, gi % 4
                src, sname = I["gall"][jj][r * 288:(r + 1) * 288, :], f"gall{jj}"
            else:
                src, sname = I["gctx"], "gctx"
            c0 = 0
            P.op("sync", lambda e, kvnT=kvnT, krT=krT, src=src, c0=c0, ntok=ntok: [
                e.dma_start(out=kvnT[:, :, :ntok], in_=src[0:256, c0:c0 + ntok].rearrange("(c p) t -> p c t", p=128)),
                e.dma_start(out=krT[0:32, :ntok], in_=src[256:288, c0:c0 + ntok])], [sname], [f"kvn{s}"], chan=f"kvn{s}", ndma=2)

        def compute(gi):
            nb, ntok, tok0, s, kvnT, krT = geom(gi)
            self.dma("sync", I["KR1"][:, tok0:tok0 + ntok], krT[0:32, :ntok], [f"kvn{s}"], [], "kr_st")
            vs = gi % 2

            def kchunk(cch):
                b = self.rot("k1bank", 2)
                pk_ = self.bank[1 + b]
                for k in range(2):
                    self.mm(pk_[:, :ntok], wk1[:, k, cch * 128:(cch + 1) * 128], kvnT[:, k, :ntok], k == 0, k == 1, [f"kvn{s}", "wbig"], [f"ps{1 + b}"])
                ks = self.rot("kst", 4)
                kst = self.v_mid(12288 + ks * 512, [512])
                self.cp("vector", kst[:, :ntok], pk_[:, :ntok], [f"ps{1 + b}"], [f"kst{ks}"])
                self.dma("sync", I["KT1"][cch, :, tok0:tok0 + ntok], kst[:, :ntok], [f"kst{ks}"], [], f"kt_st{self.rot('ktst', 4)}")

            def vpart(j, hh):
                pv = self.bank[5 + hh]
                for k in range(2):
                    self.mm(pv, kvnT[:, k, j * 128:(j + 1) * 128], wv1[:, k, hh * 512:(hh + 1) * 512], k == 0, k == 1, [f"kvn{s}", "wbig"], [f"ps{5 + hh}"])
                self.act(vst[vs][:, hh * 8:(hh + 1) * 8, j, 0:64], pv.rearrange("p (h d) -> p h d", d=64), AF.Copy, [f"ps{5 + hh}"], [f"vst{vs}"])

            vlist = [(j, hh) for j in range(nb) for hh in range(2)]
            for i in range(8):
                kchunk(i)
                if i < len(vlist):
                    vpart(*vlist[i])
            b0 = gi * 4
            self.dma("sync", I["V1"][:, :, b0:b0 + nb, :].rearrange("h p b d -> p h b d"), vst[vs][:, :, 0:nb, :], [f"vst{vs}"], [], f"vb_st{vs}")

        load(0)
        for gi in range(ngroups):
            if gi + 1 < ngroups:
                load(gi + 1)
            compute(gi)

    def phase_attC(self, I):
        P = self.P
        P.barrier()
        self.precast_ffn(1, I["w13t1"], I["w2t1"])
        osb = self.v_big(0, [18, D])
        self.osb = osb

        def make_fin(h):
            def fin(qb0, nqb, per, ob):
                for bk in range((nqb + per - 1) // per):
                    n = min(per, nqb - bk * per)
                    pv = self.bank[ob + bk][:, 0:n * 65].rearrange("p (q d) -> p q d", d=65)
                    pb = f"ps{ob + bk}"
                    r = self.rot("rsC", 2)
                    rs = self.v_sm(1280 + r * 16, [7], F32)[:, 0:n]
                    self.P.op("vector", lambda e, rs=rs, pv=pv: e.reciprocal(rs, pv[:, :, 64]), [pb], [f"rsC{r}"])
                    q0 = qb0 + bk * per
                    self.tt("vector", osb[:, q0:q0 + n, h * 64:(h + 1) * 64], pv[:, :, 0:64], rs.unsqueeze(2).to_broadcast([128, n, 64]),
                            ALU.mult, [pb, f"rsC{r}"], [f"osb{q0 + i}" for i in range(n)])
            return fin

        for h in range(16):
            for qg in [(0, 8, 0, 66), (8, 8, 0, 66)]:
                self.attn_pass(96, 0,
                               lambda q0, nq, h=h: I["QT1"][h, :, q0:q0 + nq],
                               [(0, 64, lambda c0, ncl, h=h: I["KT1"][h // 2, (h % 2) * 64:(h % 2) * 64 + 64, c0:c0 + ncl]),
                                (64, 32, lambda c0, ncl: I["KR1"][:, c0:c0 + ncl])],
                               lambda kb0, n, h=h: I["V1"][h, :, kb0:kb0 + n, :],
                               65, 96 ** -0.5, [qg], make_fin(h), "C")


import numpy as np

D = 1024; NT = 8192; OWN = 2048; CTX = 256
GRID_W = 64; THETA = 10000.0


def rope_tabs_fm(rot_dim, positions):
    axis_dim = rot_dim // 2
    q = rot_dim // 4
    inv = THETA ** (-np.arange(0, axis_dim, 2, dtype=np.float32) / axis_dim)
    row = (positions // GRID_W).astype(np.float32)
    col = (positions % GRID_W).astype(np.float32)
    ar = row[None, :] * inv[:, None]
    ac = col[None, :] * inv[:, None]
    ar = ar.astype(np.float32); ac = ac.astype(np.float32)
    C = np.concatenate([np.cos(ar), np.cos(ar), np.cos(ac), np.cos(ac)], 0)
    S = np.concatenate([-np.sin(ar), np.sin(ar), -np.sin(ac), np.sin(ac)], 0)
    return C.astype(np.float32), S.astype(np.float32)


def rope_perm(rot_dim):
    q = rot_dim // 4
    return np.concatenate([np.arange(q, 2 * q), np.arange(0, q), np.arange(3 * q, 4 * q), np.arange(2 * q, 3 * q)])


def pk(w):
    K = w.shape[0] // 128
    return np.ascontiguousarray(w.reshape(K, 128, w.shape[1]).transpose(1, 0, 2))


def prep_l0(inp, core):
    b, qc = core // 4, core % 4
    roll = qc * OWN
    f32 = np.float32
    m = {}
    m["xk"] = np.ascontiguousarray(np.roll(inp["x"][b], -roll, axis=0))
    m["ctxin"] = np.ascontiguousarray(inp["ctx"][b])
    pos = (np.arange(NT) + roll) % NT
    C, S = rope_tabs_fm(64, pos)
    C = np.concatenate([C, np.ones((64, CTX), f32)], 1)
    S = np.concatenate([S, np.zeros((64, CTX), f32)], 1)
    m["ropeC0"] = np.ascontiguousarray(np.concatenate([C, C], 0))
    m["ropeS0"] = np.ascontiguousarray(np.concatenate([S, S], 0))
    w = inp["ab_w_in"][0]
    p64 = rope_perm(64)
    def permcols(cols):
        return np.concatenate([c0 + p64 for c0 in cols])
    qa_cols = []
    for c in range(4):
        qa_cols += [64 * c, 64 * (4 + c)]
    qb_cols = [512 + 64 * i for i in range(8)]
    qcols = qa_cols + qb_cols
    kcols = [1024, 1088] + [1280 + 64 * i for i in range(8)]
    nat = lambda cols: np.concatenate([np.arange(c0, c0 + 64) for c0 in cols])
    m["w0q_f"] = pk(w[:, nat(qcols)])
    m["w0k_f"] = pk(w[:, nat(kcols)])
    pm64 = np.zeros((128, 128), f32)
    for hh in range(2):
        for d_ in range(64):
            pm64[hh * 64 + p64[d_], hh * 64 + d_] = 1.0
    m["perm64"] = pm64
    p32_ = rope_perm(32)
    pm96 = np.zeros((128, 128), f32)
    for d_ in range(64):
        pm96[d_, d_] = 1.0
    for d_ in range(32):
        pm96[64 + p32_[d_], 64 + d_] = 1.0
    m["perm96"] = pm96
    m["w0v"] = pk(np.concatenate([w[:, 1152:1280], w[:, 1792:2304]], 1))
    cf = np.stack([inp["c"][b].reshape(8, 128).T, inp["c_ctx"].reshape(8, 128).T], -1)
    m["cfm"] = np.ascontiguousarray(cf.astype(f32))
    for l in range(2):
        m[f"adaw{l}"] = np.ascontiguousarray(inp["ada_w"][l].reshape(8, 128, 6, D).transpose(2, 1, 0, 3))
        m[f"adab_fm{l}"] = np.ascontiguousarray(inp["ada_b"][l].reshape(48, 128).T)
        m[f"adab_row{l}"] = np.ascontiguousarray(inp["ada_b"][l])
        m[f"ng_fm{l}"] = np.ascontiguousarray(inp["norm_g"][l].reshape(4, 8, 128).transpose(2, 0, 1))
        m[f"ng_row{l}"] = np.ascontiguousarray(inp["norm_g"][l])
    return m


NEG = -30000.0


def prep_l0_more(inp, core, m):
    b, qc = core // 4, core % 4
    f32 = np.float32
    j = np.arange(128)[:, None]
    qi = np.arange(128)[None, :]
    prev = np.where(j >= qi, 0.0, NEG).astype(f32)
    nxt = np.where(j <= qi, 0.0, NEG).astype(f32)
    allneg = np.full((128, 128), NEG, f32)
    masks = [allneg if qc == 0 else prev, prev, nxt, allneg if qc == 3 else nxt]
    m["amask"] = np.ascontiguousarray(np.stack([np.tile(x, (1, 4)) for x in masks], 1))
    m["sink"] = np.ascontiguousarray(inp["ab_sink"][0])
    m["dlam"] = np.ascontiguousarray(inp["diff_lambda"][0].reshape(256))
    m["subg"] = np.ascontiguousarray(inp["diff_subln_g"][0])
    m["wout0"] = pk(inp["ab_w_out"][0])
    for l in (0,):
        ffn_pack(inp, l, m)
    return m


def ffn_pack(inp, l, m):
    w13 = inp["ffn_w13"][l]
    g = w13[:, :2816].reshape(8, 128, 22, 128)
    u = w13[:, 2816:].reshape(8, 128, 22, 128)
    t = np.concatenate([g, u], -1)
    m[f"w13t{l}"] = np.ascontiguousarray(t.transpose(2, 1, 0, 3))
    m[f"w2t{l}"] = pk(inp["ffn_w2"][l])


def prep_l1(inp, core, m):
    b, qc = core // 4, core % 4
    f32 = np.float32
    roll = qc * OWN
    pos = (np.arange(OWN) + roll) % NT
    C32, S32 = rope_tabs_fm(32, pos)
    p32 = rope_perm(32)
    w = inp["mla_w_in"][0]
    m["w1in"] = pk(np.concatenate([w[:, 384:640], w[:, 640:672], w[:, 640 + p32], w[:, 0:384]], 1))
    wq = inp["mla_wq_b"][0]
    permc = np.concatenate([np.concatenate([h * 96 + np.arange(64), h * 96 + 64 + p32]) for h in range(16)])
    m["wq1f"] = pk(wq)
    m["rope1C"] = np.ascontiguousarray(np.concatenate([np.ones((64, OWN), f32), C32], 0))
    m["rope1S"] = np.ascontiguousarray(np.concatenate([np.zeros((64, OWN), f32), S32], 0))
    m["kr1C"] = np.ascontiguousarray(np.concatenate([C32, np.ones((32, CTX), f32)], 1))
    m["kr1S"] = np.ascontiguousarray(np.concatenate([S32, np.zeros((32, CTX), f32)], 1))
    m["mla_ng"] = np.ascontiguousarray(np.concatenate([inp["mla_kv_norm_g"][0].reshape(2, 128).T,
                                                       inp["mla_q_norm_g"][0].reshape(3, 128).T], 1).astype(f32))
    wkv = inp["mla_wkv_b"][0].reshape(256, 16, 128)
    m["wk1"] = pk(np.ascontiguousarray(wkv[:, :, :64]).reshape(256, 1024))
    m["wv1"] = pk(np.ascontiguousarray(wkv[:, :, 64:]).reshape(256, 1024))
    m["wout1"] = pk(inp["mla_w_out"][0])
    ffn_pack(inp, 1, m)
    return m


import ml_dtypes

L0_KEYS = ["xk", "ctxin", "ropeC0", "ropeS0", "w0q_f", "w0k_f", "w0v", "cfm", "perm64", "perm96",
           "adaw0", "adab_fm0", "adab_row0", "ng_fm0", "ng_row0", "adaw1", "adab_fm1", "adab_row1", "ng_fm1", "ng_row1",
           "amask", "sink", "dlam", "subg", "wout0", "w13t0", "w2t0",
           "w1in", "wq1f", "rope1C", "rope1S", "kr1C", "kr1S", "mla_ng"]
L1_KEYS = ["cfm", "adaw1", "adab_fm1", "adab_row1", "ng_fm1", "ng_row1", "wk1", "wv1", "wout1", "w13t1", "w2t1"]


def layer0_body(kb, I):
    kb.phase_mod(0, I["cfm"], I["adaw0"], I["adab_fm0"], I["adab_row0"], I["ng_fm0"], I["ng_row0"])
    kb.phase_kv0(I)
    kb.phase_q0(I)
    kb.phase_attA(I)
    kb.phase_attB(I)
    kb.phase_outproj(0, I["wout0"])
    kb.phase_ffn(0, I["w13t0"], I["w2t0"])
    kb.phase_mod(1, I["cfm"], I["adaw1"], I["adab_fm1"], I["adab_row1"], I["ng_fm1"], I["ng_row1"])
    kb.phase_mla_pre(I)


def layer1_body(kb, I, out):
    kb.phase_kv1(I)
    kb.phase_attC(I)
    kb.phase_outproj(1, I["wout1"])
    kb.phase_ffn(1, I["w13t1"], I["w2t1"], out_ap=out)


def scratch_l0(kb, I):
    I["KT0"] = kb.dint("KT0", [5, 128, NK], BF16)
    I["VB0"] = kb.dint("VB0", [4, 128, NKB, 129], BF16)
    I["QT0"] = kb.dint("QT0", [8, 128, NQ], BF16)


def scratch_l1(kb, I):
    I["KT1"] = kb.dint("KT1", [8, 128, NK], BF16)
    I["KR1"] = kb.dint("KR1", [32, NK], BF16)
    I["V1"] = kb.dint("V1", [16, 128, NKB, 65], BF16)


def build_fused(m):
    kb = KB("FUSED")
    keys = L0_KEYS + [k for k in L1_KEYS if k not in L0_KEYS]
    I = {k: kb.din(k, m[k].shape) for k in keys}
    scratch_l0(kb, I)
    scratch_l1(kb, I)
    I["gown"] = [kb.dint(f"gown{j}", [288, 512], BF16) for j in range(4)]
    I["gall"] = [kb.dint(f"gall{j}", [4 * 288, 512], BF16) for j in range(4)]
    I["gctx"] = kb.dint("gctx", [288, CTX], BF16)
    I["QT1"] = kb.dint("QT1", [16, 96, OWN], BF16)
    out = kb.dout("out", [OWN, D])
    kb.setup_common(); kb.setup_l0()
    kb.load_perms(I)
    layer0_body(kb, I)
    layer1_body(kb, I, out)
    kb.P.emit(kb.finals)
    return kb, keys


def prep_all(inputs):
    inp = {k: np.asarray(v) for k, v in inputs.items()}
    maps = []
    for core in range(8):
        m = prep_l0(inp, core)
        prep_l0_more(inp, core, m)
        prep_l1(inp, core, m)
        maps.append(m)
    return maps


def kernel(**inputs):
    maps = prep_all(inputs)
    kb, keys = build_fused(maps[0])
    res = run_bass_kernel_spmd(kb.nc, [{k: m[k] for k in keys} for m in maps], core_ids=list(range(8)))
    out = np.zeros((2, NT, D), np.float32)
    for core in range(8):
        b, qc = core // 4, core % 4
        out[b, qc * OWN:(qc + 1) * OWN] = np.asarray(res.results[core]["out"])
    return out
```

```python
import numpy as np
import concourse.bass as bass
import concourse.mybir as mybir
from concourse.alu_op_type import AluOpType as ALU
from concourse.bass_utils import run_bass_kernel_spmd

AF = mybir.ActivationFunctionType
AX = mybir.AxisListType
F32 = mybir.dt.float32
BF16 = mybir.dt.bfloat16


class Buf:
    __slots__ = ("name", "last_w", "readers")

    def __init__(self, name):
        self.name = name
        self.last_w = None
        self.readers = []


class Op:
    __slots__ = ("eng", "fn", "deps", "needed", "sem", "val", "chan", "ndma", "idx", "cinc")

    def __init__(self, eng, fn, chan=None):
        self.eng = eng
        self.fn = fn
        self.deps = []
        self.needed = False
        self.sem = None
        self.val = None
        self.chan = chan
        self.ndma = 0
        self.idx = -1
        self.cinc = 16


ENGS = ("tensor", "scalar", "vector", "gpsimd", "sync")


class Prog:
    def __init__(self, nc):
        self.nc = nc
        self.ops = {e: [] for e in ENGS}
        self.nops = 0
        self.chan_last = {}
        self.bufs = {}
        self.bar = []
        self.bar_pending = set()

    def barrier(self):
        b = [self.ops[e][-1] for e in ENGS if self.ops[e]]
        b += list(self.chan_last.values())
        self.bar = b
        self.bar_pending = set(ENGS)

    def buf(self, name):
        b = self.bufs.get(name)
        if b is None:
            b = Buf(name)
            self.bufs[name] = b
        return b

    def _B(self, x):
        return x if isinstance(x, Buf) else self.buf(x)

    def op(self, eng, fn, reads=(), writes=(), chan=None, ndma=1, cinc=16):
        o = Op(eng, fn, chan)
        o.idx = self.nops
        self.nops += 1
        if eng in self.bar_pending:
            self.bar_pending.discard(eng)
            for p in self.bar:
                self._dep(o, p)
        if chan is not None:
            o.ndma = ndma
            o.cinc = cinc
            prev = self.chan_last.get(chan)
            if prev is not None:
                self._dep(o, prev)
            self.chan_last[chan] = o
        for r in reads:
            b = self._B(r)
            if b.last_w is not None:
                self._dep(o, b.last_w)
            b.readers.append(o)
        for w in writes:
            b = self._B(w)
            if b.last_w is not None:
                self._dep(o, b.last_w)
            for rd in b.readers:
                if rd is not o:
                    self._dep(o, rd)
            b.last_w = o
            b.readers = []
        self.ops[eng].append(o)
        return o

    def _dep(self, o, p):
        if p is o:
            return
        if p.eng == "tensor" and o.eng == "tensor" and p.chan is None and o.chan is None:
            return
        for d in o.deps:
            if d is p:
                return
        o.deps.append(p)
        p.needed = True

    def emit(self, final_ops=()):
        nc = self.nc
        for o in final_ops:
            o.needed = True
        import contextlib
        with contextlib.ExitStack() as st:
            esem = {e: st.enter_context(nc.semaphore("s_" + e)) for e in ENGS}
            chans = sorted({o.chan for e in ENGS for o in self.ops[e] if o.chan is not None})
            csem = {c: st.enter_context(nc.semaphore("c_" + c)) for c in chans}
            self.nsem = len(esem) + len(csem)
            ccount = {c: 0 for c in chans}
            allops = sorted((o for e in ENGS for o in self.ops[e]), key=lambda o: o.idx)
            ecount = {e: 0 for e in ENGS}
            for o in allops:
                if o.chan is not None:
                    ccount[o.chan] += o.cinc * o.ndma
                    o.sem, o.val = csem[o.chan], ccount[o.chan]
                elif o.needed:
                    ecount[o.eng] += 1
                    o.sem, o.val = esem[o.eng], ecount[o.eng]
            self.ecount = ecount
            block = st.enter_context(nc.Block())

            def run(ename, e):
                seen = {}
                for o in self.ops[ename]:
                    for d in o.deps:
                        k = id(d.sem)
                        if seen.get(k, 0) >= d.val:
                            continue
                        e.wait_ge(d.sem, d.val)
                        seen[k] = d.val
                    r = o.fn(e)
                    if o.chan is not None:
                        rs = r if isinstance(r, (list, tuple)) else [r]
                        assert len(rs) == o.ndma, (len(rs), o.ndma)
                        for ins in rs:
                            if o.cinc == 1:
                                ins.then_inc(o.sem)
                            else:
                                ins.then_inc(o.sem, o.cinc)
                    elif o.needed:
                        r.then_inc(o.sem, 1)
                if ename == "sync":
                    for o in final_ops:
                        e.wait_ge(o.sem, o.val)

            @block.tensor
            def _(e):
                run("tensor", e)

            @block.scalar
            def _(e):
                run("scalar", e)

            @block.vector
            def _(e):
                run("vector", e)

            @block.gpsimd
            def _(e):
                run("gpsimd", e)

            @block.sync
            def _(e):
                run("sync", e)


import contextlib
import math
import numpy as np

D = 1024
NT = 8192
OWN = 2048
CTX = 256
NQ = OWN + CTX
NK = NT + CTX
NKB = NK // 128
FFH = 2816
EPS = 1e-6
LAM0 = 0.8 - 0.6 * math.exp(-0.3 * 0)


class KB:
    def __init__(self, mode, dbg=()):
        self.mode = mode
        self.dbg = set(dbg)
        self.nc = bass.Bass("TRN2", target_bir_lowering=False)
        self.P = Prog(self.nc)
        self.st = contextlib.ExitStack()
        self.finals = []
        self.cnt = {}

    def din(self, name, shape, dt=F32):
        return self.nc.dram_tensor(name, list(shape), dt, kind="ExternalInput").ap()

    def dout(self, name, shape, dt=F32):
        return self.nc.dram_tensor(name, list(shape), dt, kind="ExternalOutput").ap()

    def dint(self, name, shape, dt=BF16):
        return self.nc.dram_tensor(name, list(shape), dt, kind="Internal").ap()

    def sb(self, name, shape, dt):
        return self.st.enter_context(self.nc.sbuf_tensor(name, list(shape), dt))

    def rot(self, key, n):
        v = self.cnt.get(key, 0)
        self.cnt[key] = v + 1
        return v % n

    def dma(self, q, out, in_, reads, writes, chan, **kw):
        return self.P.op(q, lambda e: e.dma_start(out=out, in_=in_, **kw), reads, writes, chan=chan)

    def mm(self, out, lhsT, rhs, start, stop, reads, writes):
        return self.P.op("tensor", lambda e: e.matmul(out, lhsT=lhsT, rhs=rhs, start=start, stop=stop,
                                                      skip_group_check=True), reads, writes)

    def tr(self, out, in_, reads, writes):
        ident = self.ident
        return self.P.op("tensor", lambda e: e.transpose(out, in_, ident[:]), list(reads) + ["ident"], writes)

    def act(self, out, in_, func, reads, writes, **kw):
        return self.P.op("scalar", lambda e: e.activation(out=out, in_=in_, func=func, **kw), reads, writes)

    def tt(self, eng, out, in0, in1, op, reads, writes):
        return self.P.op(eng, lambda e: e.tensor_tensor(out=out, in0=in0, in1=in1, op=op), reads, writes)

    def ts(self, eng, out, in0, s1, s2, op0, op1, reads, writes, **kw):
        if op1 is None:
            return self.P.op(eng, lambda e: e.tensor_scalar(out=out, in0=in0, scalar1=s1, scalar2=None, op0=op0, **kw), reads, writes)
        return self.P.op(eng, lambda e: e.tensor_scalar(out=out, in0=in0, scalar1=s1, scalar2=s2, op0=op0, op1=op1, **kw), reads, writes)

    def cp(self, eng, out, in_, reads, writes):
        return self.P.op(eng, lambda e: e.tensor_copy(out=out, in_=in_), reads, writes)

    def debug_out(self, name, src_ap, shape, reads, dt=F32):
        if name not in self.dbg:
            return
        o = self.dout("dbg_" + name, shape, dt)
        f = self.dma("sync", o, src_ap, reads, [], "dbg_" + name)
        self.finals.append(f)

    def setup_common(self):
        nc, P = self.nc, self.P
        self.psum = self.st.enter_context(nc.psum_tensor("psum", [128, 4096], F32))
        self.bank = [self.psum[:, i * 512:(i + 1) * 512] for i in range(8)]
        self.ident = self.sb("ident", [128, 128], BF16)
        ident = self.ident
        P.op("gpsimd", lambda e: e.memset(ident[:], 0.0), [], ["ident"])
        P.op("gpsimd", lambda e: e.affine_select(out=ident[:], in_=ident[:], pattern=[[-1, 128]],
                                                 compare_op=ALU.not_equal, fill=1.0, base=0,
                                                 channel_multiplier=1), ["ident"], ["ident"])
        self.xres = self.sb("xres", [128, 18, D], F32)
        self.big = self.sb("big", [128, 23040], BF16)
        self.mid = self.sb("mid", [128, 16896], BF16)
        self.sm = self.sb("sm", [128, 9216], BF16)
        self.aux = self.sb("aux", [128, 6144], BF16)
        self.gates = self.sb("gates", [128, 4, D], F32)
        self.modfm = self.sb("modfm", [128, 2, 8, 8], F32)
        self.ssq = self.sb("ssq", [128, 8], F32)
        self.rstd = self.sb("rstd", [128, 8], F32)
        self.sqj = self.sb("sqj", [128, D], BF16)
        self.perm64 = self.sb("perm64_sb", [128, 128], BF16)
        self.perm96 = self.sb("perm96_sb", [128, 128], BF16)
        self.epsb = self.sb("epsb", [128, 1], F32)
        epsb = self.epsb
        P.op("gpsimd", lambda e: e.memset(epsb[:], EPS), [], ["epsb"])

    def v_big(self, off, shape, dt=BF16):
        n = int(np.prod(shape))
        if dt == F32:
            ap = self.big[:, off:off + 2 * n].bitcast(F32)
        else:
            ap = self.big[:, off:off + n]
        return self._shape(ap, shape)

    def v_mid(self, off, shape, dt=BF16):
        n = int(np.prod(shape))
        if dt == F32:
            ap = self.mid[:, off:off + 2 * n].bitcast(F32)
        else:
            ap = self.mid[:, off:off + n]
        return self._shape(ap, shape)

    def v_sm(self, off, shape, dt=BF16):
        n = int(np.prod(shape))
        if dt == F32:
            ap = self.sm[:, off:off + 2 * n].bitcast(F32)
        else:
            ap = self.sm[:, off:off + n]
        return self._shape(ap, shape)

    @staticmethod
    def _shape(ap, shape):
        if len(shape) == 1:
            return ap
        if len(shape) == 2:
            return ap.rearrange("p (a b) -> p a b", b=shape[1])
        if len(shape) == 3:
            return ap.rearrange("p (a b c) -> p a b c", b=shape[1], c=shape[2])
        raise ValueError

    def phase_mod(self, layer, cfm, adaw, adab_fm, adab_row, ng_fm, ng_row):
        P = self.P
        P.barrier()
        bk = self.bank[7]
        sc = self.v_sm(0, [8, 2], F32)
        scb = self.v_sm(64, [8, 2], BF16)
        scbc = self.v_sm(128, [2, 8, 128], BF16)
        bfm = self.v_sm(2304, [48], F32)
        gfm = self.v_sm(2304 + 96, [4, 8], F32)
        modT = self.v_sm(2304 + 96 + 64, [48, 2], F32)
        brow = self.v_big(0, [2, D], F32)
        grow = self.v_big(4096, [2, D], F32)
        self.dma("sync", sc, cfm, [], ["m_sc"], "m_sc")
        self.dma("sync", bfm, adab_fm, [], ["m_bfm"], "m_bfm")
        self.dma("sync", gfm, ng_fm, [], ["m_gfm"], "m_gfm")
        self.P.op("sync", lambda e: [e.dma_start(out=brow[:, 0, :], in_=adab_row[2 * D:3 * D].partition_broadcast(128)),
                                     e.dma_start(out=brow[:, 1, :], in_=adab_row[5 * D:6 * D].partition_broadcast(128)),
                                     e.dma_start(out=grow[:, 0, :], in_=ng_row[1, :].partition_broadcast(128)),
                                     e.dma_start(out=grow[:, 1, :], in_=ng_row[3, :].partition_broadcast(128))],
                  [], ["m_rows"], chan="m_rows", ndma=4)
        self.act(sc, sc, AF.Silu, ["m_sc"], ["m_sc"])
        self.cp("vector", scb, sc, ["m_sc"], ["m_scb"])
        for j in range(2):
            self.cp("vector", scbc[:, j], sc[:, :, j:j + 1].to_broadcast([128, 8, 128]), ["m_sc"], ["m_scbc"])
        psT = bk[:, 0:96].rearrange("p (a b) -> p a b", b=2)
        for j in range(6):
            s = j % 2
            wt = self.v_mid(s * 8192, [8, D])
            self.dma("gpsimd", wt, adaw[j], [], [f"m_w{s}"], f"m_w{s}")
            for cc in range(8):
                for k in range(8):
                    self.mm(psT[:, j * 8 + cc, :], wt[:, k, cc * 128:(cc + 1) * 128], scb[:, k, :], k == 0,
                            k == 7, [f"m_w{s}", "m_scb"], ["ps7"])
            if j in (2, 5):
                gi = 0 if j == 2 else 1
                for t in range(2):
                    for hh in range(2):
                        pb = self.bank[5 + hh]
                        for k in range(8):
                            self.mm(pb, scbc[:, t, k, :], wt[:, k, hh * 512:(hh + 1) * 512], k == 0, k == 7,
                                    [f"m_w{s}", "m_scbc"], [f"ps{5 + hh}"])
                        g = self.gates[:, 2 * t + gi, hh * 512:(hh + 1) * 512]
                        self.tt("vector", g, pb, brow[:, gi, hh * 512:(hh + 1) * 512], ALU.add, [f"ps{5 + hh}", "m_rows"], ["gates"])
                        self.tt("vector", g, g, grow[:, gi, hh * 512:(hh + 1) * 512], ALU.mult, ["gates", "m_rows"], ["gates"])
        self.tt("vector", modT, psT, bfm.unsqueeze(2).to_broadcast([128, 48, 2]), ALU.add, ["ps7", "m_bfm"], ["m_modT"])
        mf = self.modfm
        for t in range(2):
            self.P.op("vector", lambda e, t=t: e.scalar_tensor_tensor(out=mf[:, layer, 4 * t + 0, :], in0=modT[:, 8:16, t], scalar=1.0,
                                                                      in1=gfm[:, 0, :], op0=ALU.add, op1=ALU.mult),
                      ["m_modT", "m_gfm"], ["modfm"])
            self.cp("vector", mf[:, layer, 4 * t + 1, :], modT[:, 0:8, t], ["m_modT"], ["modfm"])
            self.P.op("vector", lambda e, t=t: e.scalar_tensor_tensor(out=mf[:, layer, 4 * t + 2, :], in0=modT[:, 32:40, t], scalar=1.0,
                                                                      in1=gfm[:, 2, :], op0=ALU.add, op1=ALU.mult),
                      ["m_modT", "m_gfm"], ["modfm"])
            self.cp("vector", mf[:, layer, 4 * t + 3, :], modT[:, 24:32, t], ["m_modT"], ["modfm"])
        self.debug_out(f"modfm{layer}", mf[:, layer], [128, 8, 8], ["modfm"])
        self.debug_out(f"gates{layer}", self.gates[:], [128, 4, D], ["gates"])

    def norm_a(self, xsrc, xbuf, nslot=2):
        c = self.rot("ssq", 8)
        xs = self.rot(f"xn{nslot}", nslot)
        xn = self.v_sm(4096 + xs * 1024, [D])
        ssq = self.ssq[:, c:c + 1]
        rstd = self.rstd[:, c:c + 1]
        self.act(self.sqj[:], xsrc, AF.Square, [xbuf], ["sqj", f"ssq{c}"], accum_out=ssq)
        epsb = self.epsb
        self.act(rstd, ssq, AF.Sqrt, [f"ssq{c}", "epsb"], [f"rstd{c}"], scale=1.0 / D, bias=epsb[:])
        self.P.op("vector", lambda e: e.reciprocal(rstd, rstd), [f"rstd{c}"], [f"rstd{c}"])
        self.act(xn, xsrc, AF.Copy, [xbuf, f"rstd{c}"], [f"xn{xs}"], scale=rstd)
        return xn, f"xn{xs}"

    def norm_b(self, xn, xnbuf, dst, dstbuf, layer, t, stage):
        pT = self.bank[0].bitcast(BF16).rearrange("p (k t) -> p k t", t=128)
        for k in range(8):
            self.tr(pT[:, k, :], xn[:, k * 128:(k + 1) * 128], [xnbuf], ["ps0"])
        Aap = self.modfm[:, layer, 4 * t + 2 * stage, :]
        Bap = self.modfm[:, layer, 4 * t + 2 * stage + 1, :]
        self.tt("vector", dst, pT, Aap.unsqueeze(2).to_broadcast([128, 8, 128]), ALU.mult, ["ps0", "modfm"], [dstbuf])
        self.tt("vector", dst, dst, Bap.unsqueeze(2).to_broadcast([128, 8, 128]), ALU.add, [dstbuf, "modfm"], [dstbuf])

    def norm_T(self, xsrc, xbuf, dst, dstbuf, layer, t, stage):
        xn, xnbuf = self.norm_a(xsrc, xbuf)
        self.norm_b(xn, xnbuf, dst, dstbuf, layer, t, stage)

    A_SLOT = {**{b: b for b in range(17)}, 63: 17, 64: 18, 65: 19}

    def setup_l0(self):
        self.kAT = self.aux[:, 0:2560].rearrange("p (s t) -> p s t", t=128)
        self.vA = self.aux[:, 2560:5160].rearrange("p (s g d) -> p s g d", g=2, d=65)
        vA = self.vA
        self.P.op("gpsimd", lambda e: e.memset(vA[:, :, :, 64:65], 1.0), [], ["vA"])

    def load_perms(self, I):
        p64, p96 = self.perm64, self.perm96
        self.P.op("gpsimd", lambda e: [e.dma_start(out=p64[:], in_=I["perm64"]), e.dma_start(out=p96[:], in_=I["perm96"])],
                  [], ["permM"], chan="permM", ndma=2)

    def load_w(self, dst, src, kchunks, name="wbig"):
        return self.dma("gpsimd", dst, src, [], [name], name)

    def fm_x(self, ntok, hxT, hbuf, wf, col0, M, wname):
        b = self.rot("fmbank", 2)
        pm = self.bank[1 + b]
        K = hxT.shape[1]
        for k in range(K):
            self.mm(pm[0:M, :ntok], wf[:, k, col0:col0 + M], hxT[:, k, :ntok], k == 0, k == K - 1, [hbuf, wname], [f"ps{1 + b}"])
        kr = self.rot("kraw", 2)
        kraw = self.v_sm(8192 + kr * 512, [512])
        self.act(kraw[0:M, :ntok], pm[0:M, :ntok], AF.Copy, [f"ps{1 + b}"], [f"kraw{kr}"])
        return (b, kr, kraw)

    def fm_y(self, st, ntok, M, tabC, tabS, tbuf, perm=None):
        b, kr, kraw = st
        pm, pp = self.bank[1 + b], self.bank[3 + b]
        pmat = self.perm64 if perm is None else perm
        self.mm(pp[0:M, :ntok], pmat[0:M, 0:M], kraw[0:M, :ntok], True, True, [f"kraw{kr}", "permM"], [f"ps{3 + b}"])
        t1 = self.v_big(20512, [512], F32)
        t2 = self.v_big(21536, [512], F32)
        self.tt("vector", t1[0:M, :ntok], pm[0:M, :ntok], tabC[0:M, :ntok], ALU.mult, [f"ps{1 + b}", tbuf, f"kraw{kr}"], ["t1"])
        self.tt("vector", t2[0:M, :ntok], pp[0:M, :ntok], tabS[0:M, :ntok], ALU.mult, [f"ps{3 + b}", tbuf], ["t2"])
        ks = self.rot("kst", 4)
        kst = self.v_mid(12288 + ks * 512, [512])
        self.tt("gpsimd", kst[0:M, :ntok], t1[0:M, :ntok], t2[0:M, :ntok], ALU.add, ["t1", "t2"], [f"kst{ks}"])
        return kst, f"kst{ks}"

    def fm_rope_loop(self, chunks, ntok, hxT, hbuf, wf, M, tabC, tabS, tbuf, wname, perm, sink):
        if not chunks:
            return
        sts = {0: self.fm_x(ntok, hxT, hbuf, wf, chunks[0][1], M, wname)}
        for i, (cid, col0) in enumerate(chunks):
            if i + 1 < len(chunks):
                sts[i + 1] = self.fm_x(ntok, hxT, hbuf, wf, chunks[i + 1][1], M, wname)
            kst, kb = self.fm_y(sts.pop(i), ntok, M, tabC, tabS, tbuf, perm)
            sink(cid, kst, kb)

    def phase_kv0(self, I):
        P = self.P
        P.barrier()
        wkf = self.v_big(0, [8, 640])
        wkp = self.v_big(5120, [8, 640])
        wv = self.v_big(10240, [8, 640])
        self.load_w(wkf, I["w0k_f"], 8)
        self.load_w(wv, I["w0v"], 8)
        vst = [self.v_big(16384 + s * 2064, [4, 4, 129]) for s in range(2)]
        for s in range(2):
            P.op("gpsimd", lambda e, s=s: e.memset(vst[s][:, :, :, 128:129], 1.0), [], [f"vst{s}"])
        groups = [list(range(4 * g, 4 * g + 4)) for g in range(16)] + [[64, 65]]
        def geom(gi):
            blocks = groups[gi]
            nb = len(blocks)
            ntok = 128 * nb
            tok0 = blocks[0] * 128
            hs = gi % 2
            hxT = self.v_mid(hs * 4096, [8, 512])
            tabC = self.v_mid(8192 + hs * 2048, [512], F32)
            tabS = self.v_mid(8192 + hs * 2048 + 1024, [512], F32)
            return blocks, nb, ntok, tok0, hs, hxT, tabC, tabS

        xns = {}

        def part_norm(gi):
            blocks, nb, ntok, tok0, hs, hxT, tabC, tabS = geom(gi)
            P.op("sync", lambda e, tabC=tabC, tabS=tabS, tok0=tok0, ntok=ntok: [
                e.dma_start(out=tabC[:, :ntok], in_=I["ropeC0"][:, tok0:tok0 + ntok]),
                e.dma_start(out=tabS[:, :ntok], in_=I["ropeS0"][:, tok0:tok0 + ntok])], [], [f"tab{hs}"], chan=f"tab{hs}", ndma=2)
            for j, blk in enumerate(blocks):
                if blk < 16:
                    xs, xb = self.xres[:, blk, :], f"xres{blk}"
                    self.dma("sync", xs, I["xk"][blk * 128:(blk + 1) * 128, :], [], [xb], f"xres{blk % 2}")
                elif blk >= 64:
                    xs, xb = self.xres[:, 16 + blk - 64, :], f"xres{16 + blk - 64}"
                    self.dma("sync", xs, I["ctxin"][(blk - 64) * 128:(blk - 63) * 128, :], [], [xb], f"xres{blk % 2}")
                else:
                    s = self.rot("xin", 2)
                    xs, xb = self.v_sm(s * 2048, [D], F32), f"xin{s}"
                    self.dma("sync", xs, I["xk"][blk * 128:(blk + 1) * 128, :], [], [xb], f"xin{s}")
                xns[(gi, j)] = self.norm_a(xs, xb, 4)

        def part_b(gi):
            blocks, nb, ntok, tok0, hs, hxT, tabC, tabS = geom(gi)
            for j, blk in enumerate(blocks):
                xn, xnb = xns.pop((gi, j))
                self.norm_b(xn, xnb, hxT[:, :, j * 128:(j + 1) * 128], f"hxT{hs}", 0, 1 if blk >= 64 else 0, 0)

        def part_proj(gi):
            blocks, nb, ntok, tok0, hs, hxT, tabC, tabS = geom(gi)
            needA = [(j, self.A_SLOT[b]) for j, b in enumerate(blocks) if b in self.A_SLOT]
            def sink(ci, kst, kb):
                self.dma("sync", I["KT0"][ci, :, tok0:tok0 + ntok], kst[:, :ntok], [kb], [], f"kt_st{self.rot('ktst', 4)}")
                if ci == 0:
                    for j, slot in needA:
                        self.cp("gpsimd", self.kAT[:, slot, :], kst[:, j * 128:(j + 1) * 128], [kb], ["kAT"])
            chunks = [(ci, ci * 128) for ci in range(5) if not (ci == 0 and not needA)]
            self.fm_rope_loop(chunks, ntok, hxT, f"hxT{hs}", wkf, 128, tabC, tabS, f"tab{hs}", "wbig", None, sink)
            vs = gi % 2
            for j, blk in enumerate(blocks):
                vb_ = 5 + self.rot("vbank", 2)
                pv, pa = self.bank[vb_], self.bank[7]
                for k in range(8):
                    self.mm(pv, hxT[:, k, j * 128:(j + 1) * 128], wv[:, k, 128:640], k == 0, k == 7, [f"hxT{hs}", "wbig"], [f"ps{vb_}"])
                self.act(vst[vs][:, :, j, 0:128], pv.rearrange("p (h d) -> p h d", d=128), AF.Copy, [f"ps{vb_}"], [f"vst{vs}"])
                if blk in self.A_SLOT:
                    for k in range(8):
                        self.mm(pa[:, 0:128], hxT[:, k, j * 128:(j + 1) * 128], wv[:, k, 0:128], k == 0, k == 7, [f"hxT{hs}", "wbig"], ["ps7"])
                    self.act(self.vA[:, self.A_SLOT[blk], :, 0:64], pa[:, 0:128].rearrange("p (h d) -> p h d", d=64), AF.Copy, ["ps7"], ["vA"])
            b0 = blocks[0]
            self.dma("sync", I["VB0"][:, :, b0:b0 + nb, :].rearrange("h p b d -> p h b d"), vst[vs][:, :, 0:nb, :], [f"vst{vs}"], [], f"vb_st{vs}")

        part_norm(0)
        part_b(0)
        for gi in range(len(groups)):
            if gi + 1 < len(groups):
                part_norm(gi + 1)
            part_proj(gi)
            if gi + 1 < len(groups):
                part_b(gi + 1)
        self.debug_out("kAT", self.kAT[:], [128, 20, 128], ["kAT"], BF16)
        self.debug_out("vA", self.vA[:], [128, 20, 2, 65], ["vA"], BF16)

    def phase_q0(self, I):
        P = self.P
        P.barrier()
        wqf = self.v_big(0, [8, 1024])
        wqp = self.v_big(8192, [8, 1024])
        self.load_w(wqf, I["w0q_f"], 8)
        groups = [list(range(4 * g, 4 * g + 4)) for g in range(4)] + [[16, 17]]
        def geom(gi):
            blocks = groups[gi]
            nb = len(blocks)
            ntok = 128 * nb
            q0 = blocks[0] * 128
            tok0 = q0 if blocks[0] < 16 else NT
            hs = gi % 2
            hxT = self.v_mid(hs * 4096, [8, 512])
            tabC = self.v_mid(8192 + hs * 2048, [512], F32)
            tabS = self.v_mid(8192 + hs * 2048 + 1024, [512], F32)
            return blocks, ntok, q0, tok0, hs, hxT, tabC, tabS

        def part_norm(gi):
            blocks, ntok, q0, tok0, hs, hxT, tabC, tabS = geom(gi)
            P.op("sync", lambda e, tabC=tabC, tabS=tabS, tok0=tok0, ntok=ntok: [
                e.dma_start(out=tabC[:, :ntok], in_=I["ropeC0"][:, tok0:tok0 + ntok]),
                e.dma_start(out=tabS[:, :ntok], in_=I["ropeS0"][:, tok0:tok0 + ntok])], [], [f"tab{hs}"], chan=f"tab{hs}", ndma=2)
            for j, blk in enumerate(blocks):
                xns[(gi, j)] = self.norm_a(self.xres[:, blk, :], f"xres{blk}", 4)

        def part_b(gi):
            blocks, ntok, q0, tok0, hs, hxT, tabC, tabS = geom(gi)
            for j, blk in enumerate(blocks):
                xn, xnb = xns.pop((gi, j))
                self.norm_b(xn, xnb, hxT[:, :, j * 128:(j + 1) * 128], f"hxT{hs}", 0, 1 if blk >= 16 else 0, 0)

        def part_proj(gi):
            blocks, ntok, q0, tok0, hs, hxT, tabC, tabS = geom(gi)
            def sink(ci, kst, kb):
                self.dma("sync", I["QT0"][ci, :, q0:q0 + ntok], kst[:, :ntok], [kb], [], f"kt_st{self.rot('ktst', 4)}")
            self.fm_rope_loop([(ci, ci * 128) for ci in range(8)], ntok, hxT, f"hxT{hs}", wqf, 128, tabC, tabS, f"tab{hs}", "wbig", None, sink)

        xns = {}
        part_norm(0)
        part_b(0)
        for gi in range(len(groups)):
            if gi + 1 < len(groups):
                part_norm(gi + 1)
            part_proj(gi)
            if gi + 1 < len(groups):
                part_b(gi + 1)

    def phase_attA(self, I):
        P = self.P
        P.barrier()
        osb = self.v_big(0, [18, 512])
        self.osb = osb
        self.oTB = self.v_big(9216, [4, NQ])
        amask = self.v_big(18432, [4, 512])
        self.dma("gpsimd", amask, I["amask"], [], ["amask"], "amask")
        esink = self.v_sm(0, [8], F32)
        self.dma("sync", esink, I["sink"].partition_broadcast(128), [], ["esink"], "esink")
        self.act(esink, esink, AF.Exp, ["esink"], ["esink"])
        ident = self.ident
        for qi in range(18):
            qs = self.rot("qa", 2)
            qa = self.v_mid(qs * 512, [4, 128])
            self.dma("sync", qa, I["QT0"][0:4, :, qi * 128:(qi + 1) * 128].rearrange("c p t -> p c t"), [], [f"qa{qs}"], f"qa{qs}")
            if qi < 16:
                kbs = [(qi - 1 if qi > 0 else 17, 0 if qi == 0 else 1), (qi, None), (qi + 1, 3 if qi == 15 else 2), (18, None), (19, None)]
            else:
                kbs = [(18, None), (19, None)]
            its = [(g, ki, slot, mk) for g in range(2) for ki, (slot, mk) in enumerate(kbs)]
            stA = {}

            def qk(i, qa=qa, qs=qs, its=its, stA=stA):
                g, ki, slot, mk = its[i]
                sbk = self.rot("psA", 2)
                ps = self.bank[sbk]
                self.mm(ps, self.kAT[64 * g:64 * g + 64, slot, :], qa[64 * g:64 * g + 64].rearrange("p c t -> p (c t)"),
                        True, mk is None, ["kAT", f"qa{qs}"], [f"ps{sbk}"])
                if mk is not None:
                    self.mm(ps, ident[:], amask[:, mk, :], False, True, ["ident", "amask"], [f"ps{sbk}"])
                stA[i] = sbk

            def exp_pv(i, qi=qi, its=its, stA=stA, nk=len(kbs)):
                g, ki, slot, mk = its[i]
                sbk = stA.pop(i)
                ps = self.bank[sbk]
                pov = self.bank[4 + g][:, 0:260].rearrange("p (c d) -> p c d", d=65)
                pt = self.rot("pTA", 3)
                pT = self.v_mid(1024 + pt * 512, [512])
                self.act(pT, ps, AF.Exp, [f"ps{sbk}"], [f"pTA{pt}"], scale=0.125)
                for c in range(4):
                    self.mm(pov[:, c, :], pT[:, c * 128:(c + 1) * 128], self.vA[:, slot, g, :], ki == 0 and c == 0,
                            ki == nk - 1, [f"pTA{pt}", "vA"], [f"ps{4 + g}"])
                if ki == nk - 1:
                    dn = self.rot("denA", 2)
                    den = self.v_sm(64 + dn * 16, [4], F32)
                    self.tt("vector", den, pov[:, :, 64], esink[:, 4 * g:4 * g + 4], ALU.add, [f"ps{4 + g}", "esink"], [f"denA{dn}"])
                    self.P.op("vector", lambda e, den=den: e.reciprocal(den, den), [f"denA{dn}"], [f"denA{dn}"])
                    self.tt("vector", osb[:, qi, g * 256:(g + 1) * 256].rearrange("p (c d) -> p c d", d=64), pov[:, :, 0:64],
                            den.unsqueeze(2).to_broadcast([128, 4, 64]), ALU.mult, [f"ps{4 + g}", f"denA{dn}"], [f"osb{qi}"])

            qk(0)
            for i in range(len(its)):
                if i + 1 < len(its):
                    qk(i + 1)
                exp_pv(i)

    def attn_pass(self, rows, pbase, q_src, k_srcs, v_src, dv1, scale, qgroups, fin, tag):
        per = 512 // dv1
        for (qb0, nqb, kbA, nkbs) in qgroups:
            ob = 6 if dv1 == 65 else 4
            nss = 3 if dv1 == 65 else 2
            nq = nqb * 128
            qs = self.rot("qT", 2)
            qT = self.v_mid(qs * 1024, [1024])
            self.dma("sync", qT[pbase:pbase + rows, :nq], q_src(qb0 * 128, nq), [], [f"qT{qs}"], f"qT{qs}")
            po = lambda qb, ob=ob: self.bank[ob + qb // per][:, (qb % per) * dv1:(qb % per + 1) * dv1]
            pobuf = lambda qb, ob=ob: f"ps{ob + qb // per}"
            started = set()
            its = []
            done = 0
            while done < nkbs:
                npb = min(11, nkbs - done)
                for kl in range(npb):
                    its.append((kbA + done, npb, kl, done + kl == nkbs - 1))
                done += npb
            state = {}
            loaded = {}

            def emit_qk(i):
                kb0, npb, kl, last = its[i]
                if kl == 0:
                    def load_piece(kb0_, npb_):
                        sl = self.rot("kvp", 3)
                        Kp = self.v_mid(2048 + sl * 1408, [1408])
                        Vp = self.v_mid(2048 + 3 * 1408 + sl * 1420, [11, 129])[:, 0:npb_, 0:dv1] if dv1 == 129 else \
                            self.v_mid(2048 + 3 * 1408 + sl * 1420, [11 * 65])[:, 0:npb_ * 65].rearrange("p (b d) -> p b d", d=65)
                        self.P.op("sync", lambda e, Kp=Kp: [
                            e.dma_start(out=Kp[pbase + ro:pbase + ro + nr, :npb_ * 128], in_=fn(kb0_ * 128, npb_ * 128)) for (ro, nr, fn) in k_srcs],
                            [], [f"Kp{sl}"], chan=f"Kp{sl}", ndma=len(k_srcs))
                        self.dma("sync", Vp, v_src(kb0_, npb_), [], [f"Vp{sl}"], f"Vp{sl}")
                        loaded[kb0_] = (sl, Kp, Vp)
                    if kb0 not in loaded:
                        load_piece(kb0, npb)
                    nxt = [(a_, b_) for (a_, b_, c_, d_) in its[i + 1:] if c_ == 0][:1]
                    for (a_, b_) in nxt:
                        if a_ not in loaded:
                            load_piece(a_, b_)
                    state["piece"] = loaded[kb0]
                sl, Kp, Vp = state["piece"]
                ss = self.rot(f"psS{nss}", nss)
                nh = (nq + 511) // 512
                for hh in range(nh):
                    w = min(512, nq - hh * 512)
                    self.mm(self.bank[2 * ss + hh][:, :w], Kp[pbase:pbase + rows, kl * 128:(kl + 1) * 128],
                            qT[pbase:pbase + rows, hh * 512:hh * 512 + w], True, True, [f"Kp{sl}", f"qT{qs}"], [f"psS{ss}"])
                state[i] = (ss, sl, Vp, kl, last)

            def emit_exp_pv(i):
                ss, sl, Vp, kl, last = state.pop(i)
                pt = self.rot("pT", 3)
                pT = self.v_mid(2048 + 3 * 1408 + 3 * 1420 + pt * 1024, [1024])
                self.act(pT[:, :nq], self.psum[:, 2 * ss * 512:2 * ss * 512 + nq], AF.Exp, [f"psS{ss}"], [f"pT{pt}"], scale=scale)
                for qb in range(nqb):
                    bk = ob + qb // per
                    st = bk not in started
                    started.add(bk)
                    self.mm(po(qb), pT[:, qb * 128:(qb + 1) * 128], Vp[:, kl, :], st, last, [f"pT{pt}", f"Vp{sl}"], [pobuf(qb)])

            ahead = nss - 1
            for i0 in range(min(ahead, len(its))):
                emit_qk(i0)
            for i in range(len(its)):
                if i + ahead < len(its):
                    emit_qk(i + ahead)
                emit_exp_pv(i)
            fin(qb0, nqb, per, ob)

    def attn_pass_fm(self, pbase, q_src, k_src, v_src, scale, qg, fin):
        (qb0, nqb, kbA, nkbs) = qg
        rows = 64
        nq = nqb * 128
        qs = self.rot("qT", 2)
        qT = self.v_mid(qs * 1024, [1024])
        self.dma("sync", qT[pbase:pbase + rows, :nq], q_src(qb0 * 128, nq), [], [f"qT{qs}"], f"qT{qs}")
        ones = self.ones
        acc = self.v_mid(13604, [1024], F32)
        hi = self.v_sm(8192, [1024])
        lo = self.v_mid(15652, [1024])
        nh = (nq + 511) // 512
        its = []
        done = 0
        while done < nkbs:
            npb = min(11, nkbs - done)
            for kl in range(npb):
                its.append((kbA + done, npb, kl, done + kl == 0, done + kl == nkbs - 1))
            done += npb
        state = {}
        loaded = {}

        def emit_qk(i):
            kb0, npb, kl, first, last = its[i]
            if kl == 0:
                def load_piece(kb0_, npb_):
                    sl = self.rot("kvp", 3)
                    Kp = self.v_mid(2048 + sl * 1408, [1408])
                    Vp = self.v_mid(2048 + 3 * 1408 + sl * 1420, [11, 129])[:, 0:npb_, :]
                    self.dma("sync", Kp[pbase:pbase + rows, :npb_ * 128], k_src(kb0_ * 128, npb_ * 128), [], [f"Kp{sl}"], f"Kp{sl}")
                    self.dma("sync", Vp, v_src(kb0_, npb_), [], [f"Vp{sl}"], f"Vp{sl}")
                    loaded[kb0_] = (sl, Kp, Vp)
                if kb0 not in loaded:
                    load_piece(kb0, npb)
                nxt = [(a_, b_) for (a_, b_, c_, d_, e_) in its[i + 1:] if c_ == 0][:1]
                for (a_, b_) in nxt:
                    if a_ not in loaded:
                        load_piece(a_, b_)
                state["piece"] = loaded[kb0]
            sl, Kp, Vp = state["piece"]
            ss = self.rot("psS", 2)
            for hh in range(nh):
                w = min(512, nq - hh * 512)
                self.mm(self.bank[2 * ss + hh][:, :w], Kp[pbase:pbase + rows, kl * 128:(kl + 1) * 128],
                        qT[pbase:pbase + rows, hh * 512:hh * 512 + w], True, True, [f"Kp{sl}", f"qT{qs}"], [f"psS{ss}"])
            state[i] = (ss, sl, Vp, kl, first, last)

        def emit_exp_pv(i):
            ss, sl, Vp, kl, first, last = state.pop(i)
            pt = self.rot("pT", 3)
            pT = self.v_mid(2048 + 3 * 1408 + 3 * 1420 + pt * 1024, [1024])
            self.act(pT[:, :nq], self.psum[:, 2 * ss * 512:2 * ss * 512 + nq], AF.Exp, [f"psS{ss}"], [f"pT{pt}"], scale=scale)
            for hh in range(nh):
                w = min(512, nq - hh * 512)
                self.mm(self.bank[4 + hh][:, :w], Vp[:, kl, 0:128], pT[:, hh * 512:hh * 512 + w], first, last, [f"pT{pt}", f"Vp{sl}"], [f"ps{4 + hh}"])
                self.mm(self.bank[6 + hh][:, :w], ones[:], pT[:, hh * 512:hh * 512 + w], first, last, [f"pT{pt}", "ones"], [f"ps{6 + hh}"])

        emit_qk(0)
        for i in range(len(its)):
            if i + 1 < len(its):
                emit_qk(i + 1)
            emit_exp_pv(i)
        fin(qb0, nq, nh)

    def phase_attB(self, I):
        P = self.P
        P.barrier()
        self.precast_ffn(0, I["w13t0"], I["w2t0"])
        oTB = self.oTB
        self.ones = self.v_sm(3584, [128])
        ones = self.ones
        P.op("gpsimd", lambda e: e.memset(ones, 1.0), [], ["ones"])
        lv = self.v_sm(128, [4, 64], F32)
        prod = self.v_sm(640, [2, 64], F32)
        sums = self.v_sm(896, [2], F32)
        lam = self.v_sm(904, [1], F32)
        subg = self.v_sm(1024, [1], F32)
        self.dma("sync", lv, I["dlam"].partition_broadcast(128).rearrange("p (a b) -> p a b", b=64), [], ["lv"], "lv")
        self.dma("sync", subg, I["subg"].rearrange("(p o) -> p o", o=1), [], ["subg"], "subg")
        self.tt("vector", prod, lv[:, 0:4:2, :], lv[:, 1:4:2, :], ALU.mult, ["lv"], ["prod"])
        self.P.op("vector", lambda e: e.tensor_reduce(out=sums, in_=prod, axis=AX.X, op=ALU.add), ["prod"], ["sums"])
        self.act(sums, sums, AF.Exp, ["sums"], ["sums"])
        self.tt("vector", lam, sums[:, 1:2], sums[:, 0:1], ALU.subtract, ["sums"], ["lam"])
        self.ts("vector", lam, lam, -LAM0, None, ALU.add, None, ["lam"], ["lam"])
        self.ts("vector", subg, subg, 1.0 - LAM0, None, ALU.mult, None, ["subg"], ["subg"])
        R = self.v_sm(4096, [1024], F32)
        t1 = self.v_sm(6144, [1024], F32)
        sq = self.v_sm(8192, [1024])
        epsb = self.epsb

        def make_fin(h, m):
            def fin(qb0, nq, nh):
                Sb = self.psum[:, 6 * 512:6 * 512 + nq]
                O = self.psum[:, 4 * 512:4 * 512 + nq]
                pO = ["ps4", "ps5"][:nh]
                pS = ["ps6", "ps7"][:nh]
                Rv, tv = R[:, :nq], t1[:, :nq]
                self.act(Rv, Sb, AF.Ln, pS, ["Rb"])
                self.act(Rv, Rv, AF.Exp, ["Rb"], ["Rb"], scale=-1.0)
                if m == 0:
                    self.tt("vector", tv, O, Rv, ALU.mult, pO + ["Rb"], ["t1b"])
                    return
                self.tt("vector", Rv, O, Rv, ALU.mult, pO + ["Rb"], ["Rb"])
                self.P.op("vector", lambda e: e.scalar_tensor_tensor(out=tv, in0=Rv, scalar=lam[:, 0:1], in1=tv, op0=ALU.mult, op1=ALU.add),
                          ["Rb", "t1b", "lam"], ["t1b"])
                self.tt("vector", sq[:, :nq], tv, tv, ALU.mult, ["t1b"], ["sqb"])
                for hh in range(nh):
                    w = min(512, nq - hh * 512)
                    self.mm(self.bank[6 + hh][:, :w], ones[:], sq[:, hh * 512:hh * 512 + w], True, True, ["sqb", "ones"], [f"ps{6 + hh}"])
                self.act(Rv, Sb, AF.Ln, pS + ["epsb"], ["Rb"], scale=1.0 / 128, bias=epsb[:])
                self.act(Rv, Rv, AF.Exp, ["Rb"], ["Rb"], scale=-0.5)
                q0 = qb0 * 128
                self.P.op("vector", lambda e: e.scalar_tensor_tensor(out=oTB[:, h, q0:q0 + nq], in0=tv, scalar=subg[:, 0:1], in1=Rv,
                                                                      op0=ALU.mult, op1=ALU.mult), ["t1b", "Rb", "subg"], ["oTB"])
            return fin

        qgroups = [(0, 8, 0, 66), (8, 8, 0, 66), (16, 2, 64, 2)]
        for h in range(4):
            for qg in qgroups:
                for m in range(2):
                    self.attn_pass_fm(64 * m,
                                      lambda q0, nq, h=h, m=m: I["QT0"][4 + h, 64 * m:64 * m + 64, q0:q0 + nq],
                                      lambda c0, ncl, h=h, m=m: I["KT0"][1 + h, 64 * m:64 * m + 64, c0:c0 + ncl],
                                      lambda kb0, n, h=h: I["VB0"][h, :, kb0:kb0 + n, :],
                                      0.125, qg, make_fin(h, m))

    def resid(self, blk, ybanks, gidx):
        b0 = ybanks
        y = self.psum[:, b0 * 512:b0 * 512 + D]
        ybufs = [f"ps{b0}", f"ps{b0 + 1}"]
        c = self.rot("ssq", 8)
        ssq = self.ssq[:, c:c + 1]
        rstd = self.rstd[:, c:c + 1]
        epsb = self.epsb
        self.act(self.sqj[:], y, AF.Square, ybufs, ["sqj", f"ssq{c}"], accum_out=ssq)
        self.act(rstd, ssq, AF.Sqrt, [f"ssq{c}", "epsb"], [f"rstd{c}"], scale=1.0 / D, bias=epsb[:])
        self.P.op("vector", lambda e: e.reciprocal(rstd, rstd), [f"rstd{c}"], [f"rstd{c}"])
        tmpf = self.v_sm(6144, [D], F32)
        G = self.gates[:, gidx, :]
        self.P.op("vector", lambda e: e.scalar_tensor_tensor(out=tmpf, in0=y, scalar=rstd, in1=G, op0=ALU.mult, op1=ALU.mult),
                  ybufs + [f"rstd{c}", "gates"], ["tmpf"])
        xr = self.xres[:, blk, :]
        self.tt("gpsimd", xr, xr, tmpf, ALU.add, ["tmpf", f"xres{blk}"], [f"xres{blk}"])

    def phase_outproj(self, layer, wsrc):
        P = self.P
        P.barrier()
        osb = self.osb
        wout = self.v_mid(0, [8, D])
        self.load_w(wout, wsrc, 8, "wmid")
        nblk = 18 if layer == 0 else 16
        nt = 4 if layer == 0 else 8
        oTs = {}

        def stage_t(blk):
            pT = self.bank[0].bitcast(BF16).rearrange("p (k t) -> p k t", t=128)
            for k in range(nt):
                self.tr(pT[:, k, :], osb[:, blk, k * 128:(k + 1) * 128], [f"osb{blk}"], ["ps0"])
            s = self.rot("oT", 2)
            oT = self.v_mid(8192 + s * 1024, [8, 128])
            self.cp("vector", oT[:, 0:nt, :], pT[:, 0:nt, :], ["ps0"], [f"oT{s}"])
            oTs[blk] = (s, oT)

        def stage_m(blk):
            s, oT = oTs.pop(blk)
            yb = 1 + 2 * self.rot("ybank", 2)
            for hh in range(2):
                for k in range(8):
                    if k < nt:
                        lhs, rd = oT[:, k, :], f"oT{s}"
                    else:
                        lhs, rd = self.oTB[:, k - 4, blk * 128:(blk + 1) * 128], "oTB"
                    self.mm(self.bank[yb + hh], lhs, wout[:, k, hh * 512:(hh + 1) * 512], k == 0, k == 7, [rd, "wmid"], [f"ps{yb + hh}"])
            self.resid(blk, yb, 0 if blk < 16 else 2)

        stage_t(0)
        for blk in range(nblk):
            if blk + 1 < nblk:
                stage_t(blk + 1)
            stage_m(blk)

    def precast_ffn(self, layer, w13t, w2t):
        self.w13b = getattr(self, "w13b", {})
        self.w2b = getattr(self, "w2b", {})
        w13b = self.dint(f"w13b{layer}", [22, 128, 2048], BF16)
        w2b = self.dint(f"w2b{layer}", [128, 22 * D], BF16)
        self.w13b[layer], self.w2b[layer] = w13b, w2b
        for hc in range(22):
            self.dma("gpsimd", w13b[hc], w13t[hc].rearrange("p k c -> p (k c)"), [], [f"w13b{layer}_{hc}"], f"pc{self.rot('pc', 4)}")
        for hc in range(22):
            self.dma("gpsimd", w2b[:, hc * D:(hc + 1) * D], w2t[:, hc, :], [], [f"w2b{layer}"], f"pc{self.rot('pc', 4)}")

    def phase_ffn(self, layer, w13t, w2t, out_ap=None):
        P = self.P
        P.barrier()
        w2 = self.v_big(0, [22, D])
        w13b, w2b = self.w13b[layer], self.w2b[layer]
        self.dma("sync", w2, w2b.rearrange("p (k c) -> p k c", c=D), [f"w2b{layer}"], ["wbig"], "wbig_hw")
        actT = self.v_mid(0, [22, 768])
        hxT = self.aux[:, 0:6144].rearrange("p (k t) -> p k t", t=768)
        groups = [list(range(0, 6)), list(range(6, 12)), list(range(12, 18 if layer == 0 else 16))]
        for j, blk in enumerate(groups[0]):
            self.norm_T(self.xres[:, blk, :], f"xres{blk}", hxT[:, :, j * 128:(j + 1) * 128], "hxF", layer, 1 if blk >= 16 else 0, 1)
        for gi, blocks in enumerate(groups):
            ntok = 128 * len(blocks)
            nxt = groups[gi + 1] if gi + 1 < len(groups) else []
            halves = [(0, min(384, ntok))] + ([(384, ntok - 384)] if ntok > 384 else [])
            for hc in range(22):
                s = hc % 2
                wt = self.v_sm(s * 2048, [8, 256])
                self.dma("sync", wt, w13b[hc].rearrange("p (k c) -> p k c", c=256), [f"w13b{layer}_{hc}"], [f"w13_{s}"], f"w13_{s}")
                for hi, (t0, tw) in enumerate(halves):
                    pg, pu = self.bank[1 + hi], self.bank[3 + hi]
                    for k in range(8):
                        self.mm(pg[:, :tw], wt[:, k, 0:128], hxT[:, k, t0:t0 + tw], k == 0, k == 7, [f"w13_{s}", "hxF"], [f"ps{1 + hi}"])
                    for k in range(8):
                        self.mm(pu[:, :tw], wt[:, k, 128:256], hxT[:, k, t0:t0 + tw], k == 0, k == 7, [f"w13_{s}", "hxF"], [f"ps{3 + hi}"])
                    sgs = self.rot("sg", 2)
                    sg = self.v_sm(8192 + sgs * 384, [384])
                    self.act(sg[:, :tw], pg[:, :tw], AF.Silu, [f"ps{1 + hi}"], [f"sg{sgs}"])
                    self.tt("vector", actT[:, hc, t0:t0 + tw], sg[:, :tw], pu[:, :tw], ALU.mult, [f"sg{sgs}", f"ps{3 + hi}"], ["actT"])
            for j, blk in enumerate(blocks):
                pre = None
                if j < len(nxt):
                    nb_ = nxt[j]
                    pre = self.norm_a(self.xres[:, nb_, :], f"xres{nb_}")
                yb = 5 if j % 2 == 0 else 1
                for hh in range(2):
                    for hc in range(22):
                        self.mm(self.bank[yb + hh], actT[:, hc, j * 128:(j + 1) * 128], w2[:, hc, hh * 512:(hh + 1) * 512], hc == 0, hc == 21,
                                ["actT", "wbig"], [f"ps{yb + hh}"])
                if pre is not None:
                    self.norm_b(pre[0], pre[1], hxT[:, :, j * 128:(j + 1) * 128], "hxF", layer, 1 if nb_ >= 16 else 0, 1)
                self.resid(blk, yb, 1 if blk < 16 else 3)
                if out_ap is not None and blk < 16:
                    f = self.dma("sync", out_ap[blk * 128:(blk + 1) * 128, :], self.xres[:, blk, :], [f"xres{blk}"], [], f"xout{blk % 2}")
                    self.finals.append(f)

    def phase_mla_pre(self, I):
        P = self.P
        P.barrier()
        w1in = self.v_big(0, [8, 704])
        wqf = self.v_big(5632, [3, 1536])
        wqp = self.v_big(10240, [3, 1536])
        self.load_w(w1in, I["w1in"], 8, "wbig")
        self.load_w(wqf, I["wq1f"], 3, "wbig2")
        ng = self.v_sm(0, [5], F32)
        self.dma("sync", ng, I["mla_ng"], [], ["mla_ng"], "mla_ng")
        krC = self.v_mid(14336, [512], F32)
        krS = self.v_mid(15360, [512], F32)
        epsb = self.epsb
        groups = [list(range(4 * g, 4 * g + 4)) for g in range(4)] + [[16, 17]]
        def geom(gi):
            blocks = groups[gi]
            ntok = 128 * len(blocks)
            q0 = blocks[0] * 128
            s = gi % 2
            return blocks, ntok, q0, s, self.v_mid(s * 1536, [3, 512]), self.v_mid(3072 + s * 1024, [2, 512]), self.v_mid(5120 + s * 512, [512])

        def part1(gi):
            blocks, ntok, q0, s, qnT, kvnT, krT = geom(gi)
            P.op("sync", lambda e, q0=q0, ntok=ntok: [
                e.dma_start(out=krC[0:32, :ntok], in_=I["kr1C"][:, q0:q0 + ntok]),
                e.dma_start(out=krS[0:32, :ntok], in_=I["kr1S"][:, q0:q0 + ntok])], [], ["krtab"], chan="krtab", ndma=2)
            stq = {}

            def sA(j):
                blk = blocks[j]
                stq[("xn", j)] = self.norm_a(self.xres[:, blk, :], f"xres{blk}")

            def sB(j):
                blk = blocks[j]
                hs = self.rot("hx1", 2)
                hxT = self.aux[:, hs * 1024:(hs + 1) * 1024].rearrange("p (k t) -> p k t", t=128)
                xn, xnb = stq.pop(("xn", j))
                self.norm_b(xn, xnb, hxT, f"hx1_{hs}", 1, 1 if blk >= 16 else 0, 0)
                pa, pq = self.bank[5], self.bank[6]
                for k in range(8):
                    self.mm(pa[:, 0:320], hxT[:, k, :], w1in[:, k, 0:320], k == 0, k == 7, [f"hx1_{hs}", "wbig"], ["ps5"])
                for k in range(8):
                    self.mm(pq[:, 0:384], hxT[:, k, :], w1in[:, k, 320:704], k == 0, k == 7, [f"hx1_{hs}", "wbig"], ["ps6"])

            def sC1(j):
                pa, pq = self.bank[5], self.bank[6]
                c = self.rot("ssq", 8)
                c2 = self.rot("ssq", 8)
                for (cc, src, n, pb) in ((c, pa[:, 0:256], 256, "ps5"), (c2, pq[:, 0:384], 384, "ps6")):
                    ssq = self.ssq[:, cc:cc + 1]
                    rstd = self.rstd[:, cc:cc + 1]
                    self.act(self.sqj[:, 0:n], src, AF.Square, [pb], ["sqj", f"ssq{cc}"], accum_out=ssq)
                    self.act(rstd, ssq, AF.Sqrt, [f"ssq{cc}", "epsb"], [f"rstd{cc}"], scale=1.0 / n, bias=epsb[:])
                    self.P.op("vector", lambda e, rstd=rstd: e.reciprocal(rstd, rstd), [f"rstd{cc}"], [f"rstd{cc}"])
                ts_ = self.rot("tk", 2)
                tk = self.v_mid(6144 + ts_ * 704, [704])
                self.act(tk[:, 0:256], pa[:, 0:256], AF.Copy, ["ps5", f"rstd{c}"], [f"tk{ts_}"], scale=self.rstd[:, c:c + 1])
                self.cp("vector", tk[:, 256:320], pa[:, 256:320], ["ps5"], [f"tk{ts_}"])
                self.act(tk[:, 320:704], pq[:, 0:384], AF.Copy, ["ps6", f"rstd{c2}"], [f"tk{ts_}"], scale=self.rstd[:, c2:c2 + 1])
                stq[("tk", j)] = (ts_, tk)

            def sC2(j):
                ts_, tk = stq.pop(("tk", j))
                pT = self.bank[7].bitcast(BF16).rearrange("p (k t) -> p k t", t=128)
                self.tr(pT[:, 0, :], tk[:, 0:128], [f"tk{ts_}"], ["ps7"])
                self.tr(pT[:, 1, :], tk[:, 128:256], [f"tk{ts_}"], ["ps7"])
                self.tr(pT[0:32, 2, :], tk[:, 256:288], [f"tk{ts_}"], ["ps7"])
                self.tr(pT[0:32, 3, :], tk[:, 288:320], [f"tk{ts_}"], ["ps7"])
                for cc in range(3):
                    self.tr(pT[:, 4 + cc, :], tk[:, 320 + cc * 128:320 + (cc + 1) * 128], [f"tk{ts_}"], ["ps7"])
                tsl = slice(j * 128, (j + 1) * 128)
                self.tt("vector", kvnT[:, :, tsl], pT[:, 0:2, :], ng[:, 0:2].unsqueeze(2).to_broadcast([128, 2, 128]), ALU.mult,
                        ["ps7", "mla_ng"], [f"kvnT{s}"])
                self.tt("vector", qnT[:, :, tsl], pT[:, 4:7, :], ng[:, 2:5].unsqueeze(2).to_broadcast([128, 3, 128]), ALU.mult,
                        ["ps7", "mla_ng"], [f"qnT{s}"])
                t1 = self.v_big(20512, [512], F32)
                t2 = self.v_big(21536, [512], F32)
                self.tt("vector", t1[0:32, 0:128], pT[0:32, 2, :], krC[0:32, tsl], ALU.mult, ["ps7", "krtab"], ["t1"])
                self.tt("vector", t2[0:32, 0:128], pT[0:32, 3, :], krS[0:32, tsl], ALU.mult, ["ps7", "krtab"], ["t2"])
                self.tt("gpsimd", krT[0:32, tsl], t1[0:32, 0:128], t2[0:32, 0:128], ALU.add, ["t1", "t2"], [f"krT{s}"])

            nbk = len(blocks)
            sA(0)
            sB(0)
            for j in range(nbk):
                if j + 1 < nbk:
                    sA(j + 1)
                sC1(j)
                if j + 1 < nbk:
                    sB(j + 1)
                sC2(j)
            if blocks[0] < 16:
                gdst, gname = I["gown"][gi], f"gown{gi}"
            else:
                gdst, gname = I["gctx"], "gctx"
            self.dma("sync", gdst[0:256, 0:ntok].rearrange("(c p) t -> p c t", p=128), kvnT[:, :, :ntok], [f"kvnT{s}"], [gname], "g_st")
            self.dma("sync", gdst[256:288, 0:ntok], krT[0:32, :ntok], [f"krT{s}"], [gname], "g_st2")
            if blocks[0] >= 16:
                return
            if self.mode == "FUSED":
                rg = [[0, 1, 2, 3], [4, 5, 6, 7]]
                self.P.op("gpsimd", lambda e, gi=gi: e.collective_compute("AllGather", ALU.bypass, replica_groups=rg,
                                                                          ins=[I["gown"][gi]], outs=[I["gall"][gi]]),
                          [gname], [f"gall{gi}"], chan="cc", cinc=1)
            tabC = self.v_mid(8192 + s * 2048, [512], F32)
            tabS = self.v_mid(8192 + s * 2048 + 1024, [512], F32)
            P.op("sync", lambda e, tabC=tabC, tabS=tabS, q0=q0, ntok=ntok: [
                e.dma_start(out=tabC[0:96, :ntok], in_=I["rope1C"][:, q0:q0 + ntok]),
                e.dma_start(out=tabS[0:96, :ntok], in_=I["rope1S"][:, q0:q0 + ntok])], [], [f"tab{s}"], chan=f"tab{s}", ndma=2)

        def part2(gi):
            blocks, ntok, q0, s, qnT, kvnT, krT = geom(gi)
            if blocks[0] >= 16:
                return
            tabC = self.v_mid(8192 + s * 2048, [512], F32)
            tabS = self.v_mid(8192 + s * 2048 + 1024, [512], F32)
            def sink(h, kst, kbn):
                self.dma("sync", I["QT1"][h, :, q0:q0 + ntok], kst[0:96, :ntok], [kbn], [], f"kt_st{self.rot('ktst', 4)}")
            self.fm_rope_loop([(h, h * 96) for h in range(16)], ntok, qnT, f"qnT{s}", wqf, 96, tabC, tabS, f"tab{s}", "wbig2", self.perm96, sink)

        part1(0)
        for gi in range(len(groups)):
            if gi + 1 < len(groups):
                part1(gi + 1)
            part2(gi)

    def phase_kv1(self, I):
        P = self.P
        P.barrier()
        wk1 = self.v_big(0, [2, D])
        wv1 = self.v_big(2048, [2, D])
        self.load_w(wk1, I["wk1"], 2, "wbig")
        self.load_w(wv1, I["wv1"], 2, "wbig")
        vst = [self.v_big(4096 + s2 * 4160, [16, 4, 65]) for s2 in range(2)]
        for s2 in range(2):
            P.op("gpsimd", lambda e, s2=s2: e.memset(vst[s2][:, :, :, 64:65], 1.0), [], [f"vst{s2}"])
        ngroups = 17

        def geom(gi):
            nb = 4 if gi < 16 else 2
            s = gi % 2
            return nb, 128 * nb, gi * 512, s, self.v_mid(s * 1024, [2, 512]), self.v_mid(2048 + s * 512, [512])

        def load(gi):
            nb, ntok, tok0, s, kvnT, krT = geom(gi)
            if gi < 16:
                r, jj = gi // 4, gi % 4
                src, sname = I["gall"][jj][r * 288:(r + 1) * 288, :], f"gall{jj}"
            else:
                src, sname = I["gctx"], "gctx"
            c0 = 0
            P.op("sync", lambda e, kvnT=kvnT, krT=krT, src=src, c0=c0, ntok=ntok: [
                e.dma_start(out=kvnT[:, :, :ntok], in_=src[0:256, c0:c0 + ntok].rearrange("(c p) t -> p c t", p=128)),
                e.dma_start(out=krT[0:32, :ntok], in_=src[256:288, c0:c0 + ntok])], [sname], [f"kvn{s}"], chan=f"kvn{s}", ndma=2)

        def compute(gi):
            nb, ntok, tok0, s, kvnT, krT = geom(gi)
            self.dma("sync", I["KR1"][:, tok0:tok0 + ntok], krT[0:32, :ntok], [f"kvn{s}"], [], "kr_st")
            vs = gi % 2

            def kchunk(cch):
                b = self.rot("k1bank", 2)
                pk_ = self.bank[1 + b]
                for k in range(2):
                    self.mm(pk_[:, :ntok], wk1[:, k, cch * 128:(cch + 1) * 128], kvnT[:, k, :ntok], k == 0, k == 1, [f"kvn{s}", "wbig"], [f"ps{1 + b}"])
                ks = self.rot("kst", 4)
                kst = self.v_mid(12288 + ks * 512, [512])
                self.cp("vector", kst[:, :ntok], pk_[:, :ntok], [f"ps{1 + b}"], [f"kst{ks}"])
                self.dma("sync", I["KT1"][cch, :, tok0:tok0 + ntok], kst[:, :ntok], [f"kst{ks}"], [], f"kt_st{self.rot('ktst', 4)}")

            def vpart(j, hh):
                pv = self.bank[5 + hh]
                for k in range(2):
                    self.mm(pv, kvnT[:, k, j * 128:(j + 1) * 128], wv1[:, k, hh * 512:(hh + 1) * 512], k == 0, k == 1, [f"kvn{s}", "wbig"], [f"ps{5 + hh}"])
                self.act(vst[vs][:, hh * 8:(hh + 1) * 8, j, 0:64], pv.rearrange("p (h d) -> p h d", d=64), AF.Copy, [f"ps{5 + hh}"], [f"vst{vs}"])

            vlist = [(j, hh) for j in range(nb) for hh in range(2)]
            for i in range(8):
                kchunk(i)
                if i < len(vlist):
                    vpart(*vlist[i])
            b0 = gi * 4
            self.dma("sync", I["V1"][:, :, b0:b0 + nb, :].rearrange("h p b d -> p h b d"), vst[vs][:, :, 0:nb, :], [f"vst{vs}"], [], f"vb_st{vs}")

        load(0)
        for gi in range(ngroups):
            if gi + 1 < ngroups:
                load(gi + 1)
            compute(gi)

    def phase_attC(self, I):
        P = self.P
        P.barrier()
        self.precast_ffn(1, I["w13t1"], I["w2t1"])
        osb = self.v_big(0, [18, D])
        self.osb = osb

        def make_fin(h):
            def fin(qb0, nqb, per, ob):
                for bk in range((nqb + per - 1) // per):
                    n = min(per, nqb - bk * per)
                    pv = self.bank[ob + bk][:, 0:n * 65].rearrange("p (q d) -> p q d", d=65)
                    pb = f"ps{ob + bk}"
                    r = self.rot("rsC", 2)
                    rs = self.v_sm(1280 + r * 16, [7], F32)[:, 0:n]
                    self.P.op("vector", lambda e, rs=rs, pv=pv: e.reciprocal(rs, pv[:, :, 64]), [pb], [f"rsC{r}"])
                    q0 = qb0 + bk * per
                    self.tt("vector", osb[:, q0:q0 + n, h * 64:(h + 1) * 64], pv[:, :, 0:64], rs.unsqueeze(2).to_broadcast([128, n, 64]),
                            ALU.mult, [pb, f"rsC{r}"], [f"osb{q0 + i}" for i in range(n)])
            return fin

        for h in range(16):
            for qg in [(0, 8, 0, 66), (8, 8, 0, 66)]:
                self.attn_pass(96, 0,
                               lambda q0, nq, h=h: I["QT1"][h, :, q0:q0 + nq],
                               [(0, 64, lambda c0, ncl, h=h: I["KT1"][h // 2, (h % 2) * 64:(h % 2) * 64 + 64, c0:c0 + ncl]),
                                (64, 32, lambda c0, ncl: I["KR1"][:, c0:c0 + ncl])],
                               lambda kb0, n, h=h: I["V1"][h, :, kb0:kb0 + n, :],
                               65, 96 ** -0.5, [qg], make_fin(h), "C")


import numpy as np

D = 1024; NT = 8192; OWN = 2048; CTX = 256
GRID_W = 64; THETA = 10000.0


def rope_tabs_fm(rot_dim, positions):
    axis_dim = rot_dim // 2
    q = rot_dim // 4
    inv = THETA ** (-np.arange(0, axis_dim, 2, dtype=np.float32) / axis_dim)
    row = (positions // GRID_W).astype(np.float32)
    col = (positions % GRID_W).astype(np.float32)
    ar = row[None, :] * inv[:, None]
    ac = col[None, :] * inv[:, None]
    ar = ar.astype(np.float32); ac = ac.astype(np.float32)
    C = np.concatenate([np.cos(ar), np.cos(ar), np.cos(ac), np.cos(ac)], 0)
    S = np.concatenate([-np.sin(ar), np.sin(ar), -np.sin(ac), np.sin(ac)], 0)
    return C.astype(np.float32), S.astype(np.float32)


def rope_perm(rot_dim):
    q = rot_dim // 4
    return np.concatenate([np.arange(q, 2 * q), np.arange(0, q), np.arange(3 * q, 4 * q), np.arange(2 * q, 3 * q)])


def pk(w):
    K = w.shape[0] // 128
    return np.ascontiguousarray(w.reshape(K, 128, w.shape[1]).transpose(1, 0, 2))


def prep_l0(inp, core):
    b, qc = core // 4, core % 4
    roll = qc * OWN
    f32 = np.float32
    m = {}
    m["xk"] = np.ascontiguousarray(np.roll(inp["x"][b], -roll, axis=0))
    m["ctxin"] = np.ascontiguousarray(inp["ctx"][b])
    pos = (np.arange(NT) + roll) % NT
    C, S = rope_tabs_fm(64, pos)
    C = np.concatenate([C, np.ones((64, CTX), f32)], 1)
    S = np.concatenate([S, np.zeros((64, CTX), f32)], 1)
    m["ropeC0"] = np.ascontiguousarray(np.concatenate([C, C], 0))
    m["ropeS0"] = np.ascontiguousarray(np.concatenate([S, S], 0))
    w = inp["ab_w_in"][0]
    p64 = rope_perm(64)
    def permcols(cols):
        return np.concatenate([c0 + p64 for c0 in cols])
    qa_cols = []
    for c in range(4):
        qa_cols += [64 * c, 64 * (4 + c)]
    qb_cols = [512 + 64 * i for i in range(8)]
    qcols = qa_cols + qb_cols
    kcols = [1024, 1088] + [1280 + 64 * i for i in range(8)]
    nat = lambda cols: np.concatenate([np.arange(c0, c0 + 64) for c0 in cols])
    m["w0q_f"] = pk(w[:, nat(qcols)])
    m["w0k_f"] = pk(w[:, nat(kcols)])
    pm64 = np.zeros((128, 128), f32)
    for hh in range(2):
        for d_ in range(64):
            pm64[hh * 64 + p64[d_], hh * 64 + d_] = 1.0
    m["perm64"] = pm64
    p32_ = rope_perm(32)
    pm96 = np.zeros((128, 128), f32)
    for d_ in range(64):
        pm96[d_, d_] = 1.0
    for d_ in range(32):
        pm96[64 + p32_[d_], 64 + d_] = 1.0
    m["perm96"] = pm96
    m["w0v"] = pk(np.concatenate([w[:, 1152:1280], w[:, 1792:2304]], 1))
    cf = np.stack([inp["c"][b].reshape(8, 128).T, inp["c_ctx"].reshape(8, 128).T], -1)
    m["cfm"] = np.ascontiguousarray(cf.astype(f32))
    for l in range(2):
        m[f"adaw{l}"] = np.ascontiguousarray(inp["ada_w"][l].reshape(8, 128, 6, D).transpose(2, 1, 0, 3))
        m[f"adab_fm{l}"] = np.ascontiguousarray(inp["ada_b"][l].reshape(48, 128).T)
        m[f"adab_row{l}"] = np.ascontiguousarray(inp["ada_b"][l])
        m[f"ng_fm{l}"] = np.ascontiguousarray(inp["norm_g"][l].reshape(4, 8, 128).transpose(2, 0, 1))
        m[f"ng_row{l}"] = np.ascontiguousarray(inp["norm_g"][l])
    return m


NEG = -30000.0


def prep_l0_more(inp, core, m):
    b, qc = core // 4, core % 4
    f32 = np.float32
    j = np.arange(128)[:, None]
    qi = np.arange(128)[None, :]
    prev = np.where(j >= qi, 0.0, NEG).astype(f32)
    nxt = np.where(j <= qi, 0.0, NEG).astype(f32)
    allneg = np.full((128, 128), NEG, f32)
    masks = [allneg if qc == 0 else prev, prev, nxt, allneg if qc == 3 else nxt]
    m["amask"] = np.ascontiguousarray(np.stack([np.tile(x, (1, 4)) for x in masks], 1))
    m["sink"] = np.ascontiguousarray(inp["ab_sink"][0])
    m["dlam"] = np.ascontiguousarray(inp["diff_lambda"][0].reshape(256))
    m["subg"] = np.ascontiguousarray(inp["diff_subln_g"][0])
    m["wout0"] = pk(inp["ab_w_out"][0])
    for l in (0,):
        ffn_pack(inp, l, m)
    return m


def ffn_pack(inp, l, m):
    w13 = inp["ffn_w13"][l]
    g = w13[:, :2816].reshape(8, 128, 22, 128)
    u = w13[:, 2816:].reshape(8, 128, 22, 128)
    t = np.concatenate([g, u], -1)
    m[f"w13t{l}"] = np.ascontiguousarray(t.transpose(2, 1, 0, 3))
    m[f"w2t{l}"] = pk(inp["ffn_w2"][l])


def prep_l1(inp, core, m):
    b, qc = core // 4, core % 4
    f32 = np.float32
    roll = qc * OWN
    pos = (np.arange(OWN) + roll) % NT
    C32, S32 = rope_tabs_fm(32, pos)
    p32 = rope_perm(32)
    w = inp["mla_w_in"][0]
    m["w1in"] = pk(np.concatenate([w[:, 384:640], w[:, 640:672], w[:, 640 + p32], w[:, 0:384]], 1))
    wq = inp["mla_wq_b"][0]
    permc = np.concatenate([np.concatenate([h * 96 + np.arange(64), h * 96 + 64 + p32]) for h in range(16)])
    m["wq1f"] = pk(wq)
    m["rope1C"] = np.ascontiguousarray(np.concatenate([np.ones((64, OWN), f32), C32], 0))
    m["rope1S"] = np.ascontiguousarray(np.concatenate([np.zeros((64, OWN), f32), S32], 0))
    m["kr1C"] = np.ascontiguousarray(np.concatenate([C32, np.ones((32, CTX), f32)], 1))
    m["kr1S"] = np.ascontiguousarray(np.concatenate([S32, np.zeros((32, CTX), f32)], 1))
    m["mla_ng"] = np.ascontiguousarray(np.concatenate([inp["mla_kv_norm_g"][0].reshape(2, 128).T,
                                                       inp["mla_q_norm_g"][0].reshape(3, 128).T], 1).astype(f32))
    wkv = inp["mla_wkv_b"][0].reshape(256, 16, 128)
    m["wk1"] = pk(np.ascontiguousarray(wkv[:, :, :64]).reshape(256, 1024))
    m["wv1"] = pk(np.ascontiguousarray(wkv[:, :, 64:]).reshape(256, 1024))
    m["wout1"] = pk(inp["mla_w_out"][0])
    ffn_pack(inp, 1, m)
    return m


import ml_dtypes

L0_KEYS = ["xk", "ctxin", "ropeC0", "ropeS0", "w0q_f", "w0k_f", "w0v", "cfm", "perm64", "perm96",
           "adaw0", "adab_fm0", "adab_row0", "ng_fm0", "ng_row0", "adaw1", "adab_fm1", "adab_row1", "ng_fm1", "ng_row1",
           "amask", "sink", "dlam", "subg", "wout0", "w13t0", "w2t0",
           "w1in", "wq1f", "rope1C", "rope1S", "kr1C", "kr1S", "mla_ng"]
L1_KEYS = ["cfm", "adaw1", "adab_fm1", "adab_row1", "ng_fm1", "ng_row1", "wk1", "wv1", "wout1", "w13t1", "w2t1"]


def layer0_body(kb, I):
    kb.phase_mod(0, I["cfm"], I["adaw0"], I["adab_fm0"], I["adab_row0"], I["ng_fm0"], I["ng_row0"])
    kb.phase_kv0(I)
    kb.phase_q0(I)
    kb.phase_attA(I)
    kb.phase_attB(I)
    kb.phase_outproj(0, I["wout0"])
    kb.phase_ffn(0, I["w13t0"], I["w2t0"])
    kb.phase_mod(1, I["cfm"], I["adaw1"], I["adab_fm1"], I["adab_row1"], I["ng_fm1"], I["ng_row1"])
    kb.phase_mla_pre(I)


def layer1_body(kb, I, out):
    kb.phase_kv1(I)
    kb.phase_attC(I)
    kb.phase_outproj(1, I["wout1"])
    kb.phase_ffn(1, I["w13t1"], I["w2t1"], out_ap=out)


def scratch_l0(kb, I):
    I["KT0"] = kb.dint("KT0", [5, 128, NK], BF16)
    I["VB0"] = kb.dint("VB0", [4, 128, NKB, 129], BF16)
    I["QT0"] = kb.dint("QT0", [8, 128, NQ], BF16)


def scratch_l1(kb, I):
    I["KT1"] = kb.dint("KT1", [8, 128, NK], BF16)
    I["KR1"] = kb.dint("KR1", [32, NK], BF16)
    I["V1"] = kb.dint("V1", [16, 128, NKB, 65], BF16)


def build_fused(m):
    kb = KB("FUSED")
    keys = L0_KEYS + [k for k in L1_KEYS if k not in L0_KEYS]
    I = {k: kb.din(k, m[k].shape) for k in keys}
    scratch_l0(kb, I)
    scratch_l1(kb, I)
    I["gown"] = [kb.dint(f"gown{j}", [288, 512], BF16) for j in range(4)]
    I["gall"] = [kb.dint(f"gall{j}", [4 * 288, 512], BF16) for j in range(4)]
    I["gctx"] = kb.dint("gctx", [288, CTX], BF16)
    I["QT1"] = kb.dint("QT1", [16, 96, OWN], BF16)
    out = kb.dout("out", [OWN, D])
    kb.setup_common(); kb.setup_l0()
    kb.load_perms(I)
    layer0_body(kb, I)
    layer1_body(kb, I, out)
    kb.P.emit(kb.finals)
    return kb, keys


def prep_all(inputs):
    inp = {k: np.asarray(v) for k, v in inputs.items()}
    maps = []
    for core in range(8):
        m = prep_l0(inp, core)
        prep_l0_more(inp, core, m)
        prep_l1(inp, core, m)
        maps.append(m)
    return maps


def kernel(**inputs):
    maps = prep_all(inputs)
    kb, keys = build_fused(maps[0])
    res = run_bass_kernel_spmd(kb.nc, [{k: m[k] for k in keys} for m in maps], core_ids=list(range(8)))
    out = np.zeros((2, NT, D), np.float32)
    for core in range(8):
        b, qc = core // 4, core % 4
        out[b, qc * OWN:(qc + 1) * OWN] = np.asarray(res.results[core]["out"])
    return out
```
